# Optimizing a Trainium2 kernel written in Bass

```python
import math
import jax, jax.numpy as jnp
from jax import lax
import numpy as np

D_MODEL = 1024
BATCH = 4
SEQ = 4096
DEPTH = 2

N_MIXERS = 2
N_POOL_LAYERS = (DEPTH + 1) // 2
N_ATTN_LAYERS = DEPTH // 2
ALPHA = (2.0 * DEPTH) ** 0.25
BETA = (8.0 * DEPTH) ** -0.25
LN_EPS = 1e-5
POOL_WINDOWS = (2, 4, 8, 16)
N_GROUPS = len(POOL_WINDOWS)
GROUP_W = D_MODEL // N_GROUPS
HEAD_DIM = 64
N_HEADS = D_MODEL // (2 * HEAD_DIM)
V_DIM = 2 * HEAD_DIM
D_ATTN = N_HEADS * V_DIM
Q_BLOCK = 128
NUM_BUCKETS = 32
MAX_EXACT = NUM_BUCKETS // 2
MAX_DISTANCE = 128
D_FF = ((int(math.ceil(8 * D_MODEL / 3)) + 255) // 256) * 256

kernel_name = "hybrid_pool_diffattn_deepnorm"


def _layernorm(x, g, b):
    xf = x.astype(jnp.float32)
    mu = jnp.mean(xf, axis=-1, keepdims=True)
    var = jnp.mean(jnp.square(xf - mu), axis=-1, keepdims=True)
    y = (xf - mu) * lax.rsqrt(var + LN_EPS)
    return (y * g.astype(jnp.float32) + b.astype(jnp.float32)).astype(x.dtype)


def _multiscale_pool(x, w, scale):
    B, S, D = x.shape
    xg = x.astype(jnp.float32).reshape(B, S, N_GROUPS, GROUP_W)
    cs = jnp.pad(jnp.cumsum(xg, axis=1), ((0, 0), (1, 0), (0, 0), (0, 0)))
    t = jnp.arange(S)
    pooled = []
    for g, win in enumerate(POOL_WINDOWS):
        hi = cs[:, 1:, g]
        lo = jnp.pad(cs[:, :S + 1 - win, g], ((0, 0), (win - 1, 0), (0, 0)))
        cnt = jnp.minimum(t + 1, win).astype(jnp.float32)[None, :, None]
        pooled.append((hi - lo) / cnt)
    pooled = jnp.stack(pooled, axis=2)
    mixed = jnp.einsum('bsgc,gcd->bsgd', pooled - xg, w.astype(jnp.float32))
    return (mixed.reshape(B, S, D) * scale.astype(jnp.float32)).astype(x.dtype)


def _rel_bucket(rel):
    n = jnp.maximum(rel, 0)
    is_small = n < MAX_EXACT
    nf = jnp.maximum(n, 1).astype(jnp.float32)
    large = MAX_EXACT + (jnp.log(nf / MAX_EXACT) / math.log(MAX_DISTANCE / MAX_EXACT)
                         * (NUM_BUCKETS - MAX_EXACT)).astype(jnp.int32)
    large = jnp.minimum(large, NUM_BUCKETS - 1)
    return jnp.where(is_small, n, large)


def _diff_attention(x, w_qkv, w_o, lam_p, subln_g, rel_table, lambda_init):
    B, S, D = x.shape
    qkv = x @ w_qkv
    q, k, v = jnp.split(qkv, 3, axis=-1)
    q = q.reshape(B, S, N_HEADS, 2, HEAD_DIM).transpose(0, 2, 3, 1, 4)
    k = k.reshape(B, S, N_HEADS, 2, HEAD_DIM).transpose(0, 2, 3, 1, 4)
    v = v.reshape(B, S, N_HEADS, V_DIM).transpose(0, 2, 1, 3)
    lp = lam_p.astype(jnp.float32)
    lam = (jnp.exp(jnp.sum(lp[0] * lp[1])) - jnp.exp(jnp.sum(lp[2] * lp[3])) + lambda_init)
    sm_scale = HEAD_DIM ** -0.5
    outs = []
    for qb in range(S // Q_BLOCK):
        q0 = qb * Q_BLOCK
        kend = q0 + Q_BLOCK
        s = jnp.einsum('bhcqd,bhckd->bhcqk', q[:, :, :, q0:kend], k[:, :, :, :kend]).astype(jnp.float32) * sm_scale
        rel = (q0 + jnp.arange(Q_BLOCK))[:, None] - jnp.arange(kend)[None, :]
        bias = rel_table[_rel_bucket(rel)].astype(jnp.float32).transpose(2, 0, 1)
        s = jnp.where(rel >= 0, s + bias[None, :, None], -jnp.inf)
        p = jax.nn.softmax(s, axis=-1)
        a = p[:, :, 0] - lam * p[:, :, 1]
        outs.append(jnp.einsum('bhqk,bhkv->bhqv', a.astype(v.dtype), v[:, :, :kend]))
    o = jnp.concatenate(outs, axis=2).astype(jnp.float32)
    o = o * lax.rsqrt(jnp.mean(jnp.square(o), axis=-1, keepdims=True) + LN_EPS)
    o = o * subln_g.astype(jnp.float32) * (1.0 - lambda_init)
    o = o.transpose(0, 2, 1, 3).reshape(B, S, D_ATTN).astype(x.dtype)
    return o @ w_o


def _swiglu(x, w_gate, w_up, w_down):
    return (jax.nn.silu(x @ w_gate) * (x @ w_up)) @ w_down


def setup_inputs(seed: int = 0) -> dict:
    key = jax.random.key(seed)
    ks = jax.random.split(key, 20)
    f32 = jnp.float32
    nrm = lambda k, s: jax.random.normal(k, s, f32)
    x = nrm(ks[0], (BATCH, SEQ, D_MODEL))
    pool_w = nrm(ks[1], (N_POOL_LAYERS, N_GROUPS, GROUP_W, GROUP_W)) * (GROUP_W ** -0.5) * BETA
    pool_scale = 1.0 + 0.02 * nrm(ks[2], (N_POOL_LAYERS, D_MODEL))
    w_qk = nrm(ks[3], (N_ATTN_LAYERS, D_MODEL, 2 * D_ATTN)) * (D_MODEL ** -0.5)
    w_v = nrm(ks[4], (N_ATTN_LAYERS, D_MODEL, D_ATTN)) * (D_MODEL ** -0.5) * BETA
    w_qkv = jnp.concatenate([w_qk, w_v], axis=-1)
    w_o = nrm(ks[5], (N_ATTN_LAYERS, D_ATTN, D_MODEL)) * (D_ATTN ** -0.5) * BETA
    lam_p = 0.1 * nrm(ks[6], (N_ATTN_LAYERS, 4, HEAD_DIM))
    subln_g = 1.0 + 0.02 * nrm(ks[7], (N_ATTN_LAYERS, V_DIM))
    rel_table = 0.5 * nrm(ks[8], (NUM_BUCKETS, N_HEADS))
    w_gate = nrm(ks[9], (DEPTH, D_MODEL, D_FF)) * (D_MODEL ** -0.5) * BETA
    w_up = nrm(ks[10], (DEPTH, D_MODEL, D_FF)) * (D_MODEL ** -0.5) * BETA
    w_down = nrm(ks[11], (DEPTH, D_FF, D_MODEL)) * (D_FF ** -0.5) * BETA
    ln_mix_g = 1.0 + 0.02 * nrm(ks[12], (DEPTH, D_MODEL))
    ln_mix_b = 0.02 * nrm(ks[13], (DEPTH, D_MODEL))
    ln_ffn_g = 1.0 + 0.02 * nrm(ks[14], (DEPTH, D_MODEL))
    ln_ffn_b = 0.02 * nrm(ks[15], (DEPTH, D_MODEL))
    return {"x": x, "pool_w": pool_w, "pool_scale": pool_scale, "w_qkv": w_qkv, "w_o": w_o,
            "lam_p": lam_p, "subln_g": subln_g, "rel_table": rel_table, "w_gate": w_gate,
            "w_up": w_up, "w_down": w_down, "ln_mix_g": ln_mix_g, "ln_mix_b": ln_mix_b,
            "ln_ffn_g": ln_ffn_g, "ln_ffn_b": ln_ffn_b}


def reference(x, pool_w, pool_scale, w_qkv, w_o, lam_p, subln_g, rel_table, w_gate,
              w_up, w_down, ln_mix_g, ln_mix_b, ln_ffn_g, ln_ffn_b):
    for i in range(DEPTH):
        j = i // N_MIXERS
        if i % N_MIXERS == 0:
            h = _multiscale_pool(x, pool_w[j], pool_scale[j])
        else:
            lambda_init = 0.8 - 0.6 * math.exp(-0.3 * i)
            h = _diff_attention(x, w_qkv[j], w_o[j], lam_p[j], subln_g[j], rel_table, lambda_init)
        x = _layernorm(ALPHA * x + h, ln_mix_g[i], ln_mix_b[i])
        x = _layernorm(ALPHA * x + _swiglu(x, w_gate[i], w_up[i], w_down[i]), ln_ffn_g[i], ln_ffn_b[i])
    return x
```

```python
import math
import os
from contextlib import ExitStack

import numpy as np
import ml_dtypes

import concourse.bass as bass
import concourse.mybir as mybir
from concourse.bass_utils import run_bass_kernel_spmd

F32 = mybir.dt.float32
BF16 = mybir.dt.bfloat16
AF = mybir.ActivationFunctionType
ALU = mybir.AluOpType
AX = mybir.AxisListType

D = 1024
DFF = 2816
NB = 16
T = NB * 128
NH = 8
VW = 144
ALPHA = 4.0 ** 0.25
EPS = 1e-5
LAMBDA_INIT = 0.8 - 0.6 * math.exp(-0.3 * 1)
NEG = -30000.0
POOL_WINDOWS = (2, 4, 8, 16)
N_CORES = 8


class Buf:
    __slots__ = ("name", "w", "r", "rng")

    def __init__(self, name, rng=None):
        self.name = name
        self.w = None
        self.r = {}
        self.rng = rng


class Prog:
    ENG = ("pe", "act", "dve", "pool", "sp")

    def __init__(self, nc, es):
        self.nc = nc
        self.es = es
        self.q = {e: [] for e in self.ENG}
        self.sem = {}
        self.cnt = {}
        self.waited = {e: {} for e in self.ENG}
        self.ranged = []
        for e in ("pe", "act", "dve", "pool"):
            self.newsem("c_" + e)

    def newsem(self, name):
        self.sem[name] = self.es.enter_context(self.nc.semaphore(name))
        self.cnt[name] = 0

    def wait(self, eng, tok):
        if tok is None:
            return
        s, v = tok
        if self.waited[eng].get(s, 0) >= v:
            return
        self.waited[eng][s] = v
        h = self.sem[s]
        self.q[eng].append(lambda e, h=h, v=v: e.wait_ge(h, v))

    def _deps(self, eng, reads, writes):
        for b in reads:
            self.wait(eng, b.w)
        for b in writes:
            self.wait(eng, b.w)
            for t in b.r.values():
                self.wait(eng, t)
            if b.rng is not None:
                for y in self.ranged:
                    if y is not b and y.rng[0] < b.rng[1] and b.rng[0] < y.rng[1]:
                        self.wait(eng, y.w)
                        for t in y.r.values():
                            self.wait(eng, t)

    def _mark(self, eng, tok, reads, writes):
        for b in reads:
            b.r[eng] = tok
        for b in writes:
            b.w = tok
            b.r = {}

    def op(self, eng, fn, reads=(), writes=()):
        self._deps(eng, reads, writes)
        s = "c_" + eng
        self.cnt[s] += 1
        tok = (s, self.cnt[s])
        h = self.sem[s]
        self.q[eng].append(lambda e, fn=fn, h=h: fn(e).then_inc(h, 1))
        self._mark(eng, tok, reads, writes)
        return tok

    def dma(self, eng, out, in_, sem, reads=(), writes=()):
        if sem not in self.sem:
            self.newsem(sem)
        self._deps(eng, reads, writes)
        self.cnt[sem] += 16
        tok = (sem, self.cnt[sem])
        h = self.sem[sem]
        self.q[eng].append(lambda e, out=out, in_=in_, h=h: e.dma_start(out=out, in_=in_).then_inc(h, 16))
        self._mark("dma_" + sem, tok, reads, writes)
        return tok

    def emit(self):
        nc = self.nc
        with nc.Block() as block:
            @block.tensor
            def _(e):
                for f in self.q["pe"]:
                    f(e)

            @block.scalar
            def _(e):
                for f in self.q["act"]:
                    f(e)

            @block.vector
            def _(e):
                for f in self.q["dve"]:
                    f(e)

            @block.gpsimd
            def _(e):
                for f in self.q["pool"]:
                    f(e)

            @block.sync
            def _(e):
                for f in self.q["sp"]:
                    f(e)


def build(mode):
    nc = bass.Bass("TRN2", target_bir_lowering=False)
    do_l0 = mode in ("fused", "A")
    do_att = mode in ("fused", "B")

    def din(name, shape, dt=F32):
        return nc.dram_tensor(name, list(shape), dt, kind="ExternalInput")

    def dout(name, shape, dt=F32):
        return nc.dram_tensor(name, list(shape), dt, kind="ExternalOutput")

    ident_d = din("ident", [128, 128], BF16)
    identf_d = din("identf", [128, 128], F32)
    lnp_d = din("lnp", [8, D])
    if do_l0:
        x_d = din("x_own", [T, D])
        halo_d = din("x_halo", [NB, 16, D])
        amat_d = din("amat", [128, 12, 128], BF16)
        poolw_d = din("pool_w", [4, 256, 256])
        pscale_d = din("pool_scale", [1, D])
        wqkv_d = din("w_qkv", [D, 3 * D])
    wg_d = din("w_gate", [2, D, DFF])
    wu_d = din("w_up", [2, D, DFF])
    wdn_d = din("w_down", [2, DFF, D])
    if do_att:
        wo_d = din("w_o", [D, D])
        lam_d = din("lam_p", [1, 256])
        sg_d = din("subln_g", [1, 128])
        rel_d = din("rel_table", [32, NH])
        tbg_d = din("tb_g", [128, NH, 3, 128])
        tbm_d = din("tb_m", [128, 3, 128])
        out_d = dout("out", [T, D])

    if mode == "fused":
        kT_own = nc.dram_tensor("kT_own", [NH * 128, T], BF16)
        v_own = nc.dram_tensor("v_own", [NH * 128, NB * VW], BF16)
        kT_all = nc.dram_tensor("kT_all", [2 * NH * 128, T], BF16)
        v_all = nc.dram_tensor("v_all", [2 * NH * 128, NB * VW], BF16)
    elif mode == "A":
        kT_own = dout("kT_own", [NH * 128, T], BF16)
        v_own = dout("v_own", [NH * 128, NB * VW], BF16)
        x1_d = dout("x1", [T, D])
        qT_d = dout("qT", [128, NH, T], BF16)
    else:
        kT_all = din("kT_all", [2 * NH * 128, T], BF16)
        v_all = din("v_all", [2 * NH * 128, NB * VW], BF16)
        x1_d = din("x1", [T, D])
        qT_d = din("qT", [128, NH, T], BF16)

    with ExitStack() as es:
        def sb(name, shape, dt):
            return es.enter_context(nc.sbuf_tensor(name, list(shape), dt))

        def ps(name, shape, dt):
            return es.enter_context(nc.psum_tensor(name, list(shape), dt))

        resid = sb("resid", [128, NB, D], F32)
        xT = sb("xT", [128, 8, T], BF16)
        lnp_t = sb("lnp_t", [128, 2, D], F32)
        zb = sb("zb", [128, D], F32)
        identf = sb("identf_s", [128, 128], F32)
        sg = sb("sg", [128, 2, 512], BF16)
        f32m = sb("f32m", [128, 1536], F32)
        tmask = sb("tmask", [128, 384], F32)
        osb = sb("osb", [128, 2, 128], F32)
        sgb = sb("sgb", [128, 128], F32)
        ident = sb("ident_s", [128, 128], BF16)
        st = sb("st", [128, 2, 2, 6], F32)
        mv = sb("mv", [128, 2, 2], F32)
        rs = sb("rs", [128, 2, 4], F32)
        negone = sb("negone", [128, 1], F32)
        sm = sb("sm", [128, 64], F32)
        cb = sb("cb", [128, NH], F32)
        lamt = sb("lamt", [128, 256], F32)
        AR = 43008
        arena = sb("arena", [128, AR], BF16)

        def av(lo, n, pat=None, **kw):
            a = arena[:, lo:lo + n]
            return a.rearrange(pat, **kw) if pat else a

        hT = av(0, 11264, "p (a b) -> p a b", a=11)
        wd = av(11264, 11264, "p (a b) -> p a b", a=11)
        gu = [av(22528 + k * 4096, 4096, "p (g c f) -> p g c f", g=2, c=8) for k in range(3)]
        halo = av(0, 4096, "p (a b) -> p a b", a=4)
        xin_bf = av(4096, 2048, "p (a b) -> p a b", a=2)
        pmT = av(6144, 2048, "p (a c t) -> p a c t", a=2, c=8)
        amat = av(34816, 1536, "p (a b) -> p a b", a=12)
        wp = av(36352, 2048, "p (g k d) -> p g k d", g=4, k=2)
        ps_bc = f32m[:, 0:1024]
        wq = [av(22528 + k * 4096, 4096, "p (c f) -> p c f", c=8) for k in range(3)]
        kst = [av(34816 + k * 2048, 2048) for k in range(2)]
        vst = [av(38912 + k * 1152, 1152, "p (h v) -> p h v", h=NH) for k in range(2)]
        qT = av(0, 16384, "p (h t) -> p h t", h=NH)
        kTt = [av(16384 + k * 8704, 4096, "p (s t) -> p s t", s=2) for k in range(2)]
        vt = [av(16384 + k * 8704 + 4096, 4608, "p (s j v) -> p s j v", s=2, j=NB) for k in range(2)]
        pT = [av(33792 + k * 512, 512) for k in range(4)]
        pTs = [av(35840 + k * 384, 384) for k in range(4)]
        onb = [av(37376 + k * 128, 128) for k in range(2)]
        wo = av(22528, 8192, "p (a f) -> p a f", a=8)
        Tb = [f32m[:, k * 384:(k + 1) * 384] for k in range(2)]
        tmpA = [f32m[:, 768 + k * 384:768 + (k + 1) * 384] for k in range(2)]

        pq = [ps("pq%d" % k, [128, 1024], F32) for k in range(4)]
        pb8 = [pq[k // 2][:, (k % 2) * 512:(k % 2 + 1) * 512] for k in range(8)]
        pb = pb8[:6]

        P = Prog(nc, es)
        B = {}

        RNG = {"hT0": (0, 11264), "hT1": (0, 11264), "wd": (11264, 22528),
               "gu0": (22528, 26624), "gu1": (26624, 30720), "gu2": (30720, 34816),
               "halo0": (0, 4096), "halo1": (0, 4096), "halo2": (0, 4096), "halo3": (0, 4096),
               "xin0": (4096, 5120), "xin1": (5120, 6144), "pmT0": (6144, 7168), "pmT1": (7168, 8192),
               "amat": (34816, 36352), "wp": (36352, 38400),
               "kst0": (34816, 36864), "kst1": (36864, 38912), "vst0": (38912, 40064), "vst1": (40064, 41216),
               "qT": (0, 16384), "kv0": (16384, 25088), "kv1": (25088, 33792),
               "pT0": (33792, 34816), "pT2": (34816, 35840),
               "pTs0": (35840, 36608), "pTs2": (36608, 37376),
               "onb0": (37376, 37504), "onb1": (37504, 37632),
               "wo": (22528, 30720), "zb1": (38400, 40448)}

        def buf(name):
            if name not in B:
                B[name] = Buf(name, RNG.get(name))
                if name in RNG:
                    P.ranged.append(B[name])
            return B[name]

        bank = [buf("bank%d" % k) for k in range(8)]

        P.dma("sp", ident[:], ident_d[:, :], "ld_ident", writes=[buf("ident")])
        P.dma("sp", identf[:], identf_d[:, :], "ld_identf", writes=[buf("identf")])
        zbs = [zb[:], arena[:, 38400:40448].bitcast(F32)]
        P.op("dve", lambda e: e.memset(negone[:, :], -1.0), writes=[buf("negone")])

        def load_lnp(idx):
            P.dma("sp", lnp_t[:, 0, :], lnp_d[2 * idx:2 * idx + 1, :].partition_broadcast(128), "ld_lnp",
                  writes=[buf("lnp")])
            P.dma("sp", lnp_t[:, 1, :], lnp_d[2 * idx + 1:2 * idx + 2, :].partition_broadcast(128), "ld_lnp",
                  writes=[])
            B["lnp"].w = ("ld_lnp", P.cnt["ld_lnp"])

        ln_ctr = [0]

        def emit_ln(zsrc, zbuf_list, blk, want_T, us):
            k = ln_ctr[0] % 2
            ln_ctr[0] += 1
            bst, bmv, brs = buf("st%d" % k), buf("mv%d" % k), buf("rs%d" % k)
            zt = zbs[us]
            zbb = buf("zb%d" % us)

            def f_stats(e, k=k, zsrc=zsrc):
                e.bn_stats(st[:, k, 0, :], zsrc[:, 0:512])
                return e.bn_stats(st[:, k, 1, :], zsrc[:, 512:1024])
            P.op("dve", f_stats, reads=zbuf_list, writes=[bst])
            P.op("dve", lambda e, k=k: e.bn_aggr(mv[:, k, :], st[:, k, :, :]), reads=[bst], writes=[bmv])
            P.op("act", lambda e, k=k: e.activation(out=rs[:, k, 0:1], in_=mv[:, k, 1:2], func=AF.Sqrt,
                                                    bias=EPS, scale=1.0),
                 reads=[bmv], writes=[brs])
            P.op("dve", lambda e, k=k: e.reciprocal(rs[:, k, 1:2], rs[:, k, 0:1]), reads=[brs], writes=[brs])
            P.op("dve", lambda e, k=k: e.tensor_scalar(rs[:, k, 2:3], mv[:, k, 0:1], negone[:, 0:1], rs[:, k, 1:2],
                                                       ALU.mult, ALU.mult), reads=[bmv, brs, buf("negone")], writes=[brs])
            P.op("act", lambda e, k=k, zsrc=zsrc, zt=zt: e.activation(out=zt, in_=zsrc, func=AF.Identity,
                                                                      bias=rs[:, k, 2:3], scale=rs[:, k, 1:2]),
                 reads=zbuf_list + [brs], writes=[zbb])
            P.op("pool", lambda e, zt=zt: e.tensor_tensor(zt, zt, lnp_t[:, 0, :], ALU.mult),
                 reads=[buf("lnp")], writes=[zbb])
            rb = buf("resid%d" % blk)
            P.op("dve", lambda e, blk=blk, zt=zt: e.tensor_tensor(resid[:, blk, :], zt, lnp_t[:, 1, :], ALU.add),
                 reads=[zbb, buf("lnp")], writes=[rb])
            if not want_T:
                return None

            def do_T(blk=blk, rb=rb):
                def f_tp(e):
                    for c in range(8):
                        ins = e.transpose(pb8[6 + c // 4][:, (c % 4) * 128:(c % 4 + 1) * 128],
                                          resid[:, blk, c * 128:(c + 1) * 128], identf[:])
                    return ins
                P.op("pe", f_tp, reads=[rb, buf("identf")], writes=[bank[6], bank[7]])
                xb_ = buf("xT%d" % blk)
                P.op("act", lambda e: e.copy(xT[:, 0:4, blk * 128:(blk + 1) * 128],
                                             pb8[6][:, :].rearrange("p (c t) -> p c t", c=4)),
                     reads=[bank[6]], writes=[xb_])
                P.op("act", lambda e: e.copy(xT[:, 4:8, blk * 128:(blk + 1) * 128],
                                             pb8[7][:, :].rearrange("p (c t) -> p c t", c=4)),
                     reads=[bank[7]], writes=[xb_])
            return do_T

        def emit_ffn(L, lnp_idx, last):
            load_lnp(lnp_idx)
            wgv = wg_d[L].rearrange("(c p) f -> p c f", p=128)
            wuv = wu_d[L].rearrange("(c p) f -> p c f", p=128)
            wdv = wdn_d[L].rearrange("(a p) d -> p a d", p=128)
            gctr = 0
            defer_T = []
            for tt in range(2):
                for part in range(2):
                    groups = [(0, 2), (2, 2), (4, 2), (6, 2), (8, 2), (10, 1)]
                    gtok = {}

                    def load_gu(gi, part=part):
                        j0_, nj_ = groups[gi]
                        gb_ = (gctr + gi) % 3
                        f0 = (part * 11 + j0_) * 128
                        w = nj_ * 128
                        gbuf_ = buf("gu%d" % gb_)
                        P.dma("pool", gu[gb_][:, 0, :, 0:w], wgv[:, :, f0:f0 + w], "ld_gu%d" % gb_, writes=[gbuf_])
                        P.dma("pool", gu[gb_][:, 1, :, 0:w], wuv[:, :, f0:f0 + w], "ld_gu%d" % gb_, writes=[])
                        gbuf_.w = ("ld_gu%d" % gb_, P.cnt["ld_gu%d" % gb_])
                    for gi in range(3):
                        load_gu(gi)
                    P.dma("pool", wd[:, :, :], wdv[:, part * 11:(part + 1) * 11, :], "ld_wd", writes=[buf("wd")])
                    for gi, (j0, nj) in enumerate(groups):
                        gb = (gctr + gi) % 3
                        gbuf = buf("gu%d" % gb)
                        if gi >= 3:
                            load_gu(gi)
                        for j in range(nj):
                            jl = j0 + j
                            for ts in range(2):
                                k = (jl * 2 + ts) % 2
                                t0 = tt * 1024 + ts * 512
                                xbufs = [buf("xT%d" % (t0 // 128 + q)) for q in range(4)]

                                def f_gu(e, gb=gb, j=j, t0=t0, k=k):
                                    for c in range(8):
                                        e.matmul(pb[k][:, :], gu[gb][:, 0, c, j * 128:(j + 1) * 128],
                                                 xT[:, c, t0:t0 + 512], start=(c == 0), stop=(c == 7))
                                    for c in range(8):
                                        ins = e.matmul(pb[2 + k][:, :], gu[gb][:, 1, c, j * 128:(j + 1) * 128],
                                                       xT[:, c, t0:t0 + 512], start=(c == 0), stop=(c == 7))
                                    return ins
                                P.op("pe", f_gu, reads=[gbuf] + xbufs, writes=[bank[k], bank[2 + k]])
                                P.op("act", lambda e, k=k: e.activation(out=sg[:, k, :], in_=pb[k][:, :], func=AF.Silu),
                                     reads=[bank[k]], writes=[buf("sg%d" % k)])
                                P.op("dve", lambda e, k=k, jl=jl, ts=ts: e.tensor_tensor(
                                    hT[:, jl, ts * 512:(ts + 1) * 512], sg[:, k, :], pb[2 + k][:, :], ALU.mult),
                                    reads=[buf("sg%d" % k), bank[2 + k]], writes=[buf("hT%d" % ts)])
                        for _ in range(2):
                            if defer_T:
                                defer_T.pop(0)()
                    gctr += len(groups)
                    for b8 in range(8):
                        blk = tt * 8 + b8
                        ts = b8 // 4
                        rb = buf("resid%d" % blk)
                        for dh in range(2):
                            k = (b8 * 2 + dh) % 2

                            def f_dn(e, b8=b8, dh=dh, k=k):
                                for j in range(11):
                                    ins = e.matmul(pb[4 + k][:, :], hT[:, j, b8 * 128:(b8 + 1) * 128],
                                                   wd[:, j, dh * 512:(dh + 1) * 512], start=(j == 0), stop=(j == 10))
                                return ins
                            P.op("pe", f_dn, reads=[buf("hT%d" % ts), buf("wd")], writes=[bank[4 + k]])
                            if part == 0:
                                P.op("dve", lambda e, blk=blk, dh=dh, k=k: e.scalar_tensor_tensor(
                                    resid[:, blk, dh * 512:(dh + 1) * 512], resid[:, blk, dh * 512:(dh + 1) * 512],
                                    ALPHA, pb[4 + k][:, :], ALU.mult, ALU.add),
                                    reads=[bank[4 + k]], writes=[rb])
                            else:
                                P.op("dve", lambda e, blk=blk, dh=dh, k=k: e.tensor_tensor(
                                    resid[:, blk, dh * 512:(dh + 1) * 512], resid[:, blk, dh * 512:(dh + 1) * 512],
                                    pb[4 + k][:, :], ALU.add),
                                    reads=[bank[4 + k]], writes=[rb])
                        if part == 1:
                            dT = emit_ln(resid[:, blk, :], [rb], blk, want_T=not last, us=blk % 2)
                            if dT is not None:
                                defer_T.append(dT)
                            if last:
                                P.dma("sp", out_d[blk * 128:(blk + 1) * 128, :], resid[:, blk, :], "st_out", reads=[rb])
            while defer_T:
                defer_T.pop(0)()

        if do_l0:
            P.dma("sp", amat[:, :, :], amat_d[:, :, :], "ld_amat", writes=[buf("amat")])
            P.dma("sp", ps_bc, pscale_d[0:1, :].partition_broadcast(128), "ld_psbc", writes=[buf("f32m")])
            load_lnp(0)
            for q4 in range(4):
                P.dma("sp", resid[:, q4 * 4:(q4 + 1) * 4, :],
                      x_d[q4 * 512:(q4 + 1) * 512, :].rearrange("(b p) d -> p b d", p=128), "ld_x%d" % q4,
                      writes=[buf("resid%d" % (q4 * 4 + q)) for q in range(4)])
            P.dma("pool", wp[:, :, :, :], poolw_d.rearrange("g (k p) d -> p g k d", p=128), "ld_wp", writes=[buf("wp")])

            def f_wps(e):
                for g in range(4):
                    for kc in range(2):
                        ins = e.tensor_tensor(wp[:, g, kc, :], wp[:, g, kc, :], ps_bc[:, g * 256:(g + 1) * 256], ALU.mult)
                return ins
            P.op("dve", f_wps, reads=[buf("f32m")], writes=[buf("wp")])

            mixT = []

            def mix_front_a(i):
                rb = buf("resid%d" % i)
                hs = i % 4
                hb = buf("halo%d" % hs)
                P.dma("pool", halo[0:16, hs, :], halo_d[i, :, :], "ld_halo%d" % hs, writes=[hb])
                k2 = i % 2
                xb = buf("xin%d" % k2)
                P.op("act", lambda e: e.copy(xin_bf[:, k2, :], resid[:, i, :]), reads=[rb], writes=[xb])
                a0 = 0 if i == 0 else 4

                def f_pm(e):
                    for c in range(8):
                        g = c // 2
                        o = pb[c // 4][:, (c % 4) * 128:(c % 4 + 1) * 128]
                        e.matmul(o, xin_bf[:, k2, c * 128:(c + 1) * 128], amat[:, a0 + g, :], start=True, stop=False)
                        ins = e.matmul(o[:, 0:16], halo[0:16, hs, c * 128:(c + 1) * 128], amat[0:16, 8 + g, 0:16],
                                       start=False, stop=True)
                    return ins
                P.op("pe", f_pm, reads=[xb, hb, buf("amat")], writes=[bank[0], bank[1]])
                pmb = buf("pmT%d" % k2)

                def f_pmT(e):
                    e.copy(pmT[:, k2, 0:4, :], pb[0][:, :].rearrange("p (c t) -> p c t", c=4))
                    return e.copy(pmT[:, k2, 4:8, :], pb[1][:, :].rearrange("p (c t) -> p c t", c=4))
                P.op("act", f_pmT, reads=[bank[0], bank[1]], writes=[pmb])

            def mix_front_b(i):
                k2 = i % 2
                pmb = buf("pmT%d" % k2)

                def f_mix(e):
                    for g in range(4):
                        for kc in range(2):
                            ins = e.matmul(pb[2 + g // 2][:, (g % 2) * 256:(g % 2 + 1) * 256],
                                           pmT[:, k2, 2 * g + kc, :], wp[:, g, kc, :], start=(kc == 0), stop=(kc == 1))
                    return ins
                P.op("pe", f_mix, reads=[pmb, buf("wp")], writes=[bank[2], bank[3]])

            def mix_z(i):
                rb = buf("resid%d" % i)
                k2 = i % 2
                zt = zbs[k2]
                zbb = buf("zb%d" % k2)

                def f_z1(e):
                    e.scalar_tensor_tensor(zt[:, 0:512], resid[:, i, 0:512], ALPHA, pb[2][:, :], ALU.mult, ALU.add)
                    return e.scalar_tensor_tensor(zt[:, 512:1024], resid[:, i, 512:1024], ALPHA, pb[3][:, :],
                                                  ALU.mult, ALU.add)
                P.op("dve", f_z1, reads=[bank[2], bank[3], rb], writes=[zbb])

            mix_front_a(0)
            mix_front_b(0)
            for i in range(NB):
                if i + 1 < NB:
                    mix_front_a(i + 1)
                mix_z(i)
                if i + 1 < NB:
                    mix_front_b(i + 1)
                k2 = i % 2
                mixT.append(emit_ln(zbs[k2], [buf("zb%d" % k2)], i, want_T=True, us=k2))
                while len(mixT) > 1:
                    mixT.pop(0)()
            while mixT:
                mixT.pop(0)()

            emit_ffn(0, 1, last=False)

            wqv = wqkv_d.rearrange("(c p) f -> p c f", p=128)
            allx = [buf("xT%d" % q) for q in range(NB)]
            for k in range(2):
                P.op("pool", lambda e, k=k: e.memset(vst[k][:, :, 128:VW], 0.0), writes=[buf("vst%d" % k)])
                P.op("pool", lambda e, k=k: e.memset(vst[k][:, :, 128:129], 1.0), writes=[buf("vst%d" % k)])
            bctr = 0
            for gidx, grp in enumerate((2, 3, 4, 5, 0, 1)):
                wbi = gidx % 3
                wb = buf("gu%d" % wbi)
                P.dma("pool", wq[wbi][:, :, :], wqv[:, :, grp * 512:(grp + 1) * 512], "ld_gu%d" % wbi, writes=[wb])
                if grp < 4:
                    for hh in range(4):
                        h = (grp % 2) * 4 + hh
                        if grp >= 2:
                            ks = h % 2
                            kb_ = buf("kst%d" % ks)
                        for ts in range(4):
                            k = bctr % 2
                            bctr += 1

                            def f_qk(e, wbi=wbi, hh=hh, ts=ts, k=k):
                                for c in range(8):
                                    ins = e.matmul(pb[k][:, :], wq[wbi][:, c, hh * 128:(hh + 1) * 128],
                                                   xT[:, c, ts * 512:(ts + 1) * 512], start=(c == 0), stop=(c == 7))
                                return ins
                            P.op("pe", f_qk, reads=[wb] + allx[ts * 4:ts * 4 + 4], writes=[bank[k]])
                            if grp < 2:
                                P.op("act", lambda e, h=h, ts=ts, k=k: e.copy(
                                    qT[:, h, ts * 512:(ts + 1) * 512], pb[k][:, :]),
                                    reads=[bank[k]], writes=[buf("qT")])
                            else:
                                P.op("act", lambda e, ks=ks, ts=ts, k=k: e.copy(
                                    kst[ks][:, ts * 512:(ts + 1) * 512], pb[k][:, :]),
                                    reads=[bank[k]], writes=[kb_])
                        if grp >= 2:
                            P.dma("sp", kT_own[h * 128:(h + 1) * 128, :], kst[ks][:, :], "st_kst%d" % ks, reads=[kb_])
                else:
                    half = grp - 4
                    for blk in range(NB):
                        k = bctr % 2
                        bctr += 1
                        vs = blk % 2
                        vb_ = buf("vst%d" % vs)

                        def f_v(e, wbi=wbi, blk=blk, k=k):
                            for c in range(8):
                                ins = e.matmul(pb[k][:, :], xT[:, c, blk * 128:(blk + 1) * 128], wq[wbi][:, c, :],
                                               start=(c == 0), stop=(c == 7))
                            return ins
                        P.op("pe", f_v, reads=[wb, allx[blk]], writes=[bank[k]])
                        P.op("dve", lambda e, vs=vs, half=half, k=k: e.tensor_copy(
                            vst[vs][:, half * 4:(half + 1) * 4, 0:128],
                            pb[k][:, :].rearrange("p (h v) -> p h v", h=4)),
                            reads=[bank[k]], writes=[vb_])
                        dst = v_own.ap().rearrange("(h k) (j v) -> k h j v", k=128, v=VW)[:, half * 4:(half + 1) * 4, blk, :]
                        P.dma("sp", dst, vst[vs][:, half * 4:(half + 1) * 4, :], "st_vst%d" % vs, reads=[vb_])
            if mode == "A":
                for blk in range(NB):
                    P.dma("sp", x1_d[blk * 128:(blk + 1) * 128, :], resid[:, blk, :], "st_x1",
                          reads=[buf("resid%d" % blk)])
                P.dma("sp", qT_d[:, :, :], qT[:, :, :], "st_qT", reads=[buf("qT")])

        kv_ready = None
        if mode == "fused":
            for nm in ("st_kst0", "st_kst1", "st_vst0", "st_vst1"):
                P.wait("pool", (nm, P.cnt[nm]))
            P.newsem("cc")
            groups = [[2 * b, 2 * b + 1] for b in range(N_CORES // 2)]
            hcc = P.sem["cc"]

            if not NOCC:
                for h in range(NH):
                    def f_k(e, h=h):
                        return e.collective_compute("AllGather", ALU.bypass, replica_groups=groups,
                                                    ins=[kT_own[h * 128:(h + 1) * 128, :]],
                                                    outs=[kT_all[h * 256:(h + 1) * 256, :]])

                    def f_v(e, h=h):
                        return e.collective_compute("AllGather", ALU.bypass, replica_groups=groups,
                                                    ins=[v_own[h * 128:(h + 1) * 128, :]],
                                                    outs=[v_all[h * 256:(h + 1) * 256, :]])
                    P.q["pool"].append(lambda e, f=f_k: f(e).then_inc(hcc, 1))
                    P.q["pool"].append(lambda e, f=f_v: f(e).then_inc(hcc, 1))
                kv_ready = True

        if do_att:
            if mode == "B":
                for q4 in range(4):
                    P.dma("sp", resid[:, q4 * 4:(q4 + 1) * 4, :],
                          x1_d[q4 * 512:(q4 + 1) * 512, :].rearrange("(b p) d -> p b d", p=128), "ld_x%d" % q4,
                          writes=[buf("resid%d" % (q4 * 4 + q)) for q in range(4)])
                P.dma("sp", qT[:, :, :], qT_d[:, :, :], "ld_qT", writes=[buf("qT")])
            P.dma("sp", cb[:, :], rel_d[31:32, :].partition_broadcast(128), "ld_cb", writes=[buf("cb")])
            P.dma("sp", lamt[:, :], lam_d[0:1, :].partition_broadcast(128), "ld_lam", writes=[buf("lamt")])
            P.dma("sp", sgb[:, :], sg_d[0:1, :].partition_broadcast(128), "ld_sgb", writes=[buf("sgb")])
            P.op("dve", lambda e: e.tensor_scalar(sgb[:, :], sgb[:, :], (1.0 - LAMBDA_INIT) * math.sqrt(128.0), None,
                                                  ALU.mult), reads=[buf("sgb")], writes=[buf("sgb")])
            lv = lamt[:, :].rearrange("p (a b d) -> p a b d", a=2, b=2)
            P.op("dve", lambda e: e.tensor_tensor(osb[:, 0, :].rearrange("p (a d) -> p a d", a=2),
                                                  lv[:, :, 0, :], lv[:, :, 1, :], ALU.mult),
                 reads=[buf("lamt")], writes=[buf("osb0")])
            P.op("dve", lambda e: e.tensor_reduce(sm[:, 0:2], osb[:, 0, :].rearrange("p (a d) -> p a d", a=2),
                                                  AX.X, ALU.add), reads=[buf("osb0")], writes=[buf("sm_lam")])
            P.op("act", lambda e: e.activation(out=sm[:, 2:4], in_=sm[:, 0:2], func=AF.Exp),
                 reads=[buf("sm_lam")], writes=[buf("sm_lam")])
            P.op("dve", lambda e: e.tensor_tensor(sm[:, 4:5], sm[:, 3:4], sm[:, 2:3], ALU.subtract),
                 reads=[buf("sm_lam")], writes=[buf("sm_lam")])
            P.op("dve", lambda e: e.tensor_scalar(sm[:, 4:5], sm[:, 4:5], -LAMBDA_INIT, None, ALU.add),
                 reads=[buf("sm_lam")], writes=[buf("sm_lam")])
            lamb = buf("sm_lam")
            P.dma("sp", tmask[:, :].rearrange("p (a q) -> p a q", a=3), tbm_d[:, :, :], "ld_tm", writes=[buf("tmask")])

            kall = kT_all.ap().rearrange("(h s k) t -> k s h t", s=2, h=NH)
            vall = v_all.ap().rearrange("(h s k) (j v) -> k s h j v", s=2, h=NH, v=VW)
            LOOK = int(os.environ.get('K_LOOK', '1'))
            sp_ctr = [0]
            cn_ctr = [0]
            ep_ctr = [0]
            pend = []
            tpend = []

            b1pend = []
            b2pend = []

            def drain_epilogue(keep1, keep2):
                while len(b2pend) > keep2:
                    b2pend.pop(0)()
                while len(b1pend) > keep1:
                    b1pend.pop(0)()

            def emit_epilogue(h, i, ob, hb):
                obank = bank[4 + ob]
                ek = ep_ctr[0] % 2
                ep_ctr[0] += 1
                sb_ = buf("sm_e%d" % ek)
                o0 = 8 + ek * 8
                ov = pb[4 + ob][:, :].rearrange("p (c w) -> p c w", c=2)
                while len(b2pend) > 0:
                    b2pend.pop(0)()
                while len(b1pend) > 0:
                    b1pend.pop(0)()
                P.op("dve", lambda e: e.reciprocal(sm[:, o0:o0 + 2], ov[:, :, 128]), reads=[obank], writes=[sb_])
                P.op("dve", lambda e: e.tensor_tensor(sm[:, o0 + 2:o0 + 3], sm[:, o0 + 1:o0 + 2], sm[:, 4:5], ALU.mult),
                     reads=[sb_, lamb], writes=[sb_])
                osbb = buf("osb%d" % ek)
                P.op("dve", lambda e: e.tensor_scalar(osb[:, ek, :], pb[4 + ob][:, 0:128], sm[:, o0:o0 + 1], None, ALU.mult),
                     reads=[obank, sb_], writes=[osbb])
                P.op("dve", lambda e: e.scalar_tensor_tensor(
                    osb[:, ek, :], pb[4 + ob][:, 256:384], sm[:, o0 + 2:o0 + 3], osb[:, ek, :], ALU.mult, ALU.add),
                    reads=[obank, sb_], writes=[osbb])
                P.op("dve", lambda e: e.scalar_tensor_tensor(
                    sg[:, 0, 0:128], osb[:, ek, :], 1.0, osb[:, ek, :], ALU.mult, ALU.mult, accum_out=sm[:, o0 + 3:o0 + 4]),
                    reads=[osbb], writes=[sb_, buf("junk")])

                def stage_b1():
                    P.op("act", lambda e: e.activation(out=sm[:, o0 + 4:o0 + 5], in_=sm[:, o0 + 3:o0 + 4], func=AF.Ln,
                                                       bias=128.0 * EPS, scale=1.0), reads=[sb_], writes=[sb_])
                    P.op("act", lambda e: e.activation(out=sm[:, o0 + 5:o0 + 6], in_=sm[:, o0 + 4:o0 + 5], func=AF.Exp,
                                                       scale=-0.5), reads=[sb_], writes=[sb_])
                    P.op("dve", lambda e: e.scalar_tensor_tensor(
                        onb[ek][:, :], osb[:, ek, :], sm[:, o0 + 5:o0 + 6], sgb[:, :], ALU.mult, ALU.mult),
                        reads=[osbb, sb_, buf("sgb")], writes=[buf("onb%d" % ek)])

                    def stage_b2():
                        tpv = pb[4 + ob][:, 448:512].bitcast(BF16)
                        P.op("pe", lambda e: e.transpose(tpv, onb[ek][:, :], ident[:]),
                             reads=[buf("onb%d" % ek), buf("ident")], writes=[obank])
                        P.op("dve", lambda e: e.tensor_copy(xT[:, h, i * 128:(i + 1) * 128], tpv),
                             reads=[obank], writes=[buf("xT%d" % i)])
                    b2pend.append(stage_b2)
                b1pend.append(stage_b1)

            def flush(upto):
                while len(pend) > upto:
                    pend.pop(0)()

            for h in range(NH):
                hb = h % 2
                kvb = buf("kv%d" % hb)
                tbb = buf("tb%d" % hb)
                if kv_ready is not None:
                    P.wait("sp", ("cc", 2 * h + 2))
                P.dma("sp", kTt[hb][:, :, :], kall[:, :, h, :], "ld_kv%d" % hb, writes=[kvb])
                P.dma("sp", vt[hb][:, :, :, :], vall[:, :, h, :, :], "ld_kv%d" % hb, writes=[])
                kvb.w = ("ld_kv%d" % hb, P.cnt["ld_kv%d" % hb])
                P.dma("sp", Tb[hb].rearrange("p (a q) -> p a q", a=3), tbg_d[:, h, :, :], "ld_tb%d" % hb, writes=[tbb])
                P.op("dve", lambda e, hb=hb: e.tensor_tensor(Tb[hb], Tb[hb], tmask[:, :], ALU.add),
                     reads=[buf("tmask")], writes=[tbb])

                for i in range(NB):
                    ob = (h * NB + i) % 2
                    obank = bank[4 + ob]
                    special = [(0, i, 0), (1, i, 1)] + ([(1, i - 1, 2)] if i >= 1 else [])
                    consts = [(0, j) for j in range(i)] + [(1, j) for j in range(i - 1)]
                    cu = [consts[u:u + 4] for u in range(0, len(consts), 4)]
                    def special_front(blks=special, h=h, i=i, hb=hb, kvb=kvb, tbb=tbb):
                        bks = [bank[6], bank[7]]
                        pbs = [pb8[6], pb8[7]]

                        def f_qk_s(e):
                            for (s_, j, kd) in blks:
                                for c in range(2):
                                    ins = e.matmul(pbs[c][:, kd * 128:(kd + 1) * 128],
                                                   kTt[hb][c * 64:(c + 1) * 64, s_, j * 128:(j + 1) * 128],
                                                   qT[c * 64:(c + 1) * 64, h, i * 128:(i + 1) * 128],
                                                   start=True, stop=True)
                            return ins
                        P.op("pe", f_qk_s, reads=[kvb, buf("qT")], writes=bks)
                        nsp = len(blks)

                        def f_sadd(e):
                            for c in range(2):
                                ins = e.scalar_tensor_tensor(
                                    tmpA[c][:, 0:nsp * 128], pbs[c][:, 0:nsp * 128], 0.125, Tb[hb][:, 0:nsp * 128],
                                    ALU.mult, ALU.add)
                            return ins
                        P.op("dve", f_sadd, reads=bks + [tbb], writes=[buf("tmpA")])
                    special_front()

                    plist = [("c", u_) for u_ in cu] + [("s", special)]
                    for pn, (kind, blks) in enumerate(plist):
                        first_pair = (pn == 0)
                        last_pair = (pn == len(plist) - 1)
                        if kind == "s":
                            sps = sp_ctr[0] % 2
                            sp_ctr[0] += 1

                            nsp = len(blks)
                            tsb = buf("tmpA")
                            psb = buf("pTs%d" % (2 * sps))
                            ps2 = av(35840 + sps * 768, 768, "p (c n) -> p c n", c=2)
                            P.op("act", lambda e, nsp=nsp, ps2=ps2: e.activation(
                                out=ps2[:, :, 0:nsp * 128],
                                in_=f32m[:, 768:1536].rearrange("p (c n) -> p c n", c=2)[:, :, 0:nsp * 128], func=AF.Exp),
                                reads=[tsb], writes=[psb])
                            srcs = [(pTs[2 * sps], psb), (pTs[2 * sps + 1], psb)]
                            cols = [kd for (_, _, kd) in blks]
                            kbl = [(s_, j) for (s_, j, _) in blks]
                        else:
                            cps = cn_ctr[0] % 2
                            cn_ctr[0] += 1
                            bks = [bank[2 * cps], bank[2 * cps + 1]]
                            pbs = [pb[2 * cps], pb[2 * cps + 1]]

                            def f_qk_c(e, blks=blks, h=h, i=i, hb=hb, pbs=pbs):
                                for n, (s_, j) in enumerate(blks):
                                    for c in range(2):
                                        ins = e.matmul(pbs[c][:, n * 128:(n + 1) * 128],
                                                       kTt[hb][c * 64:(c + 1) * 64, s_, j * 128:(j + 1) * 128],
                                                       qT[c * 64:(c + 1) * 64, h, i * 128:(i + 1) * 128],
                                                       start=True, stop=True)
                                return ins
                            P.op("pe", f_qk_c, reads=[kvb, buf("qT")], writes=bks)
                            nb_ = len(blks)
                            ptb = buf("pT%d" % (2 * cps))
                            pt2 = av(33792 + cps * 1024, 1024, "p (c n) -> p c n", c=2)
                            P.op("act", lambda e, cps=cps, nb_=nb_, h=h, pt2=pt2: e.activation(
                                out=pt2[:, :, 0:nb_ * 128],
                                in_=pq[cps][:, :].rearrange("p (c n) -> p c n", c=2)[:, :, 0:nb_ * 128], func=AF.Exp,
                                bias=cb[:, h:h + 1], scale=0.125), reads=bks + [buf("cb")], writes=[ptb])
                            srcs = [(pT[2 * cps], ptb), (pT[2 * cps + 1], ptb)]
                            cols = list(range(nb_))
                            kbl = list(blks)

                        def mk_pv(kbl=kbl, cols=cols, srcs=srcs, hb=hb, ob=ob, obank=obank, kvb=kvb,
                                  first_pair=first_pair, last_pair=last_pair, h=h, i=i):
                            def f_pv(e):
                                for c in range(2):
                                    for n, (s_, j) in enumerate(kbl):
                                        ins = e.matmul(pb[4 + ob][:, c * 256:c * 256 + 129],
                                                       srcs[c][0][:, cols[n] * 128:(cols[n] + 1) * 128],
                                                       vt[hb][:, s_, j, 0:129],
                                                       start=(first_pair and c == 0 and n == 0),
                                                       stop=(last_pair and n == len(kbl) - 1),
                                                       skip_group_check=True)
                                return ins

                            def go():
                                P.op("pe", f_pv, reads=[srcs[0][1], srcs[1][1], kvb], writes=[obank])
                                if last_pair:
                                    emit_epilogue(h, i, ob, hb)
                            return go
                        pend.append(mk_pv())
                        flush(LOOK)
            flush(0)
            drain_epilogue(0, 0)
            drain_epilogue(0, 0)

            load_lnp(2)
            wob = buf("wo")
            if os.environ.get('K_BAR'):
                P.wait("pool", ("c_pe", P.cnt["c_pe"]))
            P.dma("pool", wo[:, :, :], wo_d.rearrange("(a p) f -> p a f", p=128), "ld_wo", writes=[wob])
            woT = []

            def wo_front(i):
                bp = 2 * (i % 2)

                def f_wo(e):
                    for half in range(2):
                        for a_ in range(8):
                            ins = e.matmul(pb[bp + half][:, :], xT[:, a_, i * 128:(i + 1) * 128],
                                           wo[:, a_, half * 512:(half + 1) * 512], start=(a_ == 0), stop=(a_ == 7))
                    return ins
                P.op("pe", f_wo, reads=[buf("xT%d" % i), wob], writes=[bank[bp], bank[bp + 1]])

            wo_front(0)
            for i in range(NB):
                rb = buf("resid%d" % i)
                if i + 1 < NB:
                    wo_front(i + 1)
                bp = 2 * (i % 2)
                zt = zbs[i % 2]
                zbb = buf("zb%d" % (i % 2))

                def f_zo(e, i=i, zt=zt, bp=bp):
                    e.scalar_tensor_tensor(zt[:, 0:512], resid[:, i, 0:512], ALPHA, pb[bp][:, :], ALU.mult, ALU.add)
                    return e.scalar_tensor_tensor(zt[:, 512:1024], resid[:, i, 512:1024], ALPHA, pb[bp + 1][:, :],
                                                  ALU.mult, ALU.add)
                P.op("dve", f_zo, reads=[bank[bp], bank[bp + 1], rb], writes=[zbb])
                woT.append(emit_ln(zt, [zbb], i, want_T=True, us=i % 2))
                while len(woT) > 1:
                    woT.pop(0)()
            while woT:
                woT.pop(0)()
            emit_ffn(1, 3, last=True)
            P.wait("sp", ("st_out", P.cnt["st_out"]))
        else:
            for nm in ("st_kst0", "st_kst1", "st_vst0", "st_vst1", "st_x1", "st_qT"):
                P.wait("sp", (nm, P.cnt[nm]))

        P.emit()
    return nc


def _rel_bucket_np(n):
    n = np.maximum(n, 0)
    nf = np.maximum(n, 1).astype(np.float32)
    large = 16 + (np.log(nf / np.float32(16)) / np.float32(math.log(8.0)) * np.float32(16)).astype(np.int32)
    large = np.minimum(large, 31)
    return np.where(n < 16, n, large)


def _amat(rank):
    A = np.zeros((128, 12, 128), np.float32)
    s = np.arange(128)[:, None]
    t = np.arange(128)[None, :]
    for g, w in enumerate(POOL_WINDOWS):
        band = ((t - s) >= 0) & ((t - s) < w)
        eye = (s == t).astype(np.float32)
        diag = band.astype(np.float32) / w - eye
        cnt = np.minimum(t + 1, w).astype(np.float32)
        first = band.astype(np.float32) / cnt - eye
        A[:, 4 + g, :] = diag
        A[:, g, :] = first if rank == 0 else diag
        sh = np.arange(16)[:, None] - 16
        bandh = ((t - sh) >= 0) & ((t - sh) < w)
        A[0:16, 8 + g, :] = bandh.astype(np.float32) / w
    return A.astype(ml_dtypes.bfloat16)


def _bias_idx(rank):
    k = np.arange(128)[:, None]
    q = np.arange(128)[None, :]
    idx = np.zeros((128, 3, 128), np.int64)
    msk = np.zeros((128, 3, 128), np.float32)
    for tdx, delta in enumerate((rank, rank - 1, rank + 1)):
        rel = delta * 128 + q - k
        idx[:, tdx, :] = _rel_bucket_np(rel)
        msk[:, tdx, :] = np.where(rel >= 0, 0.0, NEG)
    return idx, msk


_NC_CACHE = {}


def _get_nc(mode):
    if mode not in _NC_CACHE:
        _NC_CACHE[mode] = build(mode)
    return _NC_CACHE[mode]


FUSED = True
NOCC = False


def kernel(x, pool_w, pool_scale, w_qkv, w_o, lam_p, subln_g, rel_table, w_gate, w_up, w_down,
           ln_mix_g, ln_mix_b, ln_ffn_g, ln_ffn_b):
    f32 = lambda a: np.ascontiguousarray(np.asarray(a, dtype=np.float32))
    x = f32(x)
    Bn, S, _ = x.shape
    lnp = np.stack([f32(ln_mix_g)[0], f32(ln_mix_b)[0], f32(ln_ffn_g)[0], f32(ln_ffn_b)[0],
                    f32(ln_mix_g)[1], f32(ln_mix_b)[1], f32(ln_ffn_g)[1], f32(ln_ffn_b)[1]], 0)
    ident = np.eye(128, dtype=np.float32).astype(ml_dtypes.bfloat16)
    rel_table = f32(rel_table)
    common = {
        "ident": ident, "identf": np.eye(128, dtype=np.float32), "lnp": lnp,
        "w_gate": f32(w_gate), "w_up": f32(w_up), "w_down": f32(w_down),
    }
    l0 = {"pool_w": f32(pool_w)[0], "pool_scale": f32(pool_scale), "w_qkv": f32(w_qkv)[0]}
    l1 = {"w_o": f32(w_o)[0], "lam_p": f32(lam_p).reshape(1, 256), "subln_g": f32(subln_g).reshape(1, 128),
          "rel_table": rel_table}
    per_core = []
    for core in range(N_CORES):
        b, r = core // 2, core % 2
        xb = x[b].reshape(32, 128, D)
        own = xb[r::2]
        halo = np.zeros((NB, 16, D), np.float32)
        for i in range(NB):
            g = 2 * i + r
            if g > 0:
                halo[i] = xb[g - 1][112:128]
        idx, msk = _bias_idx(r)
        tb_g = np.ascontiguousarray(rel_table[idx].transpose(0, 3, 1, 2))
        per_core.append({"x_own": np.ascontiguousarray(own.reshape(T, D)), "x_halo": halo, "amat": _amat(r),
                         "tb_g": tb_g, "tb_m": msk})

    def assemble(outs):
        y = np.zeros((Bn, 32, 128, D), np.float32)
        for core in range(N_CORES):
            b, r = core // 2, core % 2
            y[b, r::2] = outs[core].reshape(NB, 128, D)
        return y.reshape(Bn, S, D)

    if FUSED:
        nc = _get_nc("fused")
        maps = []
        for core in range(N_CORES):
            m = dict(common); m.update(l0); m.update(l1); m.update(per_core[core])
            maps.append(m)
        res = run_bass_kernel_spmd(nc, maps, core_ids=list(range(N_CORES)))
        return assemble([res.results[c]["out"] for c in range(N_CORES)])

    ncA = _get_nc("A")
    mapsA = []
    for core in range(N_CORES):
        m = dict(common); m.update(l0)
        for kk in ("x_own", "x_halo", "amat"):
            m[kk] = per_core[core][kk]
        mapsA.append(m)
    resA = run_bass_kernel_spmd(ncA, mapsA, core_ids=list(range(N_CORES))).results
    ncB = _get_nc("B")
    mapsB = []
    for core in range(N_CORES):
        b = core // 2
        m = dict(common); m.update(l1)
        m["tb_g"] = per_core[core]["tb_g"]; m["tb_m"] = per_core[core]["tb_m"]
        m["x1"] = resA[core]["x1"]; m["qT"] = resA[core]["qT"]
        m["kT_all"] = np.stack([resA[2 * b]["kT_own"].reshape(NH, 128, T),
                                resA[2 * b + 1]["kT_own"].reshape(NH, 128, T)], 1).reshape(2 * NH * 128, T)
        m["v_all"] = np.stack([resA[2 * b]["v_own"].reshape(NH, 128, NB * VW),
                               resA[2 * b + 1]["v_own"].reshape(NH, 128, NB * VW)], 1).reshape(2 * NH * 128, NB * VW)
        mapsB.append(m)
    resB = run_bass_kernel_spmd(ncB, mapsB, core_ids=list(range(N_CORES))).results
    return assemble([resB[c]["out"] for c in range(N_CORES)])
```

```python
import math
import os
from contextlib import ExitStack

import numpy as np
import ml_dtypes

import concourse.bass as bass
import concourse.mybir as mybir
from concourse.bass_utils import run_bass_kernel_spmd

F32 = mybir.dt.float32
BF16 = mybir.dt.bfloat16
AF = mybir.ActivationFunctionType
ALU = mybir.AluOpType
AX = mybir.AxisListType

D = 1024
DFF = 2816
NB = 16
T = NB * 128
NH = 8
VW = 144
ALPHA = 4.0 ** 0.25
EPS = 1e-5
LAMBDA_INIT = 0.8 - 0.6 * math.exp(-0.3 * 1)
NEG = -30000.0
POOL_WINDOWS = (2, 4, 8, 16)
N_CORES = 8


class Buf:
    __slots__ = ("name", "w", "r", "rng")

    def __init__(self, name, rng=None):
        self.name = name
        self.w = None
        self.r = {}
        self.rng = rng


class Prog:
    ENG = ("pe", "act", "dve", "pool", "sp")

    def __init__(self, nc, es):
        self.nc = nc
        self.es = es
        self.q = {e: [] for e in self.ENG}
        self.sem = {}
        self.cnt = {}
        self.waited = {e: {} for e in self.ENG}
        self.ranged = []
        for e in ("pe", "act", "dve", "pool"):
            self.newsem("c_" + e)

    def newsem(self, name):
        self.sem[name] = self.es.enter_context(self.nc.semaphore(name))
        self.cnt[name] = 0

    def wait(self, eng, tok):
        if tok is None:
            return
        s, v = tok
        if self.waited[eng].get(s, 0) >= v:
            return
        self.waited[eng][s] = v
        h = self.sem[s]
        self.q[eng].append(lambda e, h=h, v=v: e.wait_ge(h, v))

    def _deps(self, eng, reads, writes):
        for b in reads:
            self.wait(eng, b.w)
        for b in writes:
            self.wait(eng, b.w)
            for t in b.r.values():
                self.wait(eng, t)
            if b.rng is not None:
                for y in self.ranged:
                    if y is not b and y.rng[0] < b.rng[1] and b.rng[0] < y.rng[1]:
                        self.wait(eng, y.w)
                        for t in y.r.values():
                            self.wait(eng, t)

    def _mark(self, eng, tok, reads, writes):
        for b in reads:
            b.r[eng] = tok
        for b in writes:
            b.w = tok
            b.r = {}

    def op(self, eng, fn, reads=(), writes=()):
        self._deps(eng, reads, writes)
        s = "c_" + eng
        self.cnt[s] += 1
        tok = (s, self.cnt[s])
        h = self.sem[s]
        self.q[eng].append(lambda e, fn=fn, h=h: fn(e).then_inc(h, 1))
        self._mark(eng, tok, reads, writes)
        return tok

    def dma(self, eng, out, in_, sem, reads=(), writes=()):
        if sem not in self.sem:
            self.newsem(sem)
        self._deps(eng, reads, writes)
        self.cnt[sem] += 16
        tok = (sem, self.cnt[sem])
        h = self.sem[sem]
        self.q[eng].append(lambda e, out=out, in_=in_, h=h: e.dma_start(out=out, in_=in_).then_inc(h, 16))
        self._mark("dma_" + sem, tok, reads, writes)
        return tok

    def emit(self):
        nc = self.nc
        with nc.Block() as block:
            @block.tensor
            def _(e):
                for f in self.q["pe"]:
                    f(e)

            @block.scalar
            def _(e):
                for f in self.q["act"]:
                    f(e)

            @block.vector
            def _(e):
                for f in self.q["dve"]:
                    f(e)

            @block.gpsimd
            def _(e):
                for f in self.q["pool"]:
                    f(e)

            @block.sync
            def _(e):
                for f in self.q["sp"]:
                    f(e)


def build(mode):
    nc = bass.Bass("TRN2", target_bir_lowering=False)
    do_l0 = mode in ("fused", "A")
    do_att = mode in ("fused", "B")

    def din(name, shape, dt=F32):
        return nc.dram_tensor(name, list(shape), dt, kind="ExternalInput")

    def dout(name, shape, dt=F32):
        return nc.dram_tensor(name, list(shape), dt, kind="ExternalOutput")

    ident_d = din("ident", [128, 128], BF16)
    identf_d = din("identf", [128, 128], F32)
    lnp_d = din("lnp", [8, D])
    if do_l0:
        x_d = din("x_own", [T, D])
        halo_d = din("x_halo", [NB, 16, D])
        amat_d = din("amat", [128, 12, 128], BF16)
        poolw_d = din("pool_w", [4, 256, 256])
        pscale_d = din("pool_scale", [1, D])
        wqkv_d = din("w_qkv", [D, 3 * D])
    wg_d = din("w_gate", [2, D, DFF])
    wu_d = din("w_up", [2, D, DFF])
    wdn_d = din("w_down", [2, DFF, D])
    if do_att:
        wo_d = din("w_o", [D, D])
        lam_d = din("lam_p", [1, 256])
        sg_d = din("subln_g", [1, 128])
        rel_d = din("rel_table", [32, NH])
        tbg_d = din("tb_g", [128, NH, 3, 128])
        tbm_d = din("tb_m", [128, 3, 128])
        out_d = dout("out", [T, D])

    if mode == "fused":
        kT_own = nc.dram_tensor("kT_own", [NH * 128, T], BF16)
        v_own = nc.dram_tensor("v_own", [NH * 128, NB * VW], BF16)
        kT_all = nc.dram_tensor("kT_all", [2 * NH * 128, T], BF16)
        v_all = nc.dram_tensor("v_all", [2 * NH * 128, NB * VW], BF16)
    elif mode == "A":
        kT_own = dout("kT_own", [NH * 128, T], BF16)
        v_own = dout("v_own", [NH * 128, NB * VW], BF16)
        x1_d = dout("x1", [T, D])
        qT_d = dout("qT", [128, NH, T], BF16)
    else:
        kT_all = din("kT_all", [2 * NH * 128, T], BF16)
        v_all = din("v_all", [2 * NH * 128, NB * VW], BF16)
        x1_d = din("x1", [T, D])
        qT_d = din("qT", [128, NH, T], BF16)

    with ExitStack() as es:
        def sb(name, shape, dt):
            return es.enter_context(nc.sbuf_tensor(name, list(shape), dt))

        def ps(name, shape, dt):
            return es.enter_context(nc.psum_tensor(name, list(shape), dt))

        resid = sb("resid", [128, NB, D], F32)
        xT = sb("xT", [128, 8, T], BF16)
        lnp_t = sb("lnp_t", [128, 2, D], F32)
        zb = sb("zb", [128, D], F32)
        identf = sb("identf_s", [128, 128], F32)
        sg = sb("sg", [128, 2, 512], BF16)
        f32m = sb("f32m", [128, 1536], F32)
        tmask = sb("tmask", [128, 384], F32)
        osb = sb("osb", [128, 2, 128], F32)
        sgb = sb("sgb", [128, 128], F32)
        ident = sb("ident_s", [128, 128], BF16)
        st = sb("st", [128, 2, 2, 6], F32)
        mv = sb("mv", [128, 2, 2], F32)
        rs = sb("rs", [128, 2, 2], F32)
        sm = sb("sm", [128, 64], F32)
        cb = sb("cb", [128, NH], F32)
        lamt = sb("lamt", [128, 256], F32)
        AR = 43008
        arena = sb("arena", [128, AR], BF16)

        def av(lo, n, pat=None, **kw):
            a = arena[:, lo:lo + n]
            return a.rearrange(pat, **kw) if pat else a

        hT = av(0, 11264, "p (a b) -> p a b", a=11)
        wd = av(11264, 11264, "p (a b) -> p a b", a=11)
        gu = [av(22528 + k * 4096, 4096, "p (g c f) -> p g c f", g=2, c=8) for k in range(3)]
        halo = av(0, 4096, "p (a b) -> p a b", a=4)
        xin_bf = av(4096, 3072, "p (a b) -> p a b", a=3)
        pmT = av(7168, 3072, "p (a c t) -> p a c t", a=3, c=8)
        amat = av(34816, 1536, "p (a b) -> p a b", a=12)
        wp = av(36352, 2048, "p (g k d) -> p g k d", g=4, k=2)
        ps_bc = f32m[:, 0:1024]
        wq = [av(22528 + k * 4096, 4096, "p (c f) -> p c f", c=8) for k in range(3)]
        kst = [av(34816 + k * 2048, 2048) for k in range(2)]
        vst = [av(38912 + k * 1152, 1152, "p (h v) -> p h v", h=NH) for k in range(2)]
        qT = av(0, 16384, "p (h t) -> p h t", h=NH)
        kTt = [av(16384 + k * 8704, 4096, "p (s t) -> p s t", s=2) for k in range(2)]
        vt = [av(16384 + k * 8704 + 4096, 4608, "p (s j v) -> p s j v", s=2, j=NB) for k in range(2)]
        pT = [av(33792 + k * 512, 512) for k in range(4)]
        pTs = [av(35840 + k * 384, 384) for k in range(4)]
        onb = [av(37376 + k * 128, 128) for k in range(2)]
        wo = av(22528, 8192, "p (a f) -> p a f", a=8)
        Tb = [f32m[:, k * 384:(k + 1) * 384] for k in range(2)]
        tmpA = [f32m[:, 768 + k * 384:768 + (k + 1) * 384] for k in range(2)]

        pq = [ps("pq%d" % k, [128, 1024], F32) for k in range(4)]
        pb8 = [pq[k // 2][:, (k % 2) * 512:(k % 2 + 1) * 512] for k in range(8)]
        pb = pb8[:6]

        P = Prog(nc, es)
        B = {}

        RNG = {"hT0": (0, 11264), "hT1": (0, 11264), "wd": (11264, 22528),
               "gu0": (22528, 26624), "gu1": (26624, 30720), "gu2": (30720, 34816),
               "halo0": (0, 4096), "halo1": (0, 4096), "halo2": (0, 4096), "halo3": (0, 4096),
               "xin0": (4096, 5120), "xin1": (5120, 6144), "xin2": (6144, 7168),
               "pmT0": (7168, 8192), "pmT1": (8192, 9216), "pmT2": (9216, 10240),
               "amat": (34816, 36352), "wp": (36352, 38400),
               "kst0": (34816, 36864), "kst1": (36864, 38912), "vst0": (38912, 40064), "vst1": (40064, 41216),
               "qT": (0, 16384), "kv0": (16384, 25088), "kv1": (25088, 33792),
               "pT0": (33792, 34816), "pT2": (34816, 35840),
               "pTs0": (35840, 36608), "pTs2": (36608, 37376),
               "onb0": (37376, 37504), "onb1": (37504, 37632),
               "wo": (22528, 30720), "zb1": (38400, 40448)}

        def buf(name):
            if name not in B:
                B[name] = Buf(name, RNG.get(name))
                if name in RNG:
                    P.ranged.append(B[name])
            return B[name]

        bank = [buf("bank%d" % k) for k in range(8)]

        P.dma("sp", ident[:], ident_d[:, :], "ld_ident", writes=[buf("ident")])
        P.dma("sp", identf[:], identf_d[:, :], "ld_identf", writes=[buf("identf")])
        zbs = [zb[:], arena[:, 38400:40448].bitcast(F32)]

        def load_lnp(idx):
            P.dma("sp", lnp_t[:, 0, :], lnp_d[2 * idx:2 * idx + 1, :].partition_broadcast(128), "ld_lnp",
                  writes=[buf("lnp")])
            P.dma("sp", lnp_t[:, 1, :], lnp_d[2 * idx + 1:2 * idx + 2, :].partition_broadcast(128), "ld_lnp",
                  writes=[])
            B["lnp"].w = ("ld_lnp", P.cnt["ld_lnp"])

        ln_ctr = [0]

        def emit_ln(zsrc, zbuf_list, blk, want_T, us):
            k = ln_ctr[0] % 2
            ln_ctr[0] += 1
            bst, bmv, brs = buf("st%d" % k), buf("mv%d" % k), buf("rs%d" % k)
            zt = zbs[us]
            zbb = buf("zb%d" % us)

            def f_stats(e, k=k, zsrc=zsrc):
                e.bn_stats(st[:, k, 0, :], zsrc[:, 0:512])
                return e.bn_stats(st[:, k, 1, :], zsrc[:, 512:1024])
            P.op("dve", f_stats, reads=zbuf_list, writes=[bst])
            P.op("dve", lambda e, k=k: e.bn_aggr(mv[:, k, :], st[:, k, :, :]), reads=[bst], writes=[bmv])
            P.op("act", lambda e, k=k: e.activation(out=rs[:, k, 0:1], in_=mv[:, k, 1:2], func=AF.Sqrt,
                                                    bias=EPS, scale=1.0),
                 reads=[bmv], writes=[brs])
            P.op("dve", lambda e, k=k, zsrc=zsrc, zt=zt: e.scalar_tensor_tensor(
                zt, zsrc, mv[:, k, 0:1], lnp_t[:, 0, :], ALU.subtract, ALU.mult),
                reads=zbuf_list + [bmv, buf("lnp")], writes=[zbb])
            P.op("dve", lambda e, k=k: e.reciprocal(rs[:, k, 1:2], rs[:, k, 0:1]), reads=[brs], writes=[brs])
            rb = buf("resid%d" % blk)
            P.op("dve", lambda e, k=k, blk=blk, zt=zt: e.scalar_tensor_tensor(
                resid[:, blk, :], zt, rs[:, k, 1:2], lnp_t[:, 1, :], ALU.mult, ALU.add),
                reads=[zbb, brs, buf("lnp")], writes=[rb])
            if not want_T:
                return None

            def do_T(blk=blk, rb=rb):
                def f_tp(e):
                    for c in range(8):
                        ins = e.transpose(pb8[6 + c // 4][:, (c % 4) * 128:(c % 4 + 1) * 128],
                                          resid[:, blk, c * 128:(c + 1) * 128], identf[:])
                    return ins
                P.op("pe", f_tp, reads=[rb, buf("identf")], writes=[bank[6], bank[7]])
                xb_ = buf("xT%d" % blk)
                P.op("act", lambda e: e.copy(xT[:, 0:4, blk * 128:(blk + 1) * 128],
                                             pb8[6][:, :].rearrange("p (c t) -> p c t", c=4)),
                     reads=[bank[6]], writes=[xb_])
                P.op("act", lambda e: e.copy(xT[:, 4:8, blk * 128:(blk + 1) * 128],
                                             pb8[7][:, :].rearrange("p (c t) -> p c t", c=4)),
                     reads=[bank[7]], writes=[xb_])
            return do_T

        def emit_ffn(L, lnp_idx, last):
            load_lnp(lnp_idx)
            wgv = wg_d[L].rearrange("(c p) f -> p c f", p=128)
            wuv = wu_d[L].rearrange("(c p) f -> p c f", p=128)
            wdv = wdn_d[L].rearrange("(a p) d -> p a d", p=128)
            gctr = 0
            defer_T = []
            for tt in range(2):
                for part in range(2):
                    groups = [(0, 2), (2, 2), (4, 2), (6, 2), (8, 2), (10, 1)]
                    gtok = {}

                    def load_gu(gi, part=part):
                        j0_, nj_ = groups[gi]
                        gb_ = (gctr + gi) % 3
                        f0 = (part * 11 + j0_) * 128
                        w = nj_ * 128
                        gbuf_ = buf("gu%d" % gb_)
                        P.dma("pool", gu[gb_][:, 0, :, 0:w], wgv[:, :, f0:f0 + w], "ld_gu%d" % gb_, writes=[gbuf_])
                        P.dma("pool", gu[gb_][:, 1, :, 0:w], wuv[:, :, f0:f0 + w], "ld_gu%d" % gb_, writes=[])
                        gbuf_.w = ("ld_gu%d" % gb_, P.cnt["ld_gu%d" % gb_])
                    for gi in range(3):
                        load_gu(gi)
                    P.dma("pool", wd[:, :, :], wdv[:, part * 11:(part + 1) * 11, :], "ld_wd", writes=[buf("wd")])
                    for gi, (j0, nj) in enumerate(groups):
                        gb = (gctr + gi) % 3
                        gbuf = buf("gu%d" % gb)
                        if gi >= 3:
                            load_gu(gi)
                        for j in range(nj):
                            jl = j0 + j
                            for ts in range(2):
                                k = (jl * 2 + ts) % 2
                                t0 = tt * 1024 + ts * 512
                                xbufs = [buf("xT%d" % (t0 // 128 + q)) for q in range(4)]

                                def f_gu(e, gb=gb, j=j, t0=t0, k=k):
                                    for c in range(8):
                                        e.matmul(pb[k][:, :], gu[gb][:, 0, c, j * 128:(j + 1) * 128],
                                                 xT[:, c, t0:t0 + 512], start=(c == 0), stop=(c == 7))
                                    for c in range(8):
                                        ins = e.matmul(pb[2 + k][:, :], gu[gb][:, 1, c, j * 128:(j + 1) * 128],
                                                       xT[:, c, t0:t0 + 512], start=(c == 0), stop=(c == 7))
                                    return ins
                                P.op("pe", f_gu, reads=[gbuf] + xbufs, writes=[bank[k], bank[2 + k]])
                                P.op("act", lambda e, k=k: e.activation(out=sg[:, k, :], in_=pb[k][:, :], func=AF.Silu),
                                     reads=[bank[k]], writes=[buf("sg%d" % k)])
                                P.op("dve", lambda e, k=k, jl=jl, ts=ts: e.tensor_tensor(
                                    hT[:, jl, ts * 512:(ts + 1) * 512], sg[:, k, :], pb[2 + k][:, :], ALU.mult),
                                    reads=[buf("sg%d" % k), bank[2 + k]], writes=[buf("hT%d" % ts)])
                        for _ in range(2):
                            if defer_T:
                                defer_T.pop(0)()
                    gctr += len(groups)
                    for b8 in range(8):
                        blk = tt * 8 + b8
                        ts = b8 // 4
                        rb = buf("resid%d" % blk)
                        for dh in range(2):
                            k = (b8 * 2 + dh) % 2

                            def f_dn(e, b8=b8, dh=dh, k=k):
                                for j in range(11):
                                    ins = e.matmul(pb[4 + k][:, :], hT[:, j, b8 * 128:(b8 + 1) * 128],
                                                   wd[:, j, dh * 512:(dh + 1) * 512], start=(j == 0), stop=(j == 10))
                                return ins
                            P.op("pe", f_dn, reads=[buf("hT%d" % ts), buf("wd")], writes=[bank[4 + k]])
                            if part == 0:
                                P.op("dve", lambda e, blk=blk, dh=dh, k=k: e.scalar_tensor_tensor(
                                    resid[:, blk, dh * 512:(dh + 1) * 512], resid[:, blk, dh * 512:(dh + 1) * 512],
                                    ALPHA, pb[4 + k][:, :], ALU.mult, ALU.add),
                                    reads=[bank[4 + k]], writes=[rb])
                            else:
                                P.op("dve", lambda e, blk=blk, dh=dh, k=k: e.tensor_tensor(
                                    resid[:, blk, dh * 512:(dh + 1) * 512], resid[:, blk, dh * 512:(dh + 1) * 512],
                                    pb[4 + k][:, :], ALU.add),
                                    reads=[bank[4 + k]], writes=[rb])
                        if part == 1:
                            dT = emit_ln(resid[:, blk, :], [rb], blk, want_T=not last, us=blk % 2)
                            if dT is not None:
                                defer_T.append(dT)
                            if last:
                                P.dma("sp", out_d[blk * 128:(blk + 1) * 128, :], resid[:, blk, :], "st_out", reads=[rb])
            while defer_T:
                defer_T.pop(0)()

        if do_l0:
            P.dma("sp", amat[:, :, :], amat_d[:, :, :], "ld_amat", writes=[buf("amat")])
            P.dma("sp", ps_bc, pscale_d[0:1, :].partition_broadcast(128), "ld_psbc", writes=[buf("f32m")])
            load_lnp(0)
            for q4 in range(4):
                P.dma("sp", resid[:, q4 * 4:(q4 + 1) * 4, :],
                      x_d[q4 * 512:(q4 + 1) * 512, :].rearrange("(b p) d -> p b d", p=128), "ld_x%d" % q4,
                      writes=[buf("resid%d" % (q4 * 4 + q)) for q in range(4)])
            P.dma("pool", wp[:, :, :, :], poolw_d.rearrange("g (k p) d -> p g k d", p=128), "ld_wp", writes=[buf("wp")])

            def f_wps(e):
                for g in range(4):
                    for kc in range(2):
                        ins = e.tensor_tensor(wp[:, g, kc, :], wp[:, g, kc, :], ps_bc[:, g * 256:(g + 1) * 256], ALU.mult)
                return ins
            P.op("dve", f_wps, reads=[buf("f32m")], writes=[buf("wp")])

            mixT = []

            def mix_front_a(i):
                rb = buf("resid%d" % i)
                hs = i % 4
                hb = buf("halo%d" % hs)
                P.dma("pool", halo[0:16, hs, :], halo_d[i, :, :], "ld_halo%d" % hs, writes=[hb])
                k2 = i % 3
                xb = buf("xin%d" % k2)
                P.op("act", lambda e: e.copy(xin_bf[:, k2, :], resid[:, i, :]), reads=[rb], writes=[xb])
                a0 = 0 if i == 0 else 4

                def f_pm(e):
                    for c in range(8):
                        g = c // 2
                        o = pb[c // 4][:, (c % 4) * 128:(c % 4 + 1) * 128]
                        e.matmul(o, xin_bf[:, k2, c * 128:(c + 1) * 128], amat[:, a0 + g, :], start=True, stop=False)
                        ins = e.matmul(o[:, 0:16], halo[0:16, hs, c * 128:(c + 1) * 128], amat[0:16, 8 + g, 0:16],
                                       start=False, stop=True)
                    return ins
                P.op("pe", f_pm, reads=[xb, hb, buf("amat")], writes=[bank[0], bank[1]])
                pmb = buf("pmT%d" % k2)

                def f_pmT(e):
                    e.copy(pmT[:, k2, 0:4, :], pb[0][:, :].rearrange("p (c t) -> p c t", c=4))
                    return e.copy(pmT[:, k2, 4:8, :], pb[1][:, :].rearrange("p (c t) -> p c t", c=4))
                P.op("act", f_pmT, reads=[bank[0], bank[1]], writes=[pmb])

            def mix_front_b(i):
                k2 = i % 3
                pmb = buf("pmT%d" % k2)
                mb = 2 + 2 * (i % 2)

                def f_mix(e):
                    for g in range(4):
                        for kc in range(2):
                            ins = e.matmul(pb[mb + g // 2][:, (g % 2) * 256:(g % 2 + 1) * 256],
                                           pmT[:, k2, 2 * g + kc, :], wp[:, g, kc, :], start=(kc == 0), stop=(kc == 1))
                    return ins
                P.op("pe", f_mix, reads=[pmb, buf("wp")], writes=[bank[mb], bank[mb + 1]])

            def mix_z(i):
                rb = buf("resid%d" % i)
                k2 = i % 2
                zt = zbs[k2]
                zbb = buf("zb%d" % k2)

                mb = 2 + 2 * (i % 2)

                def f_z1(e):
                    e.scalar_tensor_tensor(zt[:, 0:512], resid[:, i, 0:512], ALPHA, pb[mb][:, :], ALU.mult, ALU.add)
                    return e.scalar_tensor_tensor(zt[:, 512:1024], resid[:, i, 512:1024], ALPHA, pb[mb + 1][:, :],
                                                  ALU.mult, ALU.add)
                P.op("dve", f_z1, reads=[bank[mb], bank[mb + 1], rb], writes=[zbb])

            for j in range(2):
                mix_front_a(j)
                mix_front_b(j)
            for i in range(NB):
                mix_z(i)
                if i + 2 < NB:
                    mix_front_a(i + 2)
                    mix_front_b(i + 2)
                k2 = i % 2
                mixT.append(emit_ln(zbs[k2], [buf("zb%d" % k2)], i, want_T=True, us=k2))
                while len(mixT) > 1:
                    mixT.pop(0)()
            while mixT:
                mixT.pop(0)()

            emit_ffn(0, 1, last=False)

            wqv = wqkv_d.rearrange("(c p) f -> p c f", p=128)
            allx = [buf("xT%d" % q) for q in range(NB)]
            for k in range(2):
                P.op("pool", lambda e, k=k: e.memset(vst[k][:, :, 128:VW], 0.0), writes=[buf("vst%d" % k)])
                P.op("pool", lambda e, k=k: e.memset(vst[k][:, :, 128:129], 1.0), writes=[buf("vst%d" % k)])
            bctr = 0
            for gidx, grp in enumerate((2, 3, 4, 5, 0, 1)):
                wbi = gidx % 3
                wb = buf("gu%d" % wbi)
                P.dma("pool", wq[wbi][:, :, :], wqv[:, :, grp * 512:(grp + 1) * 512], "ld_gu%d" % wbi, writes=[wb])
                if grp < 4:
                    for hh in range(4):
                        h = (grp % 2) * 4 + hh
                        if grp >= 2:
                            ks = h % 2
                            kb_ = buf("kst%d" % ks)
                        for ts in range(4):
                            k = bctr % 2
                            bctr += 1

                            def f_qk(e, wbi=wbi, hh=hh, ts=ts, k=k):
                                for c in range(8):
                                    ins = e.matmul(pb[k][:, :], wq[wbi][:, c, hh * 128:(hh + 1) * 128],
                                                   xT[:, c, ts * 512:(ts + 1) * 512], start=(c == 0), stop=(c == 7))
                                return ins
                            P.op("pe", f_qk, reads=[wb] + allx[ts * 4:ts * 4 + 4], writes=[bank[k]])
                            if grp < 2:
                                P.op("act", lambda e, h=h, ts=ts, k=k: e.copy(
                                    qT[:, h, ts * 512:(ts + 1) * 512], pb[k][:, :]),
                                    reads=[bank[k]], writes=[buf("qT")])
                            else:
                                P.op("act", lambda e, ks=ks, ts=ts, k=k: e.copy(
                                    kst[ks][:, ts * 512:(ts + 1) * 512], pb[k][:, :]),
                                    reads=[bank[k]], writes=[kb_])
                        if grp >= 2:
                            P.dma("sp", kT_own[h * 128:(h + 1) * 128, :], kst[ks][:, :], "st_kst%d" % ks, reads=[kb_])
                else:
                    half = grp - 4
                    for blk in range(NB):
                        k = bctr % 2
                        bctr += 1
                        vs = blk % 2
                        vb_ = buf("vst%d" % vs)

                        def f_v(e, wbi=wbi, blk=blk, k=k):
                            for c in range(8):
                                ins = e.matmul(pb[k][:, :], xT[:, c, blk * 128:(blk + 1) * 128], wq[wbi][:, c, :],
                                               start=(c == 0), stop=(c == 7))
                            return ins
                        P.op("pe", f_v, reads=[wb, allx[blk]], writes=[bank[k]])
                        P.op("dve", lambda e, vs=vs, half=half, k=k: e.tensor_copy(
                            vst[vs][:, half * 4:(half + 1) * 4, 0:128],
                            pb[k][:, :].rearrange("p (h v) -> p h v", h=4)),
                            reads=[bank[k]], writes=[vb_])
                        dst = v_own.ap().rearrange("(h k) (j v) -> k h j v", k=128, v=VW)[:, half * 4:(half + 1) * 4, blk, :]
                        P.dma("sp", dst, vst[vs][:, half * 4:(half + 1) * 4, :], "st_vst%d" % vs, reads=[vb_])
            if mode == "A":
                for blk in range(NB):
                    P.dma("sp", x1_d[blk * 128:(blk + 1) * 128, :], resid[:, blk, :], "st_x1",
                          reads=[buf("resid%d" % blk)])
                P.dma("sp", qT_d[:, :, :], qT[:, :, :], "st_qT", reads=[buf("qT")])

        kv_ready = None
        if mode == "fused":
            for nm in ("st_kst0", "st_kst1", "st_vst0", "st_vst1"):
                P.wait("pool", (nm, P.cnt[nm]))
            P.newsem("cc")
            groups = [[2 * b, 2 * b + 1] for b in range(N_CORES // 2)]
            hcc = P.sem["cc"]

            if not NOCC:
                for h in range(NH):
                    def f_k(e, h=h):
                        return e.collective_compute("AllGather", ALU.bypass, replica_groups=groups,
                                                    ins=[kT_own[h * 128:(h + 1) * 128, :]],
                                                    outs=[kT_all[h * 256:(h + 1) * 256, :]])

                    def f_v(e, h=h):
                        return e.collective_compute("AllGather", ALU.bypass, replica_groups=groups,
                                                    ins=[v_own[h * 128:(h + 1) * 128, :]],
                                                    outs=[v_all[h * 256:(h + 1) * 256, :]])
                    P.q["pool"].append(lambda e, f=f_k: f(e).then_inc(hcc, 1))
                    P.q["pool"].append(lambda e, f=f_v: f(e).then_inc(hcc, 1))
                kv_ready = True

        if do_att:
            if mode == "B":
                for q4 in range(4):
                    P.dma("sp", resid[:, q4 * 4:(q4 + 1) * 4, :],
                          x1_d[q4 * 512:(q4 + 1) * 512, :].rearrange("(b p) d -> p b d", p=128), "ld_x%d" % q4,
                          writes=[buf("resid%d" % (q4 * 4 + q)) for q in range(4)])
                P.dma("sp", qT[:, :, :], qT_d[:, :, :], "ld_qT", writes=[buf("qT")])
            P.dma("sp", cb[:, :], rel_d[31:32, :].partition_broadcast(128), "ld_cb", writes=[buf("cb")])
            P.dma("sp", lamt[:, :], lam_d[0:1, :].partition_broadcast(128), "ld_lam", writes=[buf("lamt")])
            P.dma("sp", sgb[:, :], sg_d[0:1, :].partition_broadcast(128), "ld_sgb", writes=[buf("sgb")])
            P.op("dve", lambda e: e.tensor_scalar(sgb[:, :], sgb[:, :], (1.0 - LAMBDA_INIT) * math.sqrt(128.0), None,
                                                  ALU.mult), reads=[buf("sgb")], writes=[buf("sgb")])
            lv = lamt[:, :].rearrange("p (a b d) -> p a b d", a=2, b=2)
            P.op("dve", lambda e: e.tensor_tensor(osb[:, 0, :].rearrange("p (a d) -> p a d", a=2),
                                                  lv[:, :, 0, :], lv[:, :, 1, :], ALU.mult),
                 reads=[buf("lamt")], writes=[buf("osb0")])
            P.op("dve", lambda e: e.tensor_reduce(sm[:, 0:2], osb[:, 0, :].rearrange("p (a d) -> p a d", a=2),
                                                  AX.X, ALU.add), reads=[buf("osb0")], writes=[buf("sm_lam")])
            P.op("act", lambda e: e.activation(out=sm[:, 2:4], in_=sm[:, 0:2], func=AF.Exp),
                 reads=[buf("sm_lam")], writes=[buf("sm_lam")])
            P.op("dve", lambda e: e.tensor_tensor(sm[:, 4:5], sm[:, 3:4], sm[:, 2:3], ALU.subtract),
                 reads=[buf("sm_lam")], writes=[buf("sm_lam")])
            P.op("dve", lambda e: e.tensor_scalar(sm[:, 4:5], sm[:, 4:5], -LAMBDA_INIT, None, ALU.add),
                 reads=[buf("sm_lam")], writes=[buf("sm_lam")])
            lamb = buf("sm_lam")
            P.dma("sp", tmask[:, :].rearrange("p (a q) -> p a q", a=3), tbm_d[:, :, :], "ld_tm", writes=[buf("tmask")])

            kall = kT_all.ap().rearrange("(h s k) t -> k s h t", s=2, h=NH)
            vall = v_all.ap().rearrange("(h s k) (j v) -> k s h j v", s=2, h=NH, v=VW)
            LOOK = int(os.environ.get('K_LOOK', '1'))
            sp_ctr = [0]
            cn_ctr = [0]
            ep_ctr = [0]
            pend = []
            tpend = []

            b1pend = []
            b2pend = []

            def drain_epilogue(keep1, keep2):
                while len(b2pend) > keep2:
                    b2pend.pop(0)()
                while len(b1pend) > keep1:
                    b1pend.pop(0)()

            def emit_epilogue(h, i, ob, hb):
                obank = bank[4 + ob]
                ek = ep_ctr[0] % 2
                ep_ctr[0] += 1
                sb_ = buf("sm_e%d" % ek)
                o0 = 8 + ek * 8
                ov = pb[4 + ob][:, :].rearrange("p (c w) -> p c w", c=2)
                while len(b2pend) > 0:
                    b2pend.pop(0)()
                while len(b1pend) > 0:
                    b1pend.pop(0)()
                P.op("dve", lambda e: e.reciprocal(sm[:, o0:o0 + 2], ov[:, :, 128]), reads=[obank], writes=[sb_])
                P.op("dve", lambda e: e.tensor_tensor(sm[:, o0 + 2:o0 + 3], sm[:, o0 + 1:o0 + 2], sm[:, 4:5], ALU.mult),
                     reads=[sb_, lamb], writes=[sb_])
                osbb = buf("osb%d" % ek)
                P.op("dve", lambda e: e.tensor_scalar(osb[:, ek, :], pb[4 + ob][:, 0:128], sm[:, o0:o0 + 1], None, ALU.mult),
                     reads=[obank, sb_], writes=[osbb])
                P.op("dve", lambda e: e.scalar_tensor_tensor(
                    osb[:, ek, :], pb[4 + ob][:, 256:384], sm[:, o0 + 2:o0 + 3], osb[:, ek, :], ALU.mult, ALU.add),
                    reads=[obank, sb_], writes=[osbb])
                P.op("dve", lambda e: e.scalar_tensor_tensor(
                    sg[:, 0, 0:128], osb[:, ek, :], 1.0, osb[:, ek, :], ALU.mult, ALU.mult, accum_out=sm[:, o0 + 3:o0 + 4]),
                    reads=[osbb], writes=[sb_, buf("junk")])

                def stage_b1():
                    P.op("act", lambda e: e.activation(out=sm[:, o0 + 4:o0 + 5], in_=sm[:, o0 + 3:o0 + 4], func=AF.Ln,
                                                       bias=128.0 * EPS, scale=1.0), reads=[sb_], writes=[sb_])
                    P.op("act", lambda e: e.activation(out=sm[:, o0 + 5:o0 + 6], in_=sm[:, o0 + 4:o0 + 5], func=AF.Exp,
                                                       scale=-0.5), reads=[sb_], writes=[sb_])
                    P.op("dve", lambda e: e.scalar_tensor_tensor(
                        onb[ek][:, :], osb[:, ek, :], sm[:, o0 + 5:o0 + 6], sgb[:, :], ALU.mult, ALU.mult),
                        reads=[osbb, sb_, buf("sgb")], writes=[buf("onb%d" % ek)])

                    def stage_b2():
                        tpv = pb[4 + ob][:, 448:512].bitcast(BF16)
                        P.op("pe", lambda e: e.transpose(tpv, onb[ek][:, :], ident[:]),
                             reads=[buf("onb%d" % ek), buf("ident")], writes=[obank])
                        P.op("dve", lambda e: e.tensor_copy(xT[:, h, i * 128:(i + 1) * 128], tpv),
                             reads=[obank], writes=[buf("xT%d" % i)])
                    b2pend.append(stage_b2)
                b1pend.append(stage_b1)

            def flush(upto):
                while len(pend) > upto:
                    pend.pop(0)()

            for h in range(NH):
                hb = h % 2
                kvb = buf("kv%d" % hb)
                tbb = buf("tb%d" % hb)
                if kv_ready is not None:
                    P.wait("sp", ("cc", 2 * h + 2))
                P.dma("sp", kTt[hb][:, :, :], kall[:, :, h, :], "ld_kv%d" % hb, writes=[kvb])
                P.dma("sp", vt[hb][:, :, :, :], vall[:, :, h, :, :], "ld_kv%d" % hb, writes=[])
                kvb.w = ("ld_kv%d" % hb, P.cnt["ld_kv%d" % hb])
                P.dma("sp", Tb[hb].rearrange("p (a q) -> p a q", a=3), tbg_d[:, h, :, :], "ld_tb%d" % hb, writes=[tbb])
                P.op("dve", lambda e, hb=hb: e.tensor_tensor(Tb[hb], Tb[hb], tmask[:, :], ALU.add),
                     reads=[buf("tmask")], writes=[tbb])

                for i in range(NB):
                    ob = (h * NB + i) % 2
                    obank = bank[4 + ob]
                    special = [(0, i, 0), (1, i, 1)] + ([(1, i - 1, 2)] if i >= 1 else [])
                    consts = [(0, j) for j in range(i)] + [(1, j) for j in range(i - 1)]
                    cu = [consts[u:u + 4] for u in range(0, len(consts), 4)]
                    def special_front(blks=special, h=h, i=i, hb=hb, kvb=kvb, tbb=tbb):
                        bks = [bank[6], bank[7]]
                        pbs = [pb8[6], pb8[7]]

                        def f_qk_s(e):
                            for (s_, j, kd) in blks:
                                for c in range(2):
                                    ins = e.matmul(pbs[c][:, kd * 128:(kd + 1) * 128],
                                                   kTt[hb][c * 64:(c + 1) * 64, s_, j * 128:(j + 1) * 128],
                                                   qT[c * 64:(c + 1) * 64, h, i * 128:(i + 1) * 128],
                                                   start=True, stop=True)
                            return ins
                        P.op("pe", f_qk_s, reads=[kvb, buf("qT")], writes=bks)
                        nsp = len(blks)

                        def f_sadd(e):
                            for c in range(2):
                                ins = e.scalar_tensor_tensor(
                                    tmpA[c][:, 0:nsp * 128], pbs[c][:, 0:nsp * 128], 0.125, Tb[hb][:, 0:nsp * 128],
                                    ALU.mult, ALU.add)
                            return ins
                        P.op("dve", f_sadd, reads=bks + [tbb], writes=[buf("tmpA")])
                    special_front()

                    plist = [("c", u_) for u_ in cu] + [("s", special)]
                    for pn, (kind, blks) in enumerate(plist):
                        first_pair = (pn == 0)
                        last_pair = (pn == len(plist) - 1)
                        if kind == "s":
                            sps = sp_ctr[0] % 2
                            sp_ctr[0] += 1

                            nsp = len(blks)
                            tsb = buf("tmpA")
                            psb = buf("pTs%d" % (2 * sps))
                            ps2 = av(35840 + sps * 768, 768, "p (c n) -> p c n", c=2)
                            P.op("act", lambda e, nsp=nsp, ps2=ps2: e.activation(
                                out=ps2[:, :, 0:nsp * 128],
                                in_=f32m[:, 768:1536].rearrange("p (c n) -> p c n", c=2)[:, :, 0:nsp * 128], func=AF.Exp),
                                reads=[tsb], writes=[psb])
                            srcs = [(pTs[2 * sps], psb), (pTs[2 * sps + 1], psb)]
                            cols = [kd for (_, _, kd) in blks]
                            kbl = [(s_, j) for (s_, j, _) in blks]
                        else:
                            cps = cn_ctr[0] % 2
                            cn_ctr[0] += 1
                            bks = [bank[2 * cps], bank[2 * cps + 1]]
                            pbs = [pb[2 * cps], pb[2 * cps + 1]]

                            def f_qk_c(e, blks=blks, h=h, i=i, hb=hb, pbs=pbs):
                                for n, (s_, j) in enumerate(blks):
                                    for c in range(2):
                                        ins = e.matmul(pbs[c][:, n * 128:(n + 1) * 128],
                                                       kTt[hb][c * 64:(c + 1) * 64, s_, j * 128:(j + 1) * 128],
                                                       qT[c * 64:(c + 1) * 64, h, i * 128:(i + 1) * 128],
                                                       start=True, stop=True)
                                return ins
                            P.op("pe", f_qk_c, reads=[kvb, buf("qT")], writes=bks)
                            nb_ = len(blks)
                            ptb = buf("pT%d" % (2 * cps))
                            pt2 = av(33792 + cps * 1024, 1024, "p (c n) -> p c n", c=2)
                            P.op("act", lambda e, cps=cps, nb_=nb_, h=h, pt2=pt2: e.activation(
                                out=pt2[:, :, 0:nb_ * 128],
                                in_=pq[cps][:, :].rearrange("p (c n) -> p c n", c=2)[:, :, 0:nb_ * 128], func=AF.Exp,
                                bias=cb[:, h:h + 1], scale=0.125), reads=bks + [buf("cb")], writes=[ptb])
                            srcs = [(pT[2 * cps], ptb), (pT[2 * cps + 1], ptb)]
                            cols = list(range(nb_))
                            kbl = list(blks)

                        def mk_pv(kbl=kbl, cols=cols, srcs=srcs, hb=hb, ob=ob, obank=obank, kvb=kvb,
                                  first_pair=first_pair, last_pair=last_pair, h=h, i=i):
                            def f_pv(e):
                                for c in range(2):
                                    for n, (s_, j) in enumerate(kbl):
                                        ins = e.matmul(pb[4 + ob][:, c * 256:c * 256 + 129],
                                                       srcs[c][0][:, cols[n] * 128:(cols[n] + 1) * 128],
                                                       vt[hb][:, s_, j, 0:129],
                                                       start=(first_pair and c == 0 and n == 0),
                                                       stop=(last_pair and n == len(kbl) - 1),
                                                       skip_group_check=True)
                                return ins

                            def go():
                                P.op("pe", f_pv, reads=[srcs[0][1], srcs[1][1], kvb], writes=[obank])
                                if last_pair:
                                    emit_epilogue(h, i, ob, hb)
                            return go
                        pend.append(mk_pv())
                        flush(LOOK)
            flush(0)
            drain_epilogue(0, 0)
            drain_epilogue(0, 0)

            load_lnp(2)
            wob = buf("wo")
            if os.environ.get('K_BAR'):
                P.wait("pool", ("c_pe", P.cnt["c_pe"]))
            P.dma("pool", wo[:, :, :], wo_d.rearrange("(a p) f -> p a f", p=128), "ld_wo", writes=[wob])
            woT = []

            def wo_front(i):
                bp = 2 * (i % 2)

                def f_wo(e):
                    for half in range(2):
                        for a_ in range(8):
                            ins = e.matmul(pb[bp + half][:, :], xT[:, a_, i * 128:(i + 1) * 128],
                                           wo[:, a_, half * 512:(half + 1) * 512], start=(a_ == 0), stop=(a_ == 7))
                    return ins
                P.op("pe", f_wo, reads=[buf("xT%d" % i), wob], writes=[bank[bp], bank[bp + 1]])

            wo_front(0)
            for i in range(NB):
                rb = buf("resid%d" % i)
                if i + 1 < NB:
                    wo_front(i + 1)
                bp = 2 * (i % 2)
                zt = zbs[i % 2]
                zbb = buf("zb%d" % (i % 2))

                def f_zo(e, i=i, zt=zt, bp=bp):
                    e.scalar_tensor_tensor(zt[:, 0:512], resid[:, i, 0:512], ALPHA, pb[bp][:, :], ALU.mult, ALU.add)
                    return e.scalar_tensor_tensor(zt[:, 512:1024], resid[:, i, 512:1024], ALPHA, pb[bp + 1][:, :],
                                                  ALU.mult, ALU.add)
                P.op("dve", f_zo, reads=[bank[bp], bank[bp + 1], rb], writes=[zbb])
                woT.append(emit_ln(zt, [zbb], i, want_T=True, us=i % 2))
                while len(woT) > 1:
                    woT.pop(0)()
            while woT:
                woT.pop(0)()
            emit_ffn(1, 3, last=True)
            P.wait("sp", ("st_out", P.cnt["st_out"]))
        else:
            for nm in ("st_kst0", "st_kst1", "st_vst0", "st_vst1", "st_x1", "st_qT"):
                P.wait("sp", (nm, P.cnt[nm]))

        P.emit()
    return nc


def _rel_bucket_np(n):
    n = np.maximum(n, 0)
    nf = np.maximum(n, 1).astype(np.float32)
    large = 16 + (np.log(nf / np.float32(16)) / np.float32(math.log(8.0)) * np.float32(16)).astype(np.int32)
    large = np.minimum(large, 31)
    return np.where(n < 16, n, large)


def _amat(rank):
    A = np.zeros((128, 12, 128), np.float32)
    s = np.arange(128)[:, None]
    t = np.arange(128)[None, :]
    for g, w in enumerate(POOL_WINDOWS):
        band = ((t - s) >= 0) & ((t - s) < w)
        eye = (s == t).astype(np.float32)
        diag = band.astype(np.float32) / w - eye
        cnt = np.minimum(t + 1, w).astype(np.float32)
        first = band.astype(np.float32) / cnt - eye
        A[:, 4 + g, :] = diag
        A[:, g, :] = first if rank == 0 else diag
        sh = np.arange(16)[:, None] - 16
        bandh = ((t - sh) >= 0) & ((t - sh) < w)
        A[0:16, 8 + g, :] = bandh.astype(np.float32) / w
    return A.astype(ml_dtypes.bfloat16)


def _bias_idx(rank):
    k = np.arange(128)[:, None]
    q = np.arange(128)[None, :]
    idx = np.zeros((128, 3, 128), np.int64)
    msk = np.zeros((128, 3, 128), np.float32)
    for tdx, delta in enumerate((rank, rank - 1, rank + 1)):
        rel = delta * 128 + q - k
        idx[:, tdx, :] = _rel_bucket_np(rel)
        msk[:, tdx, :] = np.where(rel >= 0, 0.0, NEG)
    return idx, msk


_NC_CACHE = {}


def _get_nc(mode):
    if mode not in _NC_CACHE:
        _NC_CACHE[mode] = build(mode)
    return _NC_CACHE[mode]


FUSED = True
NOCC = False


def kernel(x, pool_w, pool_scale, w_qkv, w_o, lam_p, subln_g, rel_table, w_gate, w_up, w_down,
           ln_mix_g, ln_mix_b, ln_ffn_g, ln_ffn_b):
    f32 = lambda a: np.ascontiguousarray(np.asarray(a, dtype=np.float32))
    x = f32(x)
    Bn, S, _ = x.shape
    lnp = np.stack([f32(ln_mix_g)[0], f32(ln_mix_b)[0], f32(ln_ffn_g)[0], f32(ln_ffn_b)[0],
                    f32(ln_mix_g)[1], f32(ln_mix_b)[1], f32(ln_ffn_g)[1], f32(ln_ffn_b)[1]], 0)
    ident = np.eye(128, dtype=np.float32).astype(ml_dtypes.bfloat16)
    rel_table = f32(rel_table)
    common = {
        "ident": ident, "identf": np.eye(128, dtype=np.float32), "lnp": lnp,
        "w_gate": f32(w_gate), "w_up": f32(w_up), "w_down": f32(w_down),
    }
    l0 = {"pool_w": f32(pool_w)[0], "pool_scale": f32(pool_scale), "w_qkv": f32(w_qkv)[0]}
    l1 = {"w_o": f32(w_o)[0], "lam_p": f32(lam_p).reshape(1, 256), "subln_g": f32(subln_g).reshape(1, 128),
          "rel_table": rel_table}
    per_core = []
    for core in range(N_CORES):
        b, r = core // 2, core % 2
        xb = x[b].reshape(32, 128, D)
        own = xb[r::2]
        halo = np.zeros((NB, 16, D), np.float32)
        for i in range(NB):
            g = 2 * i + r
            if g > 0:
                halo[i] = xb[g - 1][112:128]
        idx, msk = _bias_idx(r)
        tb_g = np.ascontiguousarray(rel_table[idx].transpose(0, 3, 1, 2))
        per_core.append({"x_own": np.ascontiguousarray(own.reshape(T, D)), "x_halo": halo, "amat": _amat(r),
                         "tb_g": tb_g, "tb_m": msk})

    def assemble(outs):
        y = np.zeros((Bn, 32, 128, D), np.float32)
        for core in range(N_CORES):
            b, r = core // 2, core % 2
            y[b, r::2] = outs[core].reshape(NB, 128, D)
        return y.reshape(Bn, S, D)

    if FUSED:
        nc = _get_nc("fused")
        maps = []
        for core in range(N_CORES):
            m = dict(common); m.update(l0); m.update(l1); m.update(per_core[core])
            maps.append(m)
        res = run_bass_kernel_spmd(nc, maps, core_ids=list(range(N_CORES)))
        return assemble([res.results[c]["out"] for c in range(N_CORES)])

    ncA = _get_nc("A")
    mapsA = []
    for core in range(N_CORES):
        m = dict(common); m.update(l0)
        for kk in ("x_own", "x_halo", "amat"):
            m[kk] = per_core[core][kk]
        mapsA.append(m)
    resA = run_bass_kernel_spmd(ncA, mapsA, core_ids=list(range(N_CORES))).results
    ncB = _get_nc("B")
    mapsB = []
    for core in range(N_CORES):
        b = core // 2
        m = dict(common); m.update(l1)
        m["tb_g"] = per_core[core]["tb_g"]; m["tb_m"] = per_core[core]["tb_m"]
        m["x1"] = resA[core]["x1"]; m["qT"] = resA[core]["qT"]
        m["kT_all"] = np.stack([resA[2 * b]["kT_own"].reshape(NH, 128, T),
                                resA[2 * b + 1]["kT_own"].reshape(NH, 128, T)], 1).reshape(2 * NH * 128, T)
        m["v_all"] = np.stack([resA[2 * b]["v_own"].reshape(NH, 128, NB * VW),
                               resA[2 * b + 1]["v_own"].reshape(NH, 128, NB * VW)], 1).reshape(2 * NH * 128, NB * VW)
        mapsB.append(m)
    resB = run_bass_kernel_spmd(ncB, mapsB, core_ids=list(range(N_CORES))).results
    return assemble([resB[c]["out"] for c in range(N_CORES)])
```

```python
import math
import os
from contextlib import ExitStack

import numpy as np
import ml_dtypes

import concourse.bass as bass
import concourse.mybir as mybir
from concourse.bass_utils import run_bass_kernel_spmd

F32 = mybir.dt.float32
BF16 = mybir.dt.bfloat16
AF = mybir.ActivationFunctionType
ALU = mybir.AluOpType
AX = mybir.AxisListType

D = 1024
DFF = 2816
NB = 16
T = NB * 128
NH = 8
VW = 144
ALPHA = 4.0 ** 0.25
EPS = 1e-5
LAMBDA_INIT = 0.8 - 0.6 * math.exp(-0.3 * 1)
NEG = -30000.0
POOL_WINDOWS = (2, 4, 8, 16)
N_CORES = 8


class Buf:
    __slots__ = ("name", "w", "r", "rng")

    def __init__(self, name, rng=None):
        self.name = name
        self.w = None
        self.r = {}
        self.rng = rng


class Prog:
    ENG = ("pe", "act", "dve", "pool", "sp")

    def __init__(self, nc, es):
        self.nc = nc
        self.es = es
        self.q = {e: [] for e in self.ENG}
        self.sem = {}
        self.cnt = {}
        self.waited = {e: {} for e in self.ENG}
        self.ranged = []
        for e in ("pe", "act", "dve", "pool"):
            self.newsem("c_" + e)

    def newsem(self, name):
        self.sem[name] = self.es.enter_context(self.nc.semaphore(name))
        self.cnt[name] = 0

    def wait(self, eng, tok):
        if tok is None:
            return
        s, v = tok
        if self.waited[eng].get(s, 0) >= v:
            return
        self.waited[eng][s] = v
        h = self.sem[s]
        self.q[eng].append(lambda e, h=h, v=v: e.wait_ge(h, v))

    def _deps(self, eng, reads, writes):
        for b in reads:
            self.wait(eng, b.w)
        for b in writes:
            self.wait(eng, b.w)
            for t in b.r.values():
                self.wait(eng, t)
            if b.rng is not None:
                for y in self.ranged:
                    if y is not b and y.rng[0] < b.rng[1] and b.rng[0] < y.rng[1]:
                        self.wait(eng, y.w)
                        for t in y.r.values():
                            self.wait(eng, t)

    def _mark(self, eng, tok, reads, writes):
        for b in reads:
            b.r[eng] = tok
        for b in writes:
            b.w = tok
            b.r = {}

    def op(self, eng, fn, reads=(), writes=()):
        self._deps(eng, reads, writes)
        s = "c_" + eng
        self.cnt[s] += 1
        tok = (s, self.cnt[s])
        h = self.sem[s]
        self.q[eng].append(lambda e, fn=fn, h=h: fn(e).then_inc(h, 1))
        self._mark(eng, tok, reads, writes)
        return tok

    def dma(self, eng, out, in_, sem, reads=(), writes=()):
        if sem not in self.sem:
            self.newsem(sem)
        self._deps(eng, reads, writes)
        self.cnt[sem] += 16
        tok = (sem, self.cnt[sem])
        h = self.sem[sem]
        self.q[eng].append(lambda e, out=out, in_=in_, h=h: e.dma_start(out=out, in_=in_).then_inc(h, 16))
        self._mark("dma_" + sem, tok, reads, writes)
        return tok

    def emit(self):
        nc = self.nc
        with nc.Block() as block:
            @block.tensor
            def _(e):
                for f in self.q["pe"]:
                    f(e)

            @block.scalar
            def _(e):
                for f in self.q["act"]:
                    f(e)

            @block.vector
            def _(e):
                for f in self.q["dve"]:
                    f(e)

            @block.gpsimd
            def _(e):
                for f in self.q["pool"]:
                    f(e)

            @block.sync
            def _(e):
                for f in self.q["sp"]:
                    f(e)


def build(mode):
    nc = bass.Bass("TRN2", target_bir_lowering=False)
    do_l0 = mode in ("fused", "A")
    do_att = mode in ("fused", "B")

    def din(name, shape, dt=F32):
        return nc.dram_tensor(name, list(shape), dt, kind="ExternalInput")

    def dout(name, shape, dt=F32):
        return nc.dram_tensor(name, list(shape), dt, kind="ExternalOutput")

    ident_d = din("ident", [128, 128], BF16)
    identf_d = din("identf", [128, 128], F32)
    lnp_d = din("lnp", [8, D])
    if do_l0:
        x_d = din("x_own", [T, D])
        halo_d = din("x_halo", [NB, 16, D])
        amat_d = din("amat", [128, 12, 128], BF16)
        poolw_d = din("pool_w", [4, 256, 256])
        pscale_d = din("pool_scale", [1, D])
        wqkv_d = din("w_qkv", [D, 3 * D])
    wg_d = din("w_gate", [2, D, DFF])
    wu_d = din("w_up", [2, D, DFF])
    wdn_d = din("w_down", [2, DFF, D])
    if do_att:
        wo_d = din("w_o", [D, D])
        lam_d = din("lam_p", [1, 256])
        sg_d = din("subln_g", [1, 128])
        rel_d = din("rel_table", [32, NH])
        tbg_d = din("tb_g", [128, NH, 3, 128])
        tbm_d = din("tb_m", [128, 3, 128])
        out_d = dout("out", [T, D])

    if mode == "fused":
        kT_own = nc.dram_tensor("kT_own", [NH * 128, T], BF16)
        v_own = nc.dram_tensor("v_own", [NH * 128, NB * VW], BF16)
        kT_all = nc.dram_tensor("kT_all", [2 * NH * 128, T], BF16)
        v_all = nc.dram_tensor("v_all", [2 * NH * 128, NB * VW], BF16)
    elif mode == "A":
        kT_own = dout("kT_own", [NH * 128, T], BF16)
        v_own = dout("v_own", [NH * 128, NB * VW], BF16)
        x1_d = dout("x1", [T, D])
        qT_d = dout("qT", [128, NH, T], BF16)
    else:
        kT_all = din("kT_all", [2 * NH * 128, T], BF16)
        v_all = din("v_all", [2 * NH * 128, NB * VW], BF16)
        x1_d = din("x1", [T, D])
        qT_d = din("qT", [128, NH, T], BF16)

    with ExitStack() as es:
        def sb(name, shape, dt):
            return es.enter_context(nc.sbuf_tensor(name, list(shape), dt))

        def ps(name, shape, dt):
            return es.enter_context(nc.psum_tensor(name, list(shape), dt))

        resid = sb("resid", [128, NB, D], F32)
        xT = sb("xT", [128, 8, T], BF16)
        lnp_t = sb("lnp_t", [128, 2, D], F32)
        zb = sb("zb", [128, D], F32)
        identf = sb("identf_s", [128, 128], F32)
        sg = sb("sg", [128, 2, 512], BF16)
        f32m = sb("f32m", [128, 1536], F32)
        tmask = sb("tmask", [128, 384], F32)
        osb = sb("osb", [128, 2, 128], F32)
        sgb = sb("sgb", [128, 128], F32)
        ident = sb("ident_s", [128, 128], BF16)
        st = sb("st", [128, 2, 2, 6], F32)
        mv = sb("mv", [128, 2, 2], F32)
        rs = sb("rs", [128, 2, 2], F32)
        sm = sb("sm", [128, 64], F32)
        cb = sb("cb", [128, NH], F32)
        lamt = sb("lamt", [128, 256], F32)
        AR = 43008
        arena = sb("arena", [128, AR], BF16)

        def av(lo, n, pat=None, **kw):
            a = arena[:, lo:lo + n]
            return a.rearrange(pat, **kw) if pat else a

        hT = av(0, 11264, "p (a b) -> p a b", a=11)
        wd = av(11264, 11264, "p (a b) -> p a b", a=11)
        gu = [av(22528 + k * 4096, 4096, "p (g c f) -> p g c f", g=2, c=8) for k in range(3)]
        halo = av(0, 4096, "p (a b) -> p a b", a=4)
        xin_bf = av(4096, 3072, "p (a b) -> p a b", a=3)
        pmT = av(7168, 3072, "p (a c t) -> p a c t", a=3, c=8)
        amat = av(34816, 1536, "p (a b) -> p a b", a=12)
        wp = av(36352, 2048, "p (g k d) -> p g k d", g=4, k=2)
        ps_bc = f32m[:, 0:1024]
        wq = [av(22528 + k * 4096, 4096, "p (c f) -> p c f", c=8) for k in range(3)]
        kst = [av(34816 + k * 2048, 2048) for k in range(2)]
        vst = [av(38912 + k * 1152, 1152, "p (h v) -> p h v", h=NH) for k in range(2)]
        qT = av(0, 16384, "p (h t) -> p h t", h=NH)
        kTt = [av(16384 + k * 8704, 4096, "p (s t) -> p s t", s=2) for k in range(2)]
        vt = [av(16384 + k * 8704 + 4096, 4608, "p (s j v) -> p s j v", s=2, j=NB) for k in range(2)]
        pT = [av(33792 + k * 512, 512) for k in range(4)]
        pTs = [av(35840 + k * 384, 384) for k in range(4)]
        onb = [av(37376 + k * 128, 128) for k in range(2)]
        wo = av(22528, 8192, "p (a f) -> p a f", a=8)
        Tb = [f32m[:, k * 384:(k + 1) * 384] for k in range(2)]
        tmpA = [f32m[:, 768 + k * 384:768 + (k + 1) * 384] for k in range(2)]

        pq = [ps("pq%d" % k, [128, 1024], F32) for k in range(4)]
        pb8 = [pq[k // 2][:, (k % 2) * 512:(k % 2 + 1) * 512] for k in range(8)]
        pb = pb8[:6]

        P = Prog(nc, es)
        B = {}

        RNG = {"hT0": (0, 11264), "hT1": (0, 11264), "wd": (11264, 22528),
               "gu0": (22528, 26624), "gu1": (26624, 30720), "gu2": (30720, 34816),
               "halo0": (0, 4096), "halo1": (0, 4096), "halo2": (0, 4096), "halo3": (0, 4096),
               "xin0": (4096, 5120), "xin1": (5120, 6144), "xin2": (6144, 7168),
               "pmT0": (7168, 8192), "pmT1": (8192, 9216), "pmT2": (9216, 10240),
               "amat": (34816, 36352), "wp": (36352, 38400),
               "kst0": (34816, 36864), "kst1": (36864, 38912), "vst0": (38912, 40064), "vst1": (40064, 41216),
               "qT": (0, 16384), "kv0": (16384, 25088), "kv1": (25088, 33792),
               "pT0": (33792, 34816), "pT2": (34816, 35840),
               "pTs0": (35840, 36608), "pTs2": (36608, 37376),
               "onb0": (37376, 37504), "onb1": (37504, 37632),
               "wo": (22528, 30720), "zb1": (38400, 40448)}

        def buf(name):
            if name not in B:
                B[name] = Buf(name, RNG.get(name))
                if name in RNG:
                    P.ranged.append(B[name])
            return B[name]

        bank = [buf("bank%d" % k) for k in range(8)]

        P.dma("sp", ident[:], ident_d[:, :], "ld_ident", writes=[buf("ident")])
        P.dma("sp", identf[:], identf_d[:, :], "ld_identf", writes=[buf("identf")])
        zbs = [zb[:], arena[:, 38400:40448].bitcast(F32)]

        def load_lnp(idx):
            P.dma("sp", lnp_t[:, 0, :], lnp_d[2 * idx:2 * idx + 1, :].partition_broadcast(128), "ld_lnp",
                  writes=[buf("lnp")])
            P.dma("sp", lnp_t[:, 1, :], lnp_d[2 * idx + 1:2 * idx + 2, :].partition_broadcast(128), "ld_lnp",
                  writes=[])
            B["lnp"].w = ("ld_lnp", P.cnt["ld_lnp"])

        ln_ctr = [0]

        def emit_ln(zsrc, zbuf_list, blk, want_T, us, evac="act"):
            k = ln_ctr[0] % 2
            ln_ctr[0] += 1
            bst, bmv, brs = buf("st%d" % k), buf("mv%d" % k), buf("rs%d" % k)
            zt = zbs[us]
            zbb = buf("zb%d" % us)

            def f_stats(e, k=k, zsrc=zsrc):
                e.bn_stats(st[:, k, 0, :], zsrc[:, 0:512])
                return e.bn_stats(st[:, k, 1, :], zsrc[:, 512:1024])
            P.op("dve", f_stats, reads=zbuf_list, writes=[bst])
            P.op("dve", lambda e, k=k: e.bn_aggr(mv[:, k, :], st[:, k, :, :]), reads=[bst], writes=[bmv])
            P.op("act", lambda e, k=k: e.activation(out=rs[:, k, 0:1], in_=mv[:, k, 1:2], func=AF.Sqrt,
                                                    bias=EPS, scale=1.0),
                 reads=[bmv], writes=[brs])
            P.op("dve", lambda e, k=k, zsrc=zsrc, zt=zt: e.scalar_tensor_tensor(
                zt, zsrc, mv[:, k, 0:1], lnp_t[:, 0, :], ALU.subtract, ALU.mult),
                reads=zbuf_list + [bmv, buf("lnp")], writes=[zbb])
            P.op("dve", lambda e, k=k: e.reciprocal(rs[:, k, 1:2], rs[:, k, 0:1]), reads=[brs], writes=[brs])
            rb = buf("resid%d" % blk)
            P.op("dve", lambda e, k=k, blk=blk, zt=zt: e.scalar_tensor_tensor(
                resid[:, blk, :], zt, rs[:, k, 1:2], lnp_t[:, 1, :], ALU.mult, ALU.add),
                reads=[zbb, brs, buf("lnp")], writes=[rb])
            if not want_T:
                return None

            def do_T(blk=blk, rb=rb):
                def f_tp(e):
                    for c in range(8):
                        ins = e.transpose(pb8[6 + c // 4][:, (c % 4) * 128:(c % 4 + 1) * 128],
                                          resid[:, blk, c * 128:(c + 1) * 128], identf[:])
                    return ins
                P.op("pe", f_tp, reads=[rb, buf("identf")], writes=[bank[6], bank[7]])
                xb_ = buf("xT%d" % blk)
                if evac == "act":
                    P.op("act", lambda e: e.copy(xT[:, 0:4, blk * 128:(blk + 1) * 128],
                                                 pb8[6][:, :].rearrange("p (c t) -> p c t", c=4)),
                         reads=[bank[6]], writes=[xb_])
                    P.op("act", lambda e: e.copy(xT[:, 4:8, blk * 128:(blk + 1) * 128],
                                                 pb8[7][:, :].rearrange("p (c t) -> p c t", c=4)),
                         reads=[bank[7]], writes=[xb_])
                else:
                    P.op("dve", lambda e: e.tensor_copy(xT[:, 0:4, blk * 128:(blk + 1) * 128],
                                                        pb8[6][:, :].rearrange("p (c t) -> p c t", c=4)),
                         reads=[bank[6]], writes=[xb_])
                    P.op("dve", lambda e: e.tensor_copy(xT[:, 4:8, blk * 128:(blk + 1) * 128],
                                                        pb8[7][:, :].rearrange("p (c t) -> p c t", c=4)),
                         reads=[bank[7]], writes=[xb_])
            return do_T

        def emit_ffn(L, lnp_idx, last):
            load_lnp(lnp_idx)
            wgv = wg_d[L].rearrange("(c p) f -> p c f", p=128)
            wuv = wu_d[L].rearrange("(c p) f -> p c f", p=128)
            wdv = wdn_d[L].rearrange("(a p) d -> p a d", p=128)
            gctr = 0
            defer_T = []
            for tt in range(2):
                for part in range(2):
                    groups = [(0, 2), (2, 2), (4, 2), (6, 2), (8, 2), (10, 1)]
                    gtok = {}

                    def load_gu(gi, part=part):
                        j0_, nj_ = groups[gi]
                        gb_ = (gctr + gi) % 3
                        f0 = (part * 11 + j0_) * 128
                        w = nj_ * 128
                        gbuf_ = buf("gu%d" % gb_)
                        P.dma("pool", gu[gb_][:, 0, :, 0:w], wgv[:, :, f0:f0 + w], "ld_gu%d" % gb_, writes=[gbuf_])
                        P.dma("pool", gu[gb_][:, 1, :, 0:w], wuv[:, :, f0:f0 + w], "ld_gu%d" % gb_, writes=[])
                        gbuf_.w = ("ld_gu%d" % gb_, P.cnt["ld_gu%d" % gb_])
                    for gi in range(3):
                        load_gu(gi)
                    P.dma("pool", wd[:, :, :], wdv[:, part * 11:(part + 1) * 11, :], "ld_wd", writes=[buf("wd")])
                    for gi, (j0, nj) in enumerate(groups):
                        gb = (gctr + gi) % 3
                        gbuf = buf("gu%d" % gb)
                        if gi >= 3:
                            load_gu(gi)
                        for j in range(nj):
                            jl = j0 + j
                            for ts in range(2):
                                k = (jl * 2 + ts) % 2
                                t0 = tt * 1024 + ts * 512
                                xbufs = [buf("xT%d" % (t0 // 128 + q)) for q in range(4)]

                                def f_gu(e, gb=gb, j=j, t0=t0, k=k):
                                    for c in range(8):
                                        e.matmul(pb[k][:, :], gu[gb][:, 0, c, j * 128:(j + 1) * 128],
                                                 xT[:, c, t0:t0 + 512], start=(c == 0), stop=(c == 7))
                                    for c in range(8):
                                        ins = e.matmul(pb[2 + k][:, :], gu[gb][:, 1, c, j * 128:(j + 1) * 128],
                                                       xT[:, c, t0:t0 + 512], start=(c == 0), stop=(c == 7))
                                    return ins
                                P.op("pe", f_gu, reads=[gbuf] + xbufs, writes=[bank[k], bank[2 + k]])
                                P.op("act", lambda e, k=k: e.activation(out=sg[:, k, :], in_=pb[k][:, :], func=AF.Silu),
                                     reads=[bank[k]], writes=[buf("sg%d" % k)])
                                P.op("dve", lambda e, k=k, jl=jl, ts=ts: e.tensor_tensor(
                                    hT[:, jl, ts * 512:(ts + 1) * 512], sg[:, k, :], pb[2 + k][:, :], ALU.mult),
                                    reads=[buf("sg%d" % k), bank[2 + k]], writes=[buf("hT%d" % ts)])
                        for _ in range(2):
                            if defer_T:
                                defer_T.pop(0)()
                    gctr += len(groups)
                    for b8 in range(8):
                        blk = tt * 8 + b8
                        ts = b8 // 4
                        rb = buf("resid%d" % blk)
                        for dh in range(2):
                            k = (b8 * 2 + dh) % 2

                            def f_dn(e, b8=b8, dh=dh, k=k):
                                for j in range(11):
                                    ins = e.matmul(pb[4 + k][:, :], hT[:, j, b8 * 128:(b8 + 1) * 128],
                                                   wd[:, j, dh * 512:(dh + 1) * 512], start=(j == 0), stop=(j == 10))
                                return ins
                            P.op("pe", f_dn, reads=[buf("hT%d" % ts), buf("wd")], writes=[bank[4 + k]])
                            if part == 0:
                                P.op("dve", lambda e, blk=blk, dh=dh, k=k: e.scalar_tensor_tensor(
                                    resid[:, blk, dh * 512:(dh + 1) * 512], resid[:, blk, dh * 512:(dh + 1) * 512],
                                    ALPHA, pb[4 + k][:, :], ALU.mult, ALU.add),
                                    reads=[bank[4 + k]], writes=[rb])
                            else:
                                P.op("dve", lambda e, blk=blk, dh=dh, k=k: e.tensor_tensor(
                                    resid[:, blk, dh * 512:(dh + 1) * 512], resid[:, blk, dh * 512:(dh + 1) * 512],
                                    pb[4 + k][:, :], ALU.add),
                                    reads=[bank[4 + k]], writes=[rb])
                        if part == 1:
                            dT = emit_ln(resid[:, blk, :], [rb], blk, want_T=not last, us=blk % 2)
                            if dT is not None:
                                defer_T.append(dT)
                            if last:
                                P.dma("sp", out_d[blk * 128:(blk + 1) * 128, :], resid[:, blk, :], "st_out", reads=[rb])
            while defer_T:
                defer_T.pop(0)()

        if do_l0:
            P.dma("sp", amat[:, :, :], amat_d[:, :, :], "ld_amat", writes=[buf("amat")])
            P.dma("sp", ps_bc, pscale_d[0:1, :].partition_broadcast(128), "ld_psbc", writes=[buf("f32m")])
            load_lnp(0)
            for q4 in range(4):
                P.dma("sp", resid[:, q4 * 4:(q4 + 1) * 4, :],
                      x_d[q4 * 512:(q4 + 1) * 512, :].rearrange("(b p) d -> p b d", p=128), "ld_x%d" % q4,
                      writes=[buf("resid%d" % (q4 * 4 + q)) for q in range(4)])
            P.dma("pool", wp[:, :, :, :], poolw_d.rearrange("g (k p) d -> p g k d", p=128), "ld_wp", writes=[buf("wp")])

            def f_wps(e):
                for g in range(4):
                    for kc in range(2):
                        ins = e.tensor_tensor(wp[:, g, kc, :], wp[:, g, kc, :], ps_bc[:, g * 256:(g + 1) * 256], ALU.mult)
                return ins
            P.op("dve", f_wps, reads=[buf("f32m")], writes=[buf("wp")])

            mixT = []

            def mix_front_a(i):
                rb = buf("resid%d" % i)
                hs = i % 4
                hb = buf("halo%d" % hs)
                P.dma("pool", halo[0:16, hs, :], halo_d[i, :, :], "ld_halo%d" % hs, writes=[hb])
                k2 = i % 3
                xb = buf("xin%d" % k2)
                P.op("act", lambda e: e.copy(xin_bf[:, k2, :], resid[:, i, :]), reads=[rb], writes=[xb])
                a0 = 0 if i == 0 else 4

                def f_pm(e):
                    for c in range(8):
                        g = c // 2
                        o = pb[c // 4][:, (c % 4) * 128:(c % 4 + 1) * 128]
                        e.matmul(o, xin_bf[:, k2, c * 128:(c + 1) * 128], amat[:, a0 + g, :], start=True, stop=False)
                        ins = e.matmul(o[:, 0:16], halo[0:16, hs, c * 128:(c + 1) * 128], amat[0:16, 8 + g, 0:16],
                                       start=False, stop=True)
                    return ins
                P.op("pe", f_pm, reads=[xb, hb, buf("amat")], writes=[bank[0], bank[1]])
                pmb = buf("pmT%d" % k2)

                def f_pmT(e):
                    e.copy(pmT[:, k2, 0:4, :], pb[0][:, :].rearrange("p (c t) -> p c t", c=4))
                    return e.copy(pmT[:, k2, 4:8, :], pb[1][:, :].rearrange("p (c t) -> p c t", c=4))
                P.op("act", f_pmT, reads=[bank[0], bank[1]], writes=[pmb])

            def mix_front_b(i):
                k2 = i % 3
                pmb = buf("pmT%d" % k2)
                mb = 2 + 2 * (i % 2)

                def f_mix(e):
                    for g in range(4):
                        for kc in range(2):
                            ins = e.matmul(pb[mb + g // 2][:, (g % 2) * 256:(g % 2 + 1) * 256],
                                           pmT[:, k2, 2 * g + kc, :], wp[:, g, kc, :], start=(kc == 0), stop=(kc == 1))
                    return ins
                P.op("pe", f_mix, reads=[pmb, buf("wp")], writes=[bank[mb], bank[mb + 1]])

            def mix_z(i):
                rb = buf("resid%d" % i)
                k2 = i % 2
                zt = zbs[k2]
                zbb = buf("zb%d" % k2)

                mb = 2 + 2 * (i % 2)

                def f_z1(e):
                    e.scalar_tensor_tensor(zt[:, 0:512], resid[:, i, 0:512], ALPHA, pb[mb][:, :], ALU.mult, ALU.add)
                    return e.scalar_tensor_tensor(zt[:, 512:1024], resid[:, i, 512:1024], ALPHA, pb[mb + 1][:, :],
                                                  ALU.mult, ALU.add)
                P.op("dve", f_z1, reads=[bank[mb], bank[mb + 1], rb], writes=[zbb])

            for j in range(2):
                mix_front_a(j)
                mix_front_b(j)
            for i in range(NB):
                mix_z(i)
                if i + 2 < NB:
                    mix_front_a(i + 2)
                    mix_front_b(i + 2)
                k2 = i % 2
                mixT.append(emit_ln(zbs[k2], [buf("zb%d" % k2)], i, want_T=True, us=k2, evac="dve"))
                while len(mixT) > 1:
                    mixT.pop(0)()
            while mixT:
                mixT.pop(0)()

            emit_ffn(0, 1, last=False)

            wqv = wqkv_d.rearrange("(c p) f -> p c f", p=128)
            allx = [buf("xT%d" % q) for q in range(NB)]
            for k in range(2):
                P.op("pool", lambda e, k=k: e.memset(vst[k][:, :, 128:VW], 0.0), writes=[buf("vst%d" % k)])
                P.op("pool", lambda e, k=k: e.memset(vst[k][:, :, 128:129], 1.0), writes=[buf("vst%d" % k)])
            bctr = 0
            for gidx, grp in enumerate((2, 3, 4, 5, 0, 1)):
                wbi = gidx % 3
                wb = buf("gu%d" % wbi)
                P.dma("pool", wq[wbi][:, :, :], wqv[:, :, grp * 512:(grp + 1) * 512], "ld_gu%d" % wbi, writes=[wb])
                if grp < 4:
                    for hh in range(4):
                        h = (grp % 2) * 4 + hh
                        if grp >= 2:
                            ks = h % 2
                            kb_ = buf("kst%d" % ks)
                        for ts in range(4):
                            k = bctr % 2
                            bctr += 1

                            def f_qk(e, wbi=wbi, hh=hh, ts=ts, k=k):
                                for c in range(8):
                                    ins = e.matmul(pb[k][:, :], wq[wbi][:, c, hh * 128:(hh + 1) * 128],
                                                   xT[:, c, ts * 512:(ts + 1) * 512], start=(c == 0), stop=(c == 7))
                                return ins
                            P.op("pe", f_qk, reads=[wb] + allx[ts * 4:ts * 4 + 4], writes=[bank[k]])
                            if grp < 2:
                                P.op("act", lambda e, h=h, ts=ts, k=k: e.copy(
                                    qT[:, h, ts * 512:(ts + 1) * 512], pb[k][:, :]),
                                    reads=[bank[k]], writes=[buf("qT")])
                            else:
                                P.op("act", lambda e, ks=ks, ts=ts, k=k: e.copy(
                                    kst[ks][:, ts * 512:(ts + 1) * 512], pb[k][:, :]),
                                    reads=[bank[k]], writes=[kb_])
                        if grp >= 2:
                            P.dma("sp", kT_own[h * 128:(h + 1) * 128, :], kst[ks][:, :], "st_kst%d" % ks, reads=[kb_])
                else:
                    half = grp - 4
                    for blk in range(NB):
                        k = bctr % 2
                        bctr += 1
                        vs = blk % 2
                        vb_ = buf("vst%d" % vs)

                        def f_v(e, wbi=wbi, blk=blk, k=k):
                            for c in range(8):
                                ins = e.matmul(pb[k][:, :], xT[:, c, blk * 128:(blk + 1) * 128], wq[wbi][:, c, :],
                                               start=(c == 0), stop=(c == 7))
                            return ins
                        P.op("pe", f_v, reads=[wb, allx[blk]], writes=[bank[k]])
                        P.op("dve", lambda e, vs=vs, half=half, k=k: e.tensor_copy(
                            vst[vs][:, half * 4:(half + 1) * 4, 0:128],
                            pb[k][:, :].rearrange("p (h v) -> p h v", h=4)),
                            reads=[bank[k]], writes=[vb_])
                        dst = v_own.ap().rearrange("(h k) (j v) -> k h j v", k=128, v=VW)[:, half * 4:(half + 1) * 4, blk, :]
                        P.dma("sp", dst, vst[vs][:, half * 4:(half + 1) * 4, :], "st_vst%d" % vs, reads=[vb_])
            if mode == "A":
                for blk in range(NB):
                    P.dma("sp", x1_d[blk * 128:(blk + 1) * 128, :], resid[:, blk, :], "st_x1",
                          reads=[buf("resid%d" % blk)])
                P.dma("sp", qT_d[:, :, :], qT[:, :, :], "st_qT", reads=[buf("qT")])

        kv_ready = None
        if mode == "fused":
            for nm in ("st_kst0", "st_kst1", "st_vst0", "st_vst1"):
                P.wait("pool", (nm, P.cnt[nm]))
            P.newsem("cc")
            groups = [[2 * b, 2 * b + 1] for b in range(N_CORES // 2)]
            hcc = P.sem["cc"]

            if not NOCC:
                for h in range(NH):
                    def f_k(e, h=h):
                        return e.collective_compute("AllGather", ALU.bypass, replica_groups=groups,
                                                    ins=[kT_own[h * 128:(h + 1) * 128, :]],
                                                    outs=[kT_all[h * 256:(h + 1) * 256, :]])

                    def f_v(e, h=h):
                        return e.collective_compute("AllGather", ALU.bypass, replica_groups=groups,
                                                    ins=[v_own[h * 128:(h + 1) * 128, :]],
                                                    outs=[v_all[h * 256:(h + 1) * 256, :]])
                    P.q["pool"].append(lambda e, f=f_k: f(e).then_inc(hcc, 1))
                    P.q["pool"].append(lambda e, f=f_v: f(e).then_inc(hcc, 1))
                kv_ready = True

        if do_att:
            if mode == "B":
                for q4 in range(4):
                    P.dma("sp", resid[:, q4 * 4:(q4 + 1) * 4, :],
                          x1_d[q4 * 512:(q4 + 1) * 512, :].rearrange("(b p) d -> p b d", p=128), "ld_x%d" % q4,
                          writes=[buf("resid%d" % (q4 * 4 + q)) for q in range(4)])
                P.dma("sp", qT[:, :, :], qT_d[:, :, :], "ld_qT", writes=[buf("qT")])
            P.dma("sp", cb[:, :], rel_d[31:32, :].partition_broadcast(128), "ld_cb", writes=[buf("cb")])
            P.dma("sp", lamt[:, :], lam_d[0:1, :].partition_broadcast(128), "ld_lam", writes=[buf("lamt")])
            P.dma("sp", sgb[:, :], sg_d[0:1, :].partition_broadcast(128), "ld_sgb", writes=[buf("sgb")])
            P.op("dve", lambda e: e.tensor_scalar(sgb[:, :], sgb[:, :], (1.0 - LAMBDA_INIT) * math.sqrt(128.0), None,
                                                  ALU.mult), reads=[buf("sgb")], writes=[buf("sgb")])
            lv = lamt[:, :].rearrange("p (a b d) -> p a b d", a=2, b=2)
            P.op("dve", lambda e: e.tensor_tensor(osb[:, 0, :].rearrange("p (a d) -> p a d", a=2),
                                                  lv[:, :, 0, :], lv[:, :, 1, :], ALU.mult),
                 reads=[buf("lamt")], writes=[buf("osb0")])
            P.op("dve", lambda e: e.tensor_reduce(sm[:, 0:2], osb[:, 0, :].rearrange("p (a d) -> p a d", a=2),
                                                  AX.X, ALU.add), reads=[buf("osb0")], writes=[buf("sm_lam")])
            P.op("act", lambda e: e.activation(out=sm[:, 2:4], in_=sm[:, 0:2], func=AF.Exp),
                 reads=[buf("sm_lam")], writes=[buf("sm_lam")])
            P.op("dve", lambda e: e.tensor_tensor(sm[:, 4:5], sm[:, 3:4], sm[:, 2:3], ALU.subtract),
                 reads=[buf("sm_lam")], writes=[buf("sm_lam")])
            P.op("dve", lambda e: e.tensor_scalar(sm[:, 4:5], sm[:, 4:5], -LAMBDA_INIT, None, ALU.add),
                 reads=[buf("sm_lam")], writes=[buf("sm_lam")])
            lamb = buf("sm_lam")
            P.dma("sp", tmask[:, :].rearrange("p (a q) -> p a q", a=3), tbm_d[:, :, :], "ld_tm", writes=[buf("tmask")])

            kall = kT_all.ap().rearrange("(h s k) t -> k s h t", s=2, h=NH)
            vall = v_all.ap().rearrange("(h s k) (j v) -> k s h j v", s=2, h=NH, v=VW)
            LOOK = int(os.environ.get('K_LOOK', '1'))
            sp_ctr = [0]
            cn_ctr = [0]
            ep_ctr = [0]
            pend = []
            tpend = []

            b1pend = []
            b2pend = []

            def drain_epilogue(keep1, keep2):
                while len(b2pend) > keep2:
                    b2pend.pop(0)()
                while len(b1pend) > keep1:
                    b1pend.pop(0)()

            def emit_epilogue(h, i, ob, hb):
                obank = bank[4 + ob]
                ek = ep_ctr[0] % 2
                ep_ctr[0] += 1
                sb_ = buf("sm_e%d" % ek)
                o0 = 8 + ek * 8
                ov = pb[4 + ob][:, :].rearrange("p (c w) -> p c w", c=2)
                while len(b2pend) > 0:
                    b2pend.pop(0)()
                while len(b1pend) > 0:
                    b1pend.pop(0)()
                P.op("dve", lambda e: e.reciprocal(sm[:, o0:o0 + 2], ov[:, :, 128]), reads=[obank], writes=[sb_])
                P.op("dve", lambda e: e.tensor_tensor(sm[:, o0 + 2:o0 + 3], sm[:, o0 + 1:o0 + 2], sm[:, 4:5], ALU.mult),
                     reads=[sb_, lamb], writes=[sb_])
                osbb = buf("osb%d" % ek)
                P.op("dve", lambda e: e.tensor_scalar(osb[:, ek, :], pb[4 + ob][:, 0:128], sm[:, o0:o0 + 1], None, ALU.mult),
                     reads=[obank, sb_], writes=[osbb])
                P.op("dve", lambda e: e.scalar_tensor_tensor(
                    osb[:, ek, :], pb[4 + ob][:, 256:384], sm[:, o0 + 2:o0 + 3], osb[:, ek, :], ALU.mult, ALU.add),
                    reads=[obank, sb_], writes=[osbb])
                P.op("dve", lambda e: e.scalar_tensor_tensor(
                    sg[:, 0, 0:128], osb[:, ek, :], 1.0, osb[:, ek, :], ALU.mult, ALU.mult, accum_out=sm[:, o0 + 3:o0 + 4]),
                    reads=[osbb], writes=[sb_, buf("junk")])

                def stage_b1():
                    P.op("act", lambda e: e.activation(out=sm[:, o0 + 4:o0 + 5], in_=sm[:, o0 + 3:o0 + 4], func=AF.Ln,
                                                       bias=128.0 * EPS, scale=1.0), reads=[sb_], writes=[sb_])
                    P.op("act", lambda e: e.activation(out=sm[:, o0 + 5:o0 + 6], in_=sm[:, o0 + 4:o0 + 5], func=AF.Exp,
                                                       scale=-0.5), reads=[sb_], writes=[sb_])
                    P.op("dve", lambda e: e.scalar_tensor_tensor(
                        onb[ek][:, :], osb[:, ek, :], sm[:, o0 + 5:o0 + 6], sgb[:, :], ALU.mult, ALU.mult),
                        reads=[osbb, sb_, buf("sgb")], writes=[buf("onb%d" % ek)])

                    def stage_b2():
                        tpv = pb[4 + ob][:, 448:512].bitcast(BF16)
                        P.op("pe", lambda e: e.transpose(tpv, onb[ek][:, :], ident[:]),
                             reads=[buf("onb%d" % ek), buf("ident")], writes=[obank])
                        P.op("dve", lambda e: e.tensor_copy(xT[:, h, i * 128:(i + 1) * 128], tpv),
                             reads=[obank], writes=[buf("xT%d" % i)])
                    b2pend.append(stage_b2)
                b1pend.append(stage_b1)

            def flush(upto):
                while len(pend) > upto:
                    pend.pop(0)()

            for h in range(NH):
                hb = h % 2
                kvb = buf("kv%d" % hb)
                tbb = buf("tb%d" % hb)
                if kv_ready is not None:
                    P.wait("sp", ("cc", 2 * h + 2))
                P.dma("sp", kTt[hb][:, :, :], kall[:, :, h, :], "ld_kv%d" % hb, writes=[kvb])
                P.dma("sp", vt[hb][:, :, :, :], vall[:, :, h, :, :], "ld_kv%d" % hb, writes=[])
                kvb.w = ("ld_kv%d" % hb, P.cnt["ld_kv%d" % hb])
                P.dma("sp", Tb[hb].rearrange("p (a q) -> p a q", a=3), tbg_d[:, h, :, :], "ld_tb%d" % hb, writes=[tbb])
                P.op("dve", lambda e, hb=hb: e.tensor_tensor(Tb[hb], Tb[hb], tmask[:, :], ALU.add),
                     reads=[buf("tmask")], writes=[tbb])

                for i in range(NB):
                    ob = (h * NB + i) % 2
                    obank = bank[4 + ob]
                    special = [(0, i, 0), (1, i, 1)] + ([(1, i - 1, 2)] if i >= 1 else [])
                    consts = [(0, j) for j in range(i)] + [(1, j) for j in range(i - 1)]
                    cu = [consts[u:u + 4] for u in range(0, len(consts), 4)]
                    def special_front(blks=special, h=h, i=i, hb=hb, kvb=kvb, tbb=tbb):
                        bks = [bank[6], bank[7]]
                        pbs = [pb8[6], pb8[7]]

                        def f_qk_s(e):
                            for (s_, j, kd) in blks:
                                for c in range(2):
                                    ins = e.matmul(pbs[c][:, kd * 128:(kd + 1) * 128],
                                                   kTt[hb][c * 64:(c + 1) * 64, s_, j * 128:(j + 1) * 128],
                                                   qT[c * 64:(c + 1) * 64, h, i * 128:(i + 1) * 128],
                                                   start=True, stop=True)
                            return ins
                        P.op("pe", f_qk_s, reads=[kvb, buf("qT")], writes=bks)
                        nsp = len(blks)

                        def f_sadd(e):
                            for c in range(2):
                                ins = e.scalar_tensor_tensor(
                                    tmpA[c][:, 0:nsp * 128], pbs[c][:, 0:nsp * 128], 0.125, Tb[hb][:, 0:nsp * 128],
                                    ALU.mult, ALU.add)
                            return ins
                        P.op("dve", f_sadd, reads=bks + [tbb], writes=[buf("tmpA")])
                    special_front()

                    plist = [("c", u_) for u_ in cu] + [("s", special)]
                    for pn, (kind, blks) in enumerate(plist):
                        first_pair = (pn == 0)
                        last_pair = (pn == len(plist) - 1)
                        if kind == "s":
                            sps = sp_ctr[0] % 2
                            sp_ctr[0] += 1

                            nsp = len(blks)
                            tsb = buf("tmpA")
                            psb = buf("pTs%d" % (2 * sps))
                            ps2 = av(35840 + sps * 768, 768, "p (c n) -> p c n", c=2)
                            P.op("act", lambda e, nsp=nsp, ps2=ps2: e.activation(
                                out=ps2[:, :, 0:nsp * 128],
                                in_=f32m[:, 768:1536].rearrange("p (c n) -> p c n", c=2)[:, :, 0:nsp * 128], func=AF.Exp),
                                reads=[tsb], writes=[psb])
                            srcs = [(pTs[2 * sps], psb), (pTs[2 * sps + 1], psb)]
                            cols = [kd for (_, _, kd) in blks]
                            kbl = [(s_, j) for (s_, j, _) in blks]
                        else:
                            cps = cn_ctr[0] % 2
                            cn_ctr[0] += 1
                            bks = [bank[2 * cps], bank[2 * cps + 1]]
                            pbs = [pb[2 * cps], pb[2 * cps + 1]]

                            def f_qk_c(e, blks=blks, h=h, i=i, hb=hb, pbs=pbs):
                                for n, (s_, j) in enumerate(blks):
                                    for c in range(2):
                                        ins = e.matmul(pbs[c][:, n * 128:(n + 1) * 128],
                                                       kTt[hb][c * 64:(c + 1) * 64, s_, j * 128:(j + 1) * 128],
                                                       qT[c * 64:(c + 1) * 64, h, i * 128:(i + 1) * 128],
                                                       start=True, stop=True)
                                return ins
                            P.op("pe", f_qk_c, reads=[kvb, buf("qT")], writes=bks)
                            nb_ = len(blks)
                            ptb = buf("pT%d" % (2 * cps))
                            pt2 = av(33792 + cps * 1024, 1024, "p (c n) -> p c n", c=2)
                            P.op("act", lambda e, cps=cps, nb_=nb_, h=h, pt2=pt2: e.activation(
                                out=pt2[:, :, 0:nb_ * 128],
                                in_=pq[cps][:, :].rearrange("p (c n) -> p c n", c=2)[:, :, 0:nb_ * 128], func=AF.Exp,
                                bias=cb[:, h:h + 1], scale=0.125), reads=bks + [buf("cb")], writes=[ptb])
                            srcs = [(pT[2 * cps], ptb), (pT[2 * cps + 1], ptb)]
                            cols = list(range(nb_))
                            kbl = list(blks)

                        def mk_pv(kbl=kbl, cols=cols, srcs=srcs, hb=hb, ob=ob, obank=obank, kvb=kvb,
                                  first_pair=first_pair, last_pair=last_pair, h=h, i=i):
                            def f_pv(e):
                                for c in range(2):
                                    for n, (s_, j) in enumerate(kbl):
                                        ins = e.matmul(pb[4 + ob][:, c * 256:c * 256 + 129],
                                                       srcs[c][0][:, cols[n] * 128:(cols[n] + 1) * 128],
                                                       vt[hb][:, s_, j, 0:129],
                                                       start=(first_pair and c == 0 and n == 0),
                                                       stop=(last_pair and n == len(kbl) - 1),
                                                       skip_group_check=True)
                                return ins

                            def go():
                                P.op("pe", f_pv, reads=[srcs[0][1], srcs[1][1], kvb], writes=[obank])
                                if last_pair:
                                    emit_epilogue(h, i, ob, hb)
                            return go
                        pend.append(mk_pv())
                        flush(LOOK)
            flush(0)
            drain_epilogue(0, 0)
            drain_epilogue(0, 0)

            load_lnp(2)
            wob = buf("wo")
            if os.environ.get('K_BAR'):
                P.wait("pool", ("c_pe", P.cnt["c_pe"]))
            P.dma("pool", wo[:, :, :], wo_d.rearrange("(a p) f -> p a f", p=128), "ld_wo", writes=[wob])
            woT = []

            def wo_front(i):
                bp = 2 * (i % 2)

                def f_wo(e):
                    for half in range(2):
                        for a_ in range(8):
                            ins = e.matmul(pb[bp + half][:, :], xT[:, a_, i * 128:(i + 1) * 128],
                                           wo[:, a_, half * 512:(half + 1) * 512], start=(a_ == 0), stop=(a_ == 7))
                    return ins
                P.op("pe", f_wo, reads=[buf("xT%d" % i), wob], writes=[bank[bp], bank[bp + 1]])

            wo_front(0)
            for i in range(NB):
                rb = buf("resid%d" % i)
                if i + 1 < NB:
                    wo_front(i + 1)
                bp = 2 * (i % 2)
                zt = zbs[i % 2]
                zbb = buf("zb%d" % (i % 2))

                def f_zo(e, i=i, zt=zt, bp=bp):
                    e.scalar_tensor_tensor(zt[:, 0:512], resid[:, i, 0:512], ALPHA, pb[bp][:, :], ALU.mult, ALU.add)
                    return e.scalar_tensor_tensor(zt[:, 512:1024], resid[:, i, 512:1024], ALPHA, pb[bp + 1][:, :],
                                                  ALU.mult, ALU.add)
                P.op("dve", f_zo, reads=[bank[bp], bank[bp + 1], rb], writes=[zbb])
                woT.append(emit_ln(zt, [zbb], i, want_T=True, us=i % 2))
                while len(woT) > 1:
                    woT.pop(0)()
            while woT:
                woT.pop(0)()
            emit_ffn(1, 3, last=True)
            P.wait("sp", ("st_out", P.cnt["st_out"]))
        else:
            for nm in ("st_kst0", "st_kst1", "st_vst0", "st_vst1", "st_x1", "st_qT"):
                P.wait("sp", (nm, P.cnt[nm]))

        P.emit()
    return nc


def _rel_bucket_np(n):
    n = np.maximum(n, 0)
    nf = np.maximum(n, 1).astype(np.float32)
    large = 16 + (np.log(nf / np.float32(16)) / np.float32(math.log(8.0)) * np.float32(16)).astype(np.int32)
    large = np.minimum(large, 31)
    return np.where(n < 16, n, large)


def _amat(rank):
    A = np.zeros((128, 12, 128), np.float32)
    s = np.arange(128)[:, None]
    t = np.arange(128)[None, :]
    for g, w in enumerate(POOL_WINDOWS):
        band = ((t - s) >= 0) & ((t - s) < w)
        eye = (s == t).astype(np.float32)
        diag = band.astype(np.float32) / w - eye
        cnt = np.minimum(t + 1, w).astype(np.float32)
        first = band.astype(np.float32) / cnt - eye
        A[:, 4 + g, :] = diag
        A[:, g, :] = first if rank == 0 else diag
        sh = np.arange(16)[:, None] - 16
        bandh = ((t - sh) >= 0) & ((t - sh) < w)
        A[0:16, 8 + g, :] = bandh.astype(np.float32) / w
    return A.astype(ml_dtypes.bfloat16)


def _bias_idx(rank):
    k = np.arange(128)[:, None]
    q = np.arange(128)[None, :]
    idx = np.zeros((128, 3, 128), np.int64)
    msk = np.zeros((128, 3, 128), np.float32)
    for tdx, delta in enumerate((rank, rank - 1, rank + 1)):
        rel = delta * 128 + q - k
        idx[:, tdx, :] = _rel_bucket_np(rel)
        msk[:, tdx, :] = np.where(rel >= 0, 0.0, NEG)
    return idx, msk


_NC_CACHE = {}


def _get_nc(mode):
    if mode not in _NC_CACHE:
        _NC_CACHE[mode] = build(mode)
    return _NC_CACHE[mode]


FUSED = True
NOCC = False


def kernel(x, pool_w, pool_scale, w_qkv, w_o, lam_p, subln_g, rel_table, w_gate, w_up, w_down,
           ln_mix_g, ln_mix_b, ln_ffn_g, ln_ffn_b):
    f32 = lambda a: np.ascontiguousarray(np.asarray(a, dtype=np.float32))
    x = f32(x)
    Bn, S, _ = x.shape
    lnp = np.stack([f32(ln_mix_g)[0], f32(ln_mix_b)[0], f32(ln_ffn_g)[0], f32(ln_ffn_b)[0],
                    f32(ln_mix_g)[1], f32(ln_mix_b)[1], f32(ln_ffn_g)[1], f32(ln_ffn_b)[1]], 0)
    ident = np.eye(128, dtype=np.float32).astype(ml_dtypes.bfloat16)
    rel_table = f32(rel_table)
    common = {
        "ident": ident, "identf": np.eye(128, dtype=np.float32), "lnp": lnp,
        "w_gate": f32(w_gate), "w_up": f32(w_up), "w_down": f32(w_down),
    }
    l0 = {"pool_w": f32(pool_w)[0], "pool_scale": f32(pool_scale), "w_qkv": f32(w_qkv)[0]}
    l1 = {"w_o": f32(w_o)[0], "lam_p": f32(lam_p).reshape(1, 256), "subln_g": f32(subln_g).reshape(1, 128),
          "rel_table": rel_table}
    per_core = []
    for core in range(N_CORES):
        b, r = core // 2, core % 2
        xb = x[b].reshape(32, 128, D)
        own = xb[r::2]
        halo = np.zeros((NB, 16, D), np.float32)
        for i in range(NB):
            g = 2 * i + r
            if g > 0:
                halo[i] = xb[g - 1][112:128]
        idx, msk = _bias_idx(r)
        tb_g = np.ascontiguousarray(rel_table[idx].transpose(0, 3, 1, 2))
        per_core.append({"x_own": np.ascontiguousarray(own.reshape(T, D)), "x_halo": halo, "amat": _amat(r),
                         "tb_g": tb_g, "tb_m": msk})

    def assemble(outs):
        y = np.zeros((Bn, 32, 128, D), np.float32)
        for core in range(N_CORES):
            b, r = core // 2, core % 2
            y[b, r::2] = outs[core].reshape(NB, 128, D)
        return y.reshape(Bn, S, D)

    if FUSED:
        nc = _get_nc("fused")
        maps = []
        for core in range(N_CORES):
            m = dict(common); m.update(l0); m.update(l1); m.update(per_core[core])
            maps.append(m)
        res = run_bass_kernel_spmd(nc, maps, core_ids=list(range(N_CORES)))
        return assemble([res.results[c]["out"] for c in range(N_CORES)])

    ncA = _get_nc("A")
    mapsA = []
    for core in range(N_CORES):
        m = dict(common); m.update(l0)
        for kk in ("x_own", "x_halo", "amat"):
            m[kk] = per_core[core][kk]
        mapsA.append(m)
    resA = run_bass_kernel_spmd(ncA, mapsA, core_ids=list(range(N_CORES))).results
    ncB = _get_nc("B")
    mapsB = []
    for core in range(N_CORES):
        b = core // 2
        m = dict(common); m.update(l1)
        m["tb_g"] = per_core[core]["tb_g"]; m["tb_m"] = per_core[core]["tb_m"]
        m["x1"] = resA[core]["x1"]; m["qT"] = resA[core]["qT"]
        m["kT_all"] = np.stack([resA[2 * b]["kT_own"].reshape(NH, 128, T),
                                resA[2 * b + 1]["kT_own"].reshape(NH, 128, T)], 1).reshape(2 * NH * 128, T)
        m["v_all"] = np.stack([resA[2 * b]["v_own"].reshape(NH, 128, NB * VW),
                               resA[2 * b + 1]["v_own"].reshape(NH, 128, NB * VW)], 1).reshape(2 * NH * 128, NB * VW)
        mapsB.append(m)
    resB = run_bass_kernel_spmd(ncB, mapsB, core_ids=list(range(N_CORES))).results
    return assemble([resB[c]["out"] for c in range(N_CORES)])
```

```python
import math
import os
from contextlib import ExitStack

import numpy as np
import ml_dtypes

import concourse.bass as bass
import concourse.mybir as mybir
from concourse.bass_utils import run_bass_kernel_spmd

F32 = mybir.dt.float32
BF16 = mybir.dt.bfloat16
AF = mybir.ActivationFunctionType
ALU = mybir.AluOpType
AX = mybir.AxisListType

D = 1024
DFF = 2816
NB = 16
T = NB * 128
NH = 8
VW = 144
ALPHA = 4.0 ** 0.25
EPS = 1e-5
LAMBDA_INIT = 0.8 - 0.6 * math.exp(-0.3 * 1)
NEG = -30000.0
POOL_WINDOWS = (2, 4, 8, 16)
N_CORES = 8


class Buf:
    __slots__ = ("name", "w", "r", "rng")

    def __init__(self, name, rng=None):
        self.name = name
        self.w = None
        self.r = {}
        self.rng = rng


class Prog:
    ENG = ("pe", "act", "dve", "pool", "sp")

    def __init__(self, nc, es):
        self.nc = nc
        self.es = es
        self.q = {e: [] for e in self.ENG}
        self.sem = {}
        self.cnt = {}
        self.waited = {e: {} for e in self.ENG}
        self.ranged = []
        for e in ("pe", "act", "dve", "pool"):
            self.newsem("c_" + e)

    def newsem(self, name):
        self.sem[name] = self.es.enter_context(self.nc.semaphore(name))
        self.cnt[name] = 0

    def wait(self, eng, tok):
        if tok is None:
            return
        s, v = tok
        if self.waited[eng].get(s, 0) >= v:
            return
        self.waited[eng][s] = v
        h = self.sem[s]
        self.q[eng].append(lambda e, h=h, v=v: e.wait_ge(h, v))

    def _deps(self, eng, reads, writes):
        for b in reads:
            self.wait(eng, b.w)
        for b in writes:
            self.wait(eng, b.w)
            for t in b.r.values():
                self.wait(eng, t)
            if b.rng is not None:
                for y in self.ranged:
                    if y is not b and y.rng[0] < b.rng[1] and b.rng[0] < y.rng[1]:
                        self.wait(eng, y.w)
                        for t in y.r.values():
                            self.wait(eng, t)

    def _mark(self, eng, tok, reads, writes):
        for b in reads:
            b.r[eng] = tok
        for b in writes:
            b.w = tok
            b.r = {}

    def op(self, eng, fn, reads=(), writes=()):
        self._deps(eng, reads, writes)
        s = "c_" + eng
        self.cnt[s] += 1
        tok = (s, self.cnt[s])
        h = self.sem[s]
        self.q[eng].append(lambda e, fn=fn, h=h: fn(e).then_inc(h, 1))
        self._mark(eng, tok, reads, writes)
        return tok

    def dma(self, eng, out, in_, sem, reads=(), writes=()):
        if sem not in self.sem:
            self.newsem(sem)
        self._deps(eng, reads, writes)
        self.cnt[sem] += 16
        tok = (sem, self.cnt[sem])
        h = self.sem[sem]
        self.q[eng].append(lambda e, out=out, in_=in_, h=h: e.dma_start(out=out, in_=in_).then_inc(h, 16))
        self._mark("dma_" + sem, tok, reads, writes)
        return tok

    def emit(self):
        nc = self.nc
        with nc.Block() as block:
            @block.tensor
            def _(e):
                for f in self.q["pe"]:
                    f(e)

            @block.scalar
            def _(e):
                for f in self.q["act"]:
                    f(e)

            @block.vector
            def _(e):
                for f in self.q["dve"]:
                    f(e)

            @block.gpsimd
            def _(e):
                for f in self.q["pool"]:
                    f(e)

            @block.sync
            def _(e):
                for f in self.q["sp"]:
                    f(e)


def build(mode):
    nc = bass.Bass("TRN2", target_bir_lowering=False)
    do_l0 = mode in ("fused", "A")
    do_att = mode in ("fused", "B")

    def din(name, shape, dt=F32):
        return nc.dram_tensor(name, list(shape), dt, kind="ExternalInput")

    def dout(name, shape, dt=F32):
        return nc.dram_tensor(name, list(shape), dt, kind="ExternalOutput")

    ident_d = din("ident", [128, 128], BF16)
    identf_d = din("identf", [128, 128], F32)
    lnp_d = din("lnp", [8, D])
    if do_l0:
        x_d = din("x_own", [T, D])
        halo_d = din("x_halo", [NB, 16, D])
        amat_d = din("amat", [128, 12, 128], BF16)
        poolw_d = din("pool_w", [4, 256, 256])
        pscale_d = din("pool_scale", [1, D])
        wqkv_d = din("w_qkv", [D, 3 * D])
    wg_d = din("w_gate", [2, D, DFF])
    wu_d = din("w_up", [2, D, DFF])
    wdn_d = din("w_down", [2, DFF, D])
    if do_att:
        wo_d = din("w_o", [D, D])
        lam_d = din("lam_p", [1, 256])
        sg_d = din("subln_g", [1, 128])
        rel_d = din("rel_table", [32, NH])
        tbg_d = din("tb_g", [128, NH, 3, 128])
        tbm_d = din("tb_m", [128, 3, 128])
        out_d = dout("out", [T, D])

    if mode == "fused":
        kT_own = nc.dram_tensor("kT_own", [NH * 128, T], BF16)
        v_own = nc.dram_tensor("v_own", [NH * 128, NB * VW], BF16)
        kT_all = nc.dram_tensor("kT_all", [2 * NH * 128, T], BF16)
        v_all = nc.dram_tensor("v_all", [2 * NH * 128, NB * VW], BF16)
    elif mode == "A":
        kT_own = dout("kT_own", [NH * 128, T], BF16)
        v_own = dout("v_own", [NH * 128, NB * VW], BF16)
        x1_d = dout("x1", [T, D])
        qT_d = dout("qT", [128, NH, T], BF16)
    else:
        kT_all = din("kT_all", [2 * NH * 128, T], BF16)
        v_all = din("v_all", [2 * NH * 128, NB * VW], BF16)
        x1_d = din("x1", [T, D])
        qT_d = din("qT", [128, NH, T], BF16)

    with ExitStack() as es:
        def sb(name, shape, dt):
            return es.enter_context(nc.sbuf_tensor(name, list(shape), dt))

        def ps(name, shape, dt):
            return es.enter_context(nc.psum_tensor(name, list(shape), dt))

        resid = sb("resid", [128, NB, D], F32)
        xT = sb("xT", [128, 8, T], BF16)
        lnp_t = sb("lnp_t", [128, 2, D], F32)
        zb = sb("zb", [128, D], F32)
        identf = sb("identf_s", [128, 128], F32)
        sg = sb("sg", [128, 2, 512], BF16)
        f32m = sb("f32m", [128, 1536], F32)
        tmask = sb("tmask", [128, 384], F32)
        osb = sb("osb", [128, 2, 128], F32)
        sgb = sb("sgb", [128, 128], F32)
        ident = sb("ident_s", [128, 128], BF16)
        st = sb("st", [128, 2, 2, 6], F32)
        mv = sb("mv", [128, 2, 2], F32)
        rs = sb("rs", [128, 2, 2], F32)
        sm = sb("sm", [128, 64], F32)
        cb = sb("cb", [128, NH], F32)
        lamt = sb("lamt", [128, 256], F32)
        AR = 43008
        arena = sb("arena", [128, AR], BF16)

        def av(lo, n, pat=None, **kw):
            a = arena[:, lo:lo + n]
            return a.rearrange(pat, **kw) if pat else a

        hT = av(0, 11264, "p (a b) -> p a b", a=11)
        wd = av(11264, 11264, "p (a b) -> p a b", a=11)
        gu = [av(22528 + k * 4096, 4096, "p (g c f) -> p g c f", g=2, c=8) for k in range(3)]
        halo = av(0, 4096, "p (a b) -> p a b", a=4)
        xin_bf = av(4096, 3072, "p (a b) -> p a b", a=3)
        pmT = av(7168, 3072, "p (a c t) -> p a c t", a=3, c=8)
        amat = av(34816, 1536, "p (a b) -> p a b", a=12)
        wp = av(36352, 2048, "p (g k d) -> p g k d", g=4, k=2)
        ps_bc = f32m[:, 0:1024]
        wq = [av(22528 + k * 4096, 4096, "p (c f) -> p c f", c=8) for k in range(3)]
        kst = [av(34816 + k * 2048, 2048) for k in range(2)]
        vst = [av(38912 + k * 1152, 1152, "p (h v) -> p h v", h=NH) for k in range(2)]
        qT = av(0, 16384, "p (h t) -> p h t", h=NH)
        kTt = [av(16384 + k * 8704, 4096, "p (s t) -> p s t", s=2) for k in range(2)]
        vt = [av(16384 + k * 8704 + 4096, 4608, "p (s j v) -> p s j v", s=2, j=NB) for k in range(2)]
        pT = [av(33792 + k * 512, 512) for k in range(6)]
        pTs = [av(36864 + k * 384, 384) for k in range(4)]
        onb = [av(40448 + k * 128, 128) for k in range(2)]
        wo = av(22528, 8192, "p (a f) -> p a f", a=8)
        Tb = [f32m[:, k * 384:(k + 1) * 384] for k in range(2)]
        tmpA = [f32m[:, 768 + k * 384:768 + (k + 1) * 384] for k in range(2)]

        pq = [ps("pq%d" % k, [128, 1024], F32) for k in range(4)]
        pb8 = [pq[k // 2][:, (k % 2) * 512:(k % 2 + 1) * 512] for k in range(8)]
        pb = pb8[:6]

        P = Prog(nc, es)
        B = {}

        RNG = {"hT0": (0, 11264), "hT1": (0, 11264), "wd": (11264, 22528),
               "gu0": (22528, 26624), "gu1": (26624, 30720), "gu2": (30720, 34816),
               "halo0": (0, 4096), "halo1": (0, 4096), "halo2": (0, 4096), "halo3": (0, 4096),
               "xin0": (4096, 5120), "xin1": (5120, 6144), "xin2": (6144, 7168),
               "pmT0": (7168, 8192), "pmT1": (8192, 9216), "pmT2": (9216, 10240),
               "amat": (34816, 36352), "wp": (36352, 38400),
               "kst0": (34816, 36864), "kst1": (36864, 38912), "vst0": (38912, 40064), "vst1": (40064, 41216),
               "qT": (0, 16384), "kv0": (16384, 25088), "kv1": (25088, 33792),
               "pT0": (33792, 34816), "pT2": (34816, 35840), "pT4": (35840, 36864),
               "pTs0": (36864, 37632), "pTs2": (37632, 38400),
               "onb0": (40448, 40576), "onb1": (40576, 40704),
               "wo": (22528, 30720), "zb1": (38400, 40448)}

        def buf(name):
            if name not in B:
                B[name] = Buf(name, RNG.get(name))
                if name in RNG:
                    P.ranged.append(B[name])
            return B[name]

        bank = [buf("bank%d" % k) for k in range(8)]

        P.dma("sp", ident[:], ident_d[:, :], "ld_ident", writes=[buf("ident")])
        P.dma("sp", identf[:], identf_d[:, :], "ld_identf", writes=[buf("identf")])
        zbs = [zb[:], arena[:, 38400:40448].bitcast(F32)]

        def load_lnp(idx):
            P.dma("sp", lnp_t[:, 0, :], lnp_d[2 * idx:2 * idx + 1, :].partition_broadcast(128), "ld_lnp",
                  writes=[buf("lnp")])
            P.dma("sp", lnp_t[:, 1, :], lnp_d[2 * idx + 1:2 * idx + 2, :].partition_broadcast(128), "ld_lnp",
                  writes=[])
            B["lnp"].w = ("ld_lnp", P.cnt["ld_lnp"])

        ln_ctr = [0]

        def emit_ln(zsrc, zbuf_list, blk, want_T, us, evac="act"):
            k = ln_ctr[0] % 2
            ln_ctr[0] += 1
            bst, bmv, brs = buf("st%d" % k), buf("mv%d" % k), buf("rs%d" % k)
            zt = zbs[us]
            zbb = buf("zb%d" % us)

            def f_stats(e, k=k, zsrc=zsrc):
                e.bn_stats(st[:, k, 0, :], zsrc[:, 0:512])
                return e.bn_stats(st[:, k, 1, :], zsrc[:, 512:1024])
            P.op("dve", f_stats, reads=zbuf_list, writes=[bst])
            P.op("dve", lambda e, k=k: e.bn_aggr(mv[:, k, :], st[:, k, :, :]), reads=[bst], writes=[bmv])
            P.op("act", lambda e, k=k: e.activation(out=rs[:, k, 0:1], in_=mv[:, k, 1:2], func=AF.Sqrt,
                                                    bias=EPS, scale=1.0),
                 reads=[bmv], writes=[brs])
            P.op("dve", lambda e, k=k, zsrc=zsrc, zt=zt: e.scalar_tensor_tensor(
                zt, zsrc, mv[:, k, 0:1], lnp_t[:, 0, :], ALU.subtract, ALU.mult),
                reads=zbuf_list + [bmv, buf("lnp")], writes=[zbb])
            P.op("dve", lambda e, k=k: e.reciprocal(rs[:, k, 1:2], rs[:, k, 0:1]), reads=[brs], writes=[brs])
            rb = buf("resid%d" % blk)
            P.op("dve", lambda e, k=k, blk=blk, zt=zt: e.scalar_tensor_tensor(
                resid[:, blk, :], zt, rs[:, k, 1:2], lnp_t[:, 1, :], ALU.mult, ALU.add),
                reads=[zbb, brs, buf("lnp")], writes=[rb])
            if not want_T:
                return None

            def do_T(blk=blk, rb=rb):
                def f_tp(e):
                    for c in range(8):
                        ins = e.transpose(pb8[6 + c // 4][:, (c % 4) * 128:(c % 4 + 1) * 128],
                                          resid[:, blk, c * 128:(c + 1) * 128], identf[:])
                    return ins
                P.op("pe", f_tp, reads=[rb, buf("identf")], writes=[bank[6], bank[7]])
                xb_ = buf("xT%d" % blk)
                if evac == "act":
                    P.op("act", lambda e: e.copy(xT[:, 0:4, blk * 128:(blk + 1) * 128],
                                                 pb8[6][:, :].rearrange("p (c t) -> p c t", c=4)),
                         reads=[bank[6]], writes=[xb_])
                    P.op("act", lambda e: e.copy(xT[:, 4:8, blk * 128:(blk + 1) * 128],
                                                 pb8[7][:, :].rearrange("p (c t) -> p c t", c=4)),
                         reads=[bank[7]], writes=[xb_])
                else:
                    P.op("dve", lambda e: e.tensor_copy(xT[:, 0:4, blk * 128:(blk + 1) * 128],
                                                        pb8[6][:, :].rearrange("p (c t) -> p c t", c=4)),
                         reads=[bank[6]], writes=[xb_])
                    P.op("dve", lambda e: e.tensor_copy(xT[:, 4:8, blk * 128:(blk + 1) * 128],
                                                        pb8[7][:, :].rearrange("p (c t) -> p c t", c=4)),
                         reads=[bank[7]], writes=[xb_])
            return do_T

        def emit_ffn(L, lnp_idx, last):
            load_lnp(lnp_idx)
            wgv = wg_d[L].rearrange("(c p) f -> p c f", p=128)
            wuv = wu_d[L].rearrange("(c p) f -> p c f", p=128)
            wdv = wdn_d[L].rearrange("(a p) d -> p a d", p=128)
            gctr = 0
            defer_T = []
            for tt in range(2):
                for part in range(2):
                    groups = [(0, 2), (2, 2), (4, 2), (6, 2), (8, 2), (10, 1)]
                    gtok = {}

                    def load_gu(gi, part=part):
                        j0_, nj_ = groups[gi]
                        gb_ = (gctr + gi) % 3
                        f0 = (part * 11 + j0_) * 128
                        w = nj_ * 128
                        gbuf_ = buf("gu%d" % gb_)
                        P.dma("pool", gu[gb_][:, 0, :, 0:w], wgv[:, :, f0:f0 + w], "ld_gu%d" % gb_, writes=[gbuf_])
                        P.dma("pool", gu[gb_][:, 1, :, 0:w], wuv[:, :, f0:f0 + w], "ld_gu%d" % gb_, writes=[])
                        gbuf_.w = ("ld_gu%d" % gb_, P.cnt["ld_gu%d" % gb_])
                    for gi in range(3):
                        load_gu(gi)
                    P.dma("pool", wd[:, :, :], wdv[:, part * 11:(part + 1) * 11, :], "ld_wd", writes=[buf("wd")])
                    for gi, (j0, nj) in enumerate(groups):
                        gb = (gctr + gi) % 3
                        gbuf = buf("gu%d" % gb)
                        if gi >= 3:
                            load_gu(gi)
                        for j in range(nj):
                            jl = j0 + j
                            for ts in range(2):
                                k = (jl * 2 + ts) % 2
                                t0 = tt * 1024 + ts * 512
                                xbufs = [buf("xT%d" % (t0 // 128 + q)) for q in range(4)]

                                def f_gu(e, gb=gb, j=j, t0=t0, k=k):
                                    for c in range(8):
                                        e.matmul(pb[k][:, :], gu[gb][:, 0, c, j * 128:(j + 1) * 128],
                                                 xT[:, c, t0:t0 + 512], start=(c == 0), stop=(c == 7))
                                    for c in range(8):
                                        ins = e.matmul(pb[2 + k][:, :], gu[gb][:, 1, c, j * 128:(j + 1) * 128],
                                                       xT[:, c, t0:t0 + 512], start=(c == 0), stop=(c == 7))
                                    return ins
                                P.op("pe", f_gu, reads=[gbuf] + xbufs, writes=[bank[k], bank[2 + k]])
                                P.op("act", lambda e, k=k: e.activation(out=sg[:, k, :], in_=pb[k][:, :], func=AF.Silu),
                                     reads=[bank[k]], writes=[buf("sg%d" % k)])
                                P.op("dve", lambda e, k=k, jl=jl, ts=ts: e.tensor_tensor(
                                    hT[:, jl, ts * 512:(ts + 1) * 512], sg[:, k, :], pb[2 + k][:, :], ALU.mult),
                                    reads=[buf("sg%d" % k), bank[2 + k]], writes=[buf("hT%d" % ts)])
                        for _ in range(2):
                            if defer_T:
                                defer_T.pop(0)()
                    gctr += len(groups)
                    for b8 in range(8):
                        blk = tt * 8 + b8
                        ts = b8 // 4
                        rb = buf("resid%d" % blk)
                        for dh in range(2):
                            k = (b8 * 2 + dh) % 2

                            def f_dn(e, b8=b8, dh=dh, k=k):
                                for j in range(11):
                                    ins = e.matmul(pb[4 + k][:, :], hT[:, j, b8 * 128:(b8 + 1) * 128],
                                                   wd[:, j, dh * 512:(dh + 1) * 512], start=(j == 0), stop=(j == 10))
                                return ins
                            P.op("pe", f_dn, reads=[buf("hT%d" % ts), buf("wd")], writes=[bank[4 + k]])
                            if part == 0:
                                P.op("dve", lambda e, blk=blk, dh=dh, k=k: e.scalar_tensor_tensor(
                                    resid[:, blk, dh * 512:(dh + 1) * 512], resid[:, blk, dh * 512:(dh + 1) * 512],
                                    ALPHA, pb[4 + k][:, :], ALU.mult, ALU.add),
                                    reads=[bank[4 + k]], writes=[rb])
                            else:
                                P.op("dve", lambda e, blk=blk, dh=dh, k=k: e.tensor_tensor(
                                    resid[:, blk, dh * 512:(dh + 1) * 512], resid[:, blk, dh * 512:(dh + 1) * 512],
                                    pb[4 + k][:, :], ALU.add),
                                    reads=[bank[4 + k]], writes=[rb])
                        if part == 1:
                            dT = emit_ln(resid[:, blk, :], [rb], blk, want_T=not last, us=blk % 2)
                            if dT is not None:
                                defer_T.append(dT)
                            if last:
                                P.dma("sp", out_d[blk * 128:(blk + 1) * 128, :], resid[:, blk, :], "st_out", reads=[rb])
            while defer_T:
                defer_T.pop(0)()

        if do_l0:
            P.dma("sp", amat[:, :, :], amat_d[:, :, :], "ld_amat", writes=[buf("amat")])
            P.dma("sp", ps_bc, pscale_d[0:1, :].partition_broadcast(128), "ld_psbc", writes=[buf("f32m")])
            load_lnp(0)
            for q4 in range(4):
                P.dma("sp", resid[:, q4 * 4:(q4 + 1) * 4, :],
                      x_d[q4 * 512:(q4 + 1) * 512, :].rearrange("(b p) d -> p b d", p=128), "ld_x%d" % q4,
                      writes=[buf("resid%d" % (q4 * 4 + q)) for q in range(4)])
            P.dma("pool", wp[:, :, :, :], poolw_d.rearrange("g (k p) d -> p g k d", p=128), "ld_wp", writes=[buf("wp")])

            def f_wps(e):
                for g in range(4):
                    for kc in range(2):
                        ins = e.tensor_tensor(wp[:, g, kc, :], wp[:, g, kc, :], ps_bc[:, g * 256:(g + 1) * 256], ALU.mult)
                return ins
            P.op("dve", f_wps, reads=[buf("f32m")], writes=[buf("wp")])

            mixT = []

            def mix_front_a(i):
                rb = buf("resid%d" % i)
                hs = i % 4
                hb = buf("halo%d" % hs)
                P.dma("pool", halo[0:16, hs, :], halo_d[i, :, :], "ld_halo%d" % hs, writes=[hb])
                k2 = i % 3
                xb = buf("xin%d" % k2)
                P.op("act", lambda e: e.copy(xin_bf[:, k2, :], resid[:, i, :]), reads=[rb], writes=[xb])
                a0 = 0 if i == 0 else 4

                def f_pm(e):
                    for c in range(8):
                        g = c // 2
                        o = pb[c // 4][:, (c % 4) * 128:(c % 4 + 1) * 128]
                        e.matmul(o, xin_bf[:, k2, c * 128:(c + 1) * 128], amat[:, a0 + g, :], start=True, stop=False)
                        ins = e.matmul(o[:, 0:16], halo[0:16, hs, c * 128:(c + 1) * 128], amat[0:16, 8 + g, 0:16],
                                       start=False, stop=True)
                    return ins
                P.op("pe", f_pm, reads=[xb, hb, buf("amat")], writes=[bank[0], bank[1]])
                pmb = buf("pmT%d" % k2)

                def f_pmT(e):
                    e.copy(pmT[:, k2, 0:4, :], pb[0][:, :].rearrange("p (c t) -> p c t", c=4))
                    return e.copy(pmT[:, k2, 4:8, :], pb[1][:, :].rearrange("p (c t) -> p c t", c=4))
                P.op("act", f_pmT, reads=[bank[0], bank[1]], writes=[pmb])

            def mix_front_b(i):
                k2 = i % 3
                pmb = buf("pmT%d" % k2)
                mb = 2 + 2 * (i % 2)

                def f_mix(e):
                    for g in range(4):
                        for kc in range(2):
                            ins = e.matmul(pb[mb + g // 2][:, (g % 2) * 256:(g % 2 + 1) * 256],
                                           pmT[:, k2, 2 * g + kc, :], wp[:, g, kc, :], start=(kc == 0), stop=(kc == 1))
                    return ins
                P.op("pe", f_mix, reads=[pmb, buf("wp")], writes=[bank[mb], bank[mb + 1]])

            def mix_z(i):
                rb = buf("resid%d" % i)
                k2 = i % 2
                zt = zbs[k2]
                zbb = buf("zb%d" % k2)

                mb = 2 + 2 * (i % 2)

                def f_z1(e):
                    e.scalar_tensor_tensor(zt[:, 0:512], resid[:, i, 0:512], ALPHA, pb[mb][:, :], ALU.mult, ALU.add)
                    return e.scalar_tensor_tensor(zt[:, 512:1024], resid[:, i, 512:1024], ALPHA, pb[mb + 1][:, :],
                                                  ALU.mult, ALU.add)
                P.op("dve", f_z1, reads=[bank[mb], bank[mb + 1], rb], writes=[zbb])

            for j in range(2):
                mix_front_a(j)
                mix_front_b(j)
            for i in range(NB):
                mix_z(i)
                if i + 2 < NB:
                    mix_front_a(i + 2)
                    mix_front_b(i + 2)
                k2 = i % 2
                mixT.append(emit_ln(zbs[k2], [buf("zb%d" % k2)], i, want_T=True, us=k2, evac="dve"))
                while len(mixT) > 1:
                    mixT.pop(0)()
            while mixT:
                mixT.pop(0)()

            emit_ffn(0, 1, last=False)

            wqv = wqkv_d.rearrange("(c p) f -> p c f", p=128)
            allx = [buf("xT%d" % q) for q in range(NB)]
            for k in range(2):
                P.op("pool", lambda e, k=k: e.memset(vst[k][:, :, 128:VW], 0.0), writes=[buf("vst%d" % k)])
                P.op("pool", lambda e, k=k: e.memset(vst[k][:, :, 128:129], 1.0), writes=[buf("vst%d" % k)])
            bctr = 0
            for gidx, grp in enumerate((2, 3, 4, 5, 0, 1)):
                wbi = gidx % 3
                wb = buf("gu%d" % wbi)
                P.dma("pool", wq[wbi][:, :, :], wqv[:, :, grp * 512:(grp + 1) * 512], "ld_gu%d" % wbi, writes=[wb])
                if grp < 4:
                    for hh in range(4):
                        h = (grp % 2) * 4 + hh
                        if grp >= 2:
                            ks = h % 2
                            kb_ = buf("kst%d" % ks)
                        for ts in range(4):
                            k = bctr % 2
                            bctr += 1

                            def f_qk(e, wbi=wbi, hh=hh, ts=ts, k=k):
                                for c in range(8):
                                    ins = e.matmul(pb[k][:, :], wq[wbi][:, c, hh * 128:(hh + 1) * 128],
                                                   xT[:, c, ts * 512:(ts + 1) * 512], start=(c == 0), stop=(c == 7))
                                return ins
                            P.op("pe", f_qk, reads=[wb] + allx[ts * 4:ts * 4 + 4], writes=[bank[k]])
                            if grp < 2:
                                P.op("act", lambda e, h=h, ts=ts, k=k: e.copy(
                                    qT[:, h, ts * 512:(ts + 1) * 512], pb[k][:, :]),
                                    reads=[bank[k]], writes=[buf("qT")])
                            else:
                                P.op("act", lambda e, ks=ks, ts=ts, k=k: e.copy(
                                    kst[ks][:, ts * 512:(ts + 1) * 512], pb[k][:, :]),
                                    reads=[bank[k]], writes=[kb_])
                        if grp >= 2:
                            P.dma("sp", kT_own[h * 128:(h + 1) * 128, :], kst[ks][:, :], "st_kst%d" % ks, reads=[kb_])
                else:
                    half = grp - 4
                    for blk in range(NB):
                        k = bctr % 2
                        bctr += 1
                        vs = blk % 2
                        vb_ = buf("vst%d" % vs)

                        def f_v(e, wbi=wbi, blk=blk, k=k):
                            for c in range(8):
                                ins = e.matmul(pb[k][:, :], xT[:, c, blk * 128:(blk + 1) * 128], wq[wbi][:, c, :],
                                               start=(c == 0), stop=(c == 7))
                            return ins
                        P.op("pe", f_v, reads=[wb, allx[blk]], writes=[bank[k]])
                        P.op("dve", lambda e, vs=vs, half=half, k=k: e.tensor_copy(
                            vst[vs][:, half * 4:(half + 1) * 4, 0:128],
                            pb[k][:, :].rearrange("p (h v) -> p h v", h=4)),
                            reads=[bank[k]], writes=[vb_])
                        dst = v_own.ap().rearrange("(h k) (j v) -> k h j v", k=128, v=VW)[:, half * 4:(half + 1) * 4, blk, :]
                        P.dma("sp", dst, vst[vs][:, half * 4:(half + 1) * 4, :], "st_vst%d" % vs, reads=[vb_])
            if mode == "A":
                for blk in range(NB):
                    P.dma("sp", x1_d[blk * 128:(blk + 1) * 128, :], resid[:, blk, :], "st_x1",
                          reads=[buf("resid%d" % blk)])
                P.dma("sp", qT_d[:, :, :], qT[:, :, :], "st_qT", reads=[buf("qT")])

        kv_ready = None
        if mode == "fused":
            for nm in ("st_kst0", "st_kst1", "st_vst0", "st_vst1"):
                P.wait("pool", (nm, P.cnt[nm]))
            P.newsem("cc")
            groups = [[2 * b, 2 * b + 1] for b in range(N_CORES // 2)]
            hcc = P.sem["cc"]

            if not NOCC:
                for h in range(NH):
                    def f_k(e, h=h):
                        return e.collective_compute("AllGather", ALU.bypass, replica_groups=groups,
                                                    ins=[kT_own[h * 128:(h + 1) * 128, :]],
                                                    outs=[kT_all[h * 256:(h + 1) * 256, :]])

                    def f_v(e, h=h):
                        return e.collective_compute("AllGather", ALU.bypass, replica_groups=groups,
                                                    ins=[v_own[h * 128:(h + 1) * 128, :]],
                                                    outs=[v_all[h * 256:(h + 1) * 256, :]])
                    P.q["pool"].append(lambda e, f=f_k: f(e).then_inc(hcc, 1))
                    P.q["pool"].append(lambda e, f=f_v: f(e).then_inc(hcc, 1))
                kv_ready = True

        if do_att:
            if mode == "B":
                for q4 in range(4):
                    P.dma("sp", resid[:, q4 * 4:(q4 + 1) * 4, :],
                          x1_d[q4 * 512:(q4 + 1) * 512, :].rearrange("(b p) d -> p b d", p=128), "ld_x%d" % q4,
                          writes=[buf("resid%d" % (q4 * 4 + q)) for q in range(4)])
                P.dma("sp", qT[:, :, :], qT_d[:, :, :], "ld_qT", writes=[buf("qT")])
            P.dma("sp", cb[:, :], rel_d[31:32, :].partition_broadcast(128), "ld_cb", writes=[buf("cb")])
            P.dma("sp", lamt[:, :], lam_d[0:1, :].partition_broadcast(128), "ld_lam", writes=[buf("lamt")])
            P.dma("sp", sgb[:, :], sg_d[0:1, :].partition_broadcast(128), "ld_sgb", writes=[buf("sgb")])
            P.op("dve", lambda e: e.tensor_scalar(sgb[:, :], sgb[:, :], (1.0 - LAMBDA_INIT) * math.sqrt(128.0), None,
                                                  ALU.mult), reads=[buf("sgb")], writes=[buf("sgb")])
            lv = lamt[:, :].rearrange("p (a b d) -> p a b d", a=2, b=2)
            P.op("dve", lambda e: e.tensor_tensor(osb[:, 0, :].rearrange("p (a d) -> p a d", a=2),
                                                  lv[:, :, 0, :], lv[:, :, 1, :], ALU.mult),
                 reads=[buf("lamt")], writes=[buf("osb0")])
            P.op("dve", lambda e: e.tensor_reduce(sm[:, 0:2], osb[:, 0, :].rearrange("p (a d) -> p a d", a=2),
                                                  AX.X, ALU.add), reads=[buf("osb0")], writes=[buf("sm_lam")])
            P.op("act", lambda e: e.activation(out=sm[:, 2:4], in_=sm[:, 0:2], func=AF.Exp),
                 reads=[buf("sm_lam")], writes=[buf("sm_lam")])
            P.op("dve", lambda e: e.tensor_tensor(sm[:, 4:5], sm[:, 3:4], sm[:, 2:3], ALU.subtract),
                 reads=[buf("sm_lam")], writes=[buf("sm_lam")])
            P.op("dve", lambda e: e.tensor_scalar(sm[:, 4:5], sm[:, 4:5], -LAMBDA_INIT, None, ALU.add),
                 reads=[buf("sm_lam")], writes=[buf("sm_lam")])
            lamb = buf("sm_lam")
            P.dma("sp", tmask[:, :].rearrange("p (a q) -> p a q", a=3), tbm_d[:, :, :], "ld_tm", writes=[buf("tmask")])

            kall = kT_all.ap().rearrange("(h s k) t -> k s h t", s=2, h=NH)
            vall = v_all.ap().rearrange("(h s k) (j v) -> k s h j v", s=2, h=NH, v=VW)
            LOOK = int(os.environ.get('K_LOOK', '2'))
            sp_ctr = [0]
            cn_ctr = [0]
            ep_ctr = [0]
            pend = []
            tpend = []

            b1pend = []
            b2pend = []

            def drain_epilogue(keep1, keep2):
                while len(b2pend) > keep2:
                    b2pend.pop(0)()
                while len(b1pend) > keep1:
                    b1pend.pop(0)()

            def emit_epilogue(h, i, ob, hb):
                obank = bank[4 + ob]
                ek = ep_ctr[0] % 2
                ep_ctr[0] += 1
                sb_ = buf("sm_e%d" % ek)
                o0 = 8 + ek * 8
                ov = pb[4 + ob][:, :].rearrange("p (c w) -> p c w", c=2)
                while len(b2pend) > 0:
                    b2pend.pop(0)()
                while len(b1pend) > 0:
                    b1pend.pop(0)()
                P.op("dve", lambda e: e.reciprocal(sm[:, o0:o0 + 2], ov[:, :, 128]), reads=[obank], writes=[sb_])
                P.op("dve", lambda e: e.tensor_tensor(sm[:, o0 + 2:o0 + 3], sm[:, o0 + 1:o0 + 2], sm[:, 4:5], ALU.mult),
                     reads=[sb_, lamb], writes=[sb_])
                osbb = buf("osb%d" % ek)
                P.op("dve", lambda e: e.tensor_scalar(osb[:, ek, :], pb[4 + ob][:, 0:128], sm[:, o0:o0 + 1], None, ALU.mult),
                     reads=[obank, sb_], writes=[osbb])
                P.op("dve", lambda e: e.scalar_tensor_tensor(
                    osb[:, ek, :], pb[4 + ob][:, 256:384], sm[:, o0 + 2:o0 + 3], osb[:, ek, :], ALU.mult, ALU.add),
                    reads=[obank, sb_], writes=[osbb])
                P.op("dve", lambda e: e.scalar_tensor_tensor(
                    sg[:, 0, 0:128], osb[:, ek, :], 1.0, osb[:, ek, :], ALU.mult, ALU.mult, accum_out=sm[:, o0 + 3:o0 + 4]),
                    reads=[osbb], writes=[sb_, buf("junk")])

                def stage_b1():
                    P.op("act", lambda e: e.activation(out=sm[:, o0 + 4:o0 + 5], in_=sm[:, o0 + 3:o0 + 4], func=AF.Ln,
                                                       bias=128.0 * EPS, scale=1.0), reads=[sb_], writes=[sb_])
                    P.op("act", lambda e: e.activation(out=sm[:, o0 + 5:o0 + 6], in_=sm[:, o0 + 4:o0 + 5], func=AF.Exp,
                                                       scale=-0.5), reads=[sb_], writes=[sb_])
                    P.op("dve", lambda e: e.scalar_tensor_tensor(
                        onb[ek][:, :], osb[:, ek, :], sm[:, o0 + 5:o0 + 6], sgb[:, :], ALU.mult, ALU.mult),
                        reads=[osbb, sb_, buf("sgb")], writes=[buf("onb%d" % ek)])

                    def stage_b2():
                        tpv = pb[4 + ob][:, 448:512].bitcast(BF16)
                        P.op("pe", lambda e: e.transpose(tpv, onb[ek][:, :], ident[:]),
                             reads=[buf("onb%d" % ek), buf("ident")], writes=[obank])
                        P.op("dve", lambda e: e.tensor_copy(xT[:, h, i * 128:(i + 1) * 128], tpv),
                             reads=[obank], writes=[buf("xT%d" % i)])
                    b2pend.append(stage_b2)
                b1pend.append(stage_b1)

            def flush(upto):
                while len(pend) > upto:
                    pend.pop(0)()

            for h in range(NH):
                hb = h % 2
                kvb = buf("kv%d" % hb)
                tbb = buf("tb%d" % hb)
                if kv_ready is not None:
                    P.wait("sp", ("cc", 2 * h + 2))
                P.dma("sp", kTt[hb][:, :, :], kall[:, :, h, :], "ld_kv%d" % hb, writes=[kvb])
                P.dma("sp", vt[hb][:, :, :, :], vall[:, :, h, :, :], "ld_kv%d" % hb, writes=[])
                kvb.w = ("ld_kv%d" % hb, P.cnt["ld_kv%d" % hb])
                P.dma("sp", Tb[hb].rearrange("p (a q) -> p a q", a=3), tbg_d[:, h, :, :], "ld_tb%d" % hb, writes=[tbb])
                P.op("dve", lambda e, hb=hb, h=h: e.scalar_tensor_tensor(Tb[hb], Tb[hb], cb[:, h:h + 1], tmask[:, :],
                                                                         ALU.subtract, ALU.add),
                     reads=[buf("tmask"), buf("cb")], writes=[tbb])

                for i in range(NB):
                    ob = (h * NB + i) % 2
                    obank = bank[4 + ob]
                    special = [(0, i, 0), (1, i, 1)] + ([(1, i - 1, 2)] if i >= 1 else [])
                    consts = [(0, j) for j in range(i)] + [(1, j) for j in range(i - 1)]
                    cu = [consts[u:u + 4] for u in range(0, len(consts), 4)]
                    def special_front(blks=special, h=h, i=i, hb=hb, kvb=kvb, tbb=tbb):
                        bks = [bank[6], bank[7]]
                        pbs = [pb8[6], pb8[7]]

                        def f_qk_s(e):
                            for (s_, j, kd) in blks:
                                for c in range(2):
                                    ins = e.matmul(pbs[c][:, kd * 128:(kd + 1) * 128],
                                                   kTt[hb][c * 64:(c + 1) * 64, s_, j * 128:(j + 1) * 128],
                                                   qT[c * 64:(c + 1) * 64, h, i * 128:(i + 1) * 128],
                                                   start=True, stop=True)
                            return ins
                        P.op("pe", f_qk_s, reads=[kvb, buf("qT")], writes=bks)
                        nsp = len(blks)

                        def f_sadd(e):
                            for c in range(2):
                                ins = e.scalar_tensor_tensor(
                                    tmpA[c][:, 0:nsp * 128], pbs[c][:, 0:nsp * 128], 0.125, Tb[hb][:, 0:nsp * 128],
                                    ALU.mult, ALU.add)
                            return ins
                        P.op("dve", f_sadd, reads=bks + [tbb], writes=[buf("tmpA")])
                    special_front()

                    plist = [("c", u_) for u_ in cu] + [("s", special)]
                    for pn, (kind, blks) in enumerate(plist):
                        first_pair = (pn == 0)
                        last_pair = (pn == len(plist) - 1)
                        if kind == "s":
                            sps = sp_ctr[0] % 2
                            sp_ctr[0] += 1

                            nsp = len(blks)
                            tsb = buf("tmpA")
                            psb = buf("pTs%d" % (2 * sps))
                            ps2 = av(36864 + sps * 768, 768, "p (c n) -> p c n", c=2)
                            P.op("act", lambda e, nsp=nsp, ps2=ps2: e.activation(
                                out=ps2[:, :, 0:nsp * 128],
                                in_=f32m[:, 768:1536].rearrange("p (c n) -> p c n", c=2)[:, :, 0:nsp * 128], func=AF.Exp),
                                reads=[tsb], writes=[psb])
                            srcs = [(pTs[2 * sps], psb), (pTs[2 * sps + 1], psb)]
                            cols = [kd for (_, _, kd) in blks]
                            kbl = [(s_, j) for (s_, j, _) in blks]
                        else:
                            cps = cn_ctr[0] % 2
                            pps = cn_ctr[0] % 3
                            cn_ctr[0] += 1
                            bks = [bank[2 * cps], bank[2 * cps + 1]]
                            pbs = [pb[2 * cps], pb[2 * cps + 1]]

                            def f_qk_c(e, blks=blks, h=h, i=i, hb=hb, pbs=pbs):
                                for n, (s_, j) in enumerate(blks):
                                    for c in range(2):
                                        ins = e.matmul(pbs[c][:, n * 128:(n + 1) * 128],
                                                       kTt[hb][c * 64:(c + 1) * 64, s_, j * 128:(j + 1) * 128],
                                                       qT[c * 64:(c + 1) * 64, h, i * 128:(i + 1) * 128],
                                                       start=True, stop=True)
                                return ins
                            P.op("pe", f_qk_c, reads=[kvb, buf("qT")], writes=bks)
                            nb_ = len(blks)
                            ptb = buf("pT%d" % (2 * pps))
                            pt2 = av(33792 + pps * 1024, 1024, "p (c n) -> p c n", c=2)
                            P.op("act", lambda e, cps=cps, nb_=nb_, h=h, pt2=pt2: e.activation(
                                out=pt2[:, :, 0:nb_ * 128],
                                in_=pq[cps][:, :].rearrange("p (c n) -> p c n", c=2)[:, :, 0:nb_ * 128], func=AF.Exp,
                                scale=0.125), reads=bks, writes=[ptb])
                            srcs = [(pT[2 * pps], ptb), (pT[2 * pps + 1], ptb)]
                            cols = list(range(nb_))
                            kbl = list(blks)

                        def mk_pv(kbl=kbl, cols=cols, srcs=srcs, hb=hb, ob=ob, obank=obank, kvb=kvb,
                                  first_pair=first_pair, last_pair=last_pair, h=h, i=i):
                            def f_pv(e):
                                for c in range(2):
                                    for n, (s_, j) in enumerate(kbl):
                                        ins = e.matmul(pb[4 + ob][:, c * 256:c * 256 + 129],
                                                       srcs[c][0][:, cols[n] * 128:(cols[n] + 1) * 128],
                                                       vt[hb][:, s_, j, 0:129],
                                                       start=(first_pair and c == 0 and n == 0),
                                                       stop=(last_pair and n == len(kbl) - 1),
                                                       skip_group_check=True)
                                return ins

                            def go():
                                P.op("pe", f_pv, reads=[srcs[0][1], srcs[1][1], kvb], writes=[obank])
                                if last_pair:
                                    emit_epilogue(h, i, ob, hb)
                            return go
                        pend.append(mk_pv())
                        flush(LOOK)
            flush(0)
            drain_epilogue(0, 0)
            drain_epilogue(0, 0)

            load_lnp(2)
            wob = buf("wo")
            if os.environ.get('K_BAR'):
                P.wait("pool", ("c_pe", P.cnt["c_pe"]))
            P.dma("pool", wo[:, :, :], wo_d.rearrange("(a p) f -> p a f", p=128), "ld_wo", writes=[wob])
            woT = []

            def wo_front(i):
                bp = 2 * (i % 2)

                def f_wo(e):
                    for half in range(2):
                        for a_ in range(8):
                            ins = e.matmul(pb[bp + half][:, :], xT[:, a_, i * 128:(i + 1) * 128],
                                           wo[:, a_, half * 512:(half + 1) * 512], start=(a_ == 0), stop=(a_ == 7))
                    return ins
                P.op("pe", f_wo, reads=[buf("xT%d" % i), wob], writes=[bank[bp], bank[bp + 1]])

            wo_front(0)
            for i in range(NB):
                rb = buf("resid%d" % i)
                if i + 1 < NB:
                    wo_front(i + 1)
                bp = 2 * (i % 2)
                zt = zbs[i % 2]
                zbb = buf("zb%d" % (i % 2))

                def f_zo(e, i=i, zt=zt, bp=bp):
                    e.scalar_tensor_tensor(zt[:, 0:512], resid[:, i, 0:512], ALPHA, pb[bp][:, :], ALU.mult, ALU.add)
                    return e.scalar_tensor_tensor(zt[:, 512:1024], resid[:, i, 512:1024], ALPHA, pb[bp + 1][:, :],
                                                  ALU.mult, ALU.add)
                P.op("dve", f_zo, reads=[bank[bp], bank[bp + 1], rb], writes=[zbb])
                woT.append(emit_ln(zt, [zbb], i, want_T=True, us=i % 2))
                while len(woT) > 1:
                    woT.pop(0)()
            while woT:
                woT.pop(0)()
            emit_ffn(1, 3, last=True)
            P.wait("sp", ("st_out", P.cnt["st_out"]))
        else:
            for nm in ("st_kst0", "st_kst1", "st_vst0", "st_vst1", "st_x1", "st_qT"):
                P.wait("sp", (nm, P.cnt[nm]))

        P.emit()
    return nc


def _rel_bucket_np(n):
    n = np.maximum(n, 0)
    nf = np.maximum(n, 1).astype(np.float32)
    large = 16 + (np.log(nf / np.float32(16)) / np.float32(math.log(8.0)) * np.float32(16)).astype(np.int32)
    large = np.minimum(large, 31)
    return np.where(n < 16, n, large)


def _amat(rank):
    A = np.zeros((128, 12, 128), np.float32)
    s = np.arange(128)[:, None]
    t = np.arange(128)[None, :]
    for g, w in enumerate(POOL_WINDOWS):
        band = ((t - s) >= 0) & ((t - s) < w)
        eye = (s == t).astype(np.float32)
        diag = band.astype(np.float32) / w - eye
        cnt = np.minimum(t + 1, w).astype(np.float32)
        first = band.astype(np.float32) / cnt - eye
        A[:, 4 + g, :] = diag
        A[:, g, :] = first if rank == 0 else diag
        sh = np.arange(16)[:, None] - 16
        bandh = ((t - sh) >= 0) & ((t - sh) < w)
        A[0:16, 8 + g, :] = bandh.astype(np.float32) / w
    return A.astype(ml_dtypes.bfloat16)


def _bias_idx(rank):
    k = np.arange(128)[:, None]
    q = np.arange(128)[None, :]
    idx = np.zeros((128, 3, 128), np.int64)
    msk = np.zeros((128, 3, 128), np.float32)
    for tdx, delta in enumerate((rank, rank - 1, rank + 1)):
        rel = delta * 128 + q - k
        idx[:, tdx, :] = _rel_bucket_np(rel)
        msk[:, tdx, :] = np.where(rel >= 0, 0.0, NEG)
    return idx, msk


_NC_CACHE = {}


def _get_nc(mode):
    if mode not in _NC_CACHE:
        _NC_CACHE[mode] = build(mode)
    return _NC_CACHE[mode]


FUSED = True
NOCC = False


def kernel(x, pool_w, pool_scale, w_qkv, w_o, lam_p, subln_g, rel_table, w_gate, w_up, w_down,
           ln_mix_g, ln_mix_b, ln_ffn_g, ln_ffn_b):
    f32 = lambda a: np.ascontiguousarray(np.asarray(a, dtype=np.float32))
    x = f32(x)
    Bn, S, _ = x.shape
    lnp = np.stack([f32(ln_mix_g)[0], f32(ln_mix_b)[0], f32(ln_ffn_g)[0], f32(ln_ffn_b)[0],
                    f32(ln_mix_g)[1], f32(ln_mix_b)[1], f32(ln_ffn_g)[1], f32(ln_ffn_b)[1]], 0)
    ident = np.eye(128, dtype=np.float32).astype(ml_dtypes.bfloat16)
    rel_table = f32(rel_table)
    common = {
        "ident": ident, "identf": np.eye(128, dtype=np.float32), "lnp": lnp,
        "w_gate": f32(w_gate), "w_up": f32(w_up), "w_down": f32(w_down),
    }
    l0 = {"pool_w": f32(pool_w)[0], "pool_scale": f32(pool_scale), "w_qkv": f32(w_qkv)[0]}
    l1 = {"w_o": f32(w_o)[0], "lam_p": f32(lam_p).reshape(1, 256), "subln_g": f32(subln_g).reshape(1, 128),
          "rel_table": rel_table}
    per_core = []
    for core in range(N_CORES):
        b, r = core // 2, core % 2
        xb = x[b].reshape(32, 128, D)
        own = xb[r::2]
        halo = np.zeros((NB, 16, D), np.float32)
        for i in range(NB):
            g = 2 * i + r
            if g > 0:
                halo[i] = xb[g - 1][112:128]
        idx, msk = _bias_idx(r)
        tb_g = np.ascontiguousarray(rel_table[idx].transpose(0, 3, 1, 2))
        per_core.append({"x_own": np.ascontiguousarray(own.reshape(T, D)), "x_halo": halo, "amat": _amat(r),
                         "tb_g": tb_g, "tb_m": msk})

    def assemble(outs):
        y = np.zeros((Bn, 32, 128, D), np.float32)
        for core in range(N_CORES):
            b, r = core // 2, core % 2
            y[b, r::2] = outs[core].reshape(NB, 128, D)
        return y.reshape(Bn, S, D)

    if FUSED:
        nc = _get_nc("fused")
        maps = []
        for core in range(N_CORES):
            m = dict(common); m.update(l0); m.update(l1); m.update(per_core[core])
            maps.append(m)
        res = run_bass_kernel_spmd(nc, maps, core_ids=list(range(N_CORES)))
        return assemble([res.results[c]["out"] for c in range(N_CORES)])

    ncA = _get_nc("A")
    mapsA = []
    for core in range(N_CORES):
        m = dict(common); m.update(l0)
        for kk in ("x_own", "x_halo", "amat"):
            m[kk] = per_core[core][kk]
        mapsA.append(m)
    resA = run_bass_kernel_spmd(ncA, mapsA, core_ids=list(range(N_CORES))).results
    ncB = _get_nc("B")
    mapsB = []
    for core in range(N_CORES):
        b = core // 2
        m = dict(common); m.update(l1)
        m["tb_g"] = per_core[core]["tb_g"]; m["tb_m"] = per_core[core]["tb_m"]
        m["x1"] = resA[core]["x1"]; m["qT"] = resA[core]["qT"]
        m["kT_all"] = np.stack([resA[2 * b]["kT_own"].reshape(NH, 128, T),
                                resA[2 * b + 1]["kT_own"].reshape(NH, 128, T)], 1).reshape(2 * NH * 128, T)
        m["v_all"] = np.stack([resA[2 * b]["v_own"].reshape(NH, 128, NB * VW),
                               resA[2 * b + 1]["v_own"].reshape(NH, 128, NB * VW)], 1).reshape(2 * NH * 128, NB * VW)
        mapsB.append(m)
    resB = run_bass_kernel_spmd(ncB, mapsB, core_ids=list(range(N_CORES))).results
    return assemble([resB[c]["out"] for c in range(N_CORES)])
```

```python
import math
import os
from contextlib import ExitStack

import numpy as np
import ml_dtypes

import concourse.bass as bass
import concourse.mybir as mybir
from concourse.bass_utils import run_bass_kernel_spmd

F32 = mybir.dt.float32
BF16 = mybir.dt.bfloat16
AF = mybir.ActivationFunctionType
ALU = mybir.AluOpType
AX = mybir.AxisListType

D = 1024
DFF = 2816
NB = 16
T = NB * 128
NH = 8
VW = 144
ALPHA = 4.0 ** 0.25
EPS = 1e-5
LAMBDA_INIT = 0.8 - 0.6 * math.exp(-0.3 * 1)
NEG = -30000.0
POOL_WINDOWS = (2, 4, 8, 16)
N_CORES = 8


class Buf:
    __slots__ = ("name", "w", "r", "rng")

    def __init__(self, name, rng=None):
        self.name = name
        self.w = None
        self.r = {}
        self.rng = rng


class Prog:
    ENG = ("pe", "act", "dve", "pool", "sp")

    def __init__(self, nc, es):
        self.nc = nc
        self.es = es
        self.q = {e: [] for e in self.ENG}
        self.sem = {}
        self.cnt = {}
        self.waited = {e: {} for e in self.ENG}
        self.ranged = []
        for e in ("pe", "act", "dve", "pool"):
            self.newsem("c_" + e)

    def newsem(self, name):
        self.sem[name] = self.es.enter_context(self.nc.semaphore(name))
        self.cnt[name] = 0

    def wait(self, eng, tok):
        if tok is None:
            return
        s, v = tok
        if self.waited[eng].get(s, 0) >= v:
            return
        self.waited[eng][s] = v
        h = self.sem[s]
        self.q[eng].append(lambda e, h=h, v=v: e.wait_ge(h, v))

    def _deps(self, eng, reads, writes):
        for b in reads:
            self.wait(eng, b.w)
        for b in writes:
            self.wait(eng, b.w)
            for t in b.r.values():
                self.wait(eng, t)
            if b.rng is not None:
                for y in self.ranged:
                    if y is not b and y.rng[0] < b.rng[1] and b.rng[0] < y.rng[1]:
                        self.wait(eng, y.w)
                        for t in y.r.values():
                            self.wait(eng, t)

    def _mark(self, eng, tok, reads, writes):
        for b in reads:
            b.r[eng] = tok
        for b in writes:
            b.w = tok
            b.r = {}

    def op(self, eng, fn, reads=(), writes=()):
        self._deps(eng, reads, writes)
        s = "c_" + eng
        self.cnt[s] += 1
        tok = (s, self.cnt[s])
        h = self.sem[s]
        self.q[eng].append(lambda e, fn=fn, h=h: fn(e).then_inc(h, 1))
        self._mark(eng, tok, reads, writes)
        return tok

    def dma(self, eng, out, in_, sem, reads=(), writes=()):
        if sem not in self.sem:
            self.newsem(sem)
        self._deps(eng, reads, writes)
        self.cnt[sem] += 16
        tok = (sem, self.cnt[sem])
        h = self.sem[sem]
        self.q[eng].append(lambda e, out=out, in_=in_, h=h: e.dma_start(out=out, in_=in_).then_inc(h, 16))
        self._mark("dma_" + sem, tok, reads, writes)
        return tok

    def emit(self):
        nc = self.nc
        with nc.Block() as block:
            @block.tensor
            def _(e):
                for f in self.q["pe"]:
                    f(e)

            @block.scalar
            def _(e):
                for f in self.q["act"]:
                    f(e)

            @block.vector
            def _(e):
                for f in self.q["dve"]:
                    f(e)

            @block.gpsimd
            def _(e):
                for f in self.q["pool"]:
                    f(e)

            @block.sync
            def _(e):
                for f in self.q["sp"]:
                    f(e)


def build(mode):
    nc = bass.Bass("TRN2", target_bir_lowering=False)
    do_l0 = mode in ("fused", "A")
    do_att = mode in ("fused", "B")

    def din(name, shape, dt=F32):
        return nc.dram_tensor(name, list(shape), dt, kind="ExternalInput")

    def dout(name, shape, dt=F32):
        return nc.dram_tensor(name, list(shape), dt, kind="ExternalOutput")

    ident_d = din("ident", [128, 128], BF16)
    identf_d = din("identf", [128, 128], F32)
    lnp_d = din("lnp", [8, D])
    if do_l0:
        x_d = din("x_own", [T, D])
        halo_d = din("x_halo", [NB, 16, D])
        amat_d = din("amat", [128, 12, 128], BF16)
        poolw_d = din("pool_w", [4, 256, 256])
        pscale_d = din("pool_scale", [1, D])
        wqkv_d = din("w_qkv", [D, 3 * D])
    wg_d = din("w_gate", [2, D, DFF])
    wu_d = din("w_up", [2, D, DFF])
    wdn_d = din("w_down", [2, DFF, D])
    if do_att:
        wo_d = din("w_o", [D, D])
        lam_d = din("lam_p", [1, 256])
        sg_d = din("subln_g", [1, 128])
        rel_d = din("rel_table", [32, NH])
        tbg_d = din("tb_g", [128, NH, 3, 128])
        tbm_d = din("tb_m", [128, 3, 128])
        out_d = dout("out", [T, D])

    if mode == "fused":
        kT_own = nc.dram_tensor("kT_own", [NH * 128, T], BF16)
        v_own = nc.dram_tensor("v_own", [NH * 128, NB * VW], BF16)
        kT_all = nc.dram_tensor("kT_all", [2 * NH * 128, T], BF16)
        v_all = nc.dram_tensor("v_all", [2 * NH * 128, NB * VW], BF16)
    elif mode == "A":
        kT_own = dout("kT_own", [NH * 128, T], BF16)
        v_own = dout("v_own", [NH * 128, NB * VW], BF16)
        x1_d = dout("x1", [T, D])
        qT_d = dout("qT", [128, NH, T], BF16)
    else:
        kT_all = din("kT_all", [2 * NH * 128, T], BF16)
        v_all = din("v_all", [2 * NH * 128, NB * VW], BF16)
        x1_d = din("x1", [T, D])
        qT_d = din("qT", [128, NH, T], BF16)

    with ExitStack() as es:
        def sb(name, shape, dt):
            return es.enter_context(nc.sbuf_tensor(name, list(shape), dt))

        def ps(name, shape, dt):
            return es.enter_context(nc.psum_tensor(name, list(shape), dt))

        resid = sb("resid", [128, NB, D], F32)
        xT = sb("xT", [128, 8, T], BF16)
        lnp_t = sb("lnp_t", [128, 2, D], F32)
        zb = sb("zb", [128, D], F32)
        identf = sb("identf_s", [128, 128], F32)
        sg = sb("sg", [128, 2, 512], BF16)
        f32m = sb("f32m", [128, 1536], F32)
        tmask = sb("tmask", [128, 384], F32)
        osb = sb("osb", [128, 2, 128], F32)
        sgb = sb("sgb", [128, 128], F32)
        ident = sb("ident_s", [128, 128], BF16)
        st = sb("st", [128, 2, 2, 6], F32)
        mv = sb("mv", [128, 2, 2], F32)
        rs = sb("rs", [128, 2, 2], F32)
        sm = sb("sm", [128, 64], F32)
        cb = sb("cb", [128, NH], F32)
        lamt = sb("lamt", [128, 256], F32)
        AR = 43008
        arena = sb("arena", [128, AR], BF16)

        def av(lo, n, pat=None, **kw):
            a = arena[:, lo:lo + n]
            return a.rearrange(pat, **kw) if pat else a

        hT = av(0, 11264, "p (a b) -> p a b", a=11)
        wd = av(11264, 11264, "p (a b) -> p a b", a=11)
        gu = [av(22528 + k * 4096, 4096, "p (g c f) -> p g c f", g=2, c=8) for k in range(3)]
        halo = av(0, 4096, "p (a b) -> p a b", a=4)
        xin_bf = av(4096, 3072, "p (a b) -> p a b", a=3)
        pmT = av(7168, 3072, "p (a c t) -> p a c t", a=3, c=8)
        amat = av(34816, 1536, "p (a b) -> p a b", a=12)
        wp = av(36352, 2048, "p (g k d) -> p g k d", g=4, k=2)
        ps_bc = f32m[:, 0:1024]
        wq = [av(22528 + k * 4096, 4096, "p (c f) -> p c f", c=8) for k in range(3)]
        kst = [av(34816 + k * 2048, 2048) for k in range(2)]
        vst = [av(38912 + k * 1152, 1152, "p (h v) -> p h v", h=NH) for k in range(2)]
        qT = av(0, 16384, "p (h t) -> p h t", h=NH)
        kTt = [av(16384 + k * 8704, 4096, "p (s t) -> p s t", s=2) for k in range(2)]
        vt = [av(16384 + k * 8704 + 4096, 4608, "p (s j v) -> p s j v", s=2, j=NB) for k in range(2)]
        pT = [av(33792 + k * 512, 512) for k in range(6)]
        pTs = [av(36864 + k * 384, 384) for k in range(4)]
        onb = [av(40448 + k * 128, 128) for k in range(2)]
        wo = av(22528, 8192, "p (a f) -> p a f", a=8)
        Tb = [f32m[:, k * 384:(k + 1) * 384] for k in range(2)]
        tmpA = [f32m[:, 768 + k * 384:768 + (k + 1) * 384] for k in range(2)]

        pq = [ps("pq%d" % k, [128, 1024], F32) for k in range(4)]
        pb8 = [pq[k // 2][:, (k % 2) * 512:(k % 2 + 1) * 512] for k in range(8)]
        pb = pb8[:6]

        P = Prog(nc, es)
        B = {}

        RNG = {"hT0": (0, 11264), "hT1": (0, 11264), "wd": (11264, 22528),
               "gu0": (22528, 26624), "gu1": (26624, 30720), "gu2": (30720, 34816),
               "halo0": (0, 4096), "halo1": (0, 4096), "halo2": (0, 4096), "halo3": (0, 4096),
               "xin0": (4096, 5120), "xin1": (5120, 6144), "xin2": (6144, 7168),
               "pmT0": (7168, 8192), "pmT1": (8192, 9216), "pmT2": (9216, 10240),
               "amat": (34816, 36352), "wp": (36352, 38400),
               "kst0": (34816, 36864), "kst1": (36864, 38912), "vst0": (38912, 40064), "vst1": (40064, 41216),
               "qT": (0, 16384), "kv0": (16384, 25088), "kv1": (25088, 33792),
               "pT0": (33792, 34816), "pT2": (34816, 35840), "pT4": (35840, 36864),
               "pTs0": (36864, 37632), "pTs2": (37632, 38400),
               "onb0": (40448, 40576), "onb1": (40576, 40704),
               "wo": (22528, 30720), "zb1": (38400, 40448)}

        def buf(name):
            if name not in B:
                B[name] = Buf(name, RNG.get(name))
                if name in RNG:
                    P.ranged.append(B[name])
            return B[name]

        bank = [buf("bank%d" % k) for k in range(8)]

        P.dma("sp", ident[:], ident_d[:, :], "ld_ident", writes=[buf("ident")])
        P.dma("sp", identf[:], identf_d[:, :], "ld_identf", writes=[buf("identf")])
        zbs = [zb[:], arena[:, 38400:40448].bitcast(F32)]

        def load_lnp(idx):
            P.dma("sp", lnp_t[:, 0, :], lnp_d[2 * idx:2 * idx + 1, :].partition_broadcast(128), "ld_lnp",
                  writes=[buf("lnp")])
            P.dma("sp", lnp_t[:, 1, :], lnp_d[2 * idx + 1:2 * idx + 2, :].partition_broadcast(128), "ld_lnp",
                  writes=[])
            B["lnp"].w = ("ld_lnp", P.cnt["ld_lnp"])

        ln_ctr = [0]

        def emit_ln(zsrc, zbuf_list, blk, want_T, us, evac="act"):
            k = ln_ctr[0] % 2
            ln_ctr[0] += 1
            bst, bmv, brs = buf("st%d" % k), buf("mv%d" % k), buf("rs%d" % k)
            zt = zbs[us]
            zbb = buf("zb%d" % us)

            def f_stats(e, k=k, zsrc=zsrc):
                e.bn_stats(st[:, k, 0, :], zsrc[:, 0:512])
                return e.bn_stats(st[:, k, 1, :], zsrc[:, 512:1024])
            P.op("dve", f_stats, reads=zbuf_list, writes=[bst])
            P.op("dve", lambda e, k=k: e.bn_aggr(mv[:, k, :], st[:, k, :, :]), reads=[bst], writes=[bmv])
            P.op("act", lambda e, k=k: e.activation(out=rs[:, k, 0:1], in_=mv[:, k, 1:2], func=AF.Sqrt,
                                                    bias=EPS, scale=1.0),
                 reads=[bmv], writes=[brs])
            P.op("dve", lambda e, k=k, zsrc=zsrc, zt=zt: e.scalar_tensor_tensor(
                zt, zsrc, mv[:, k, 0:1], lnp_t[:, 0, :], ALU.subtract, ALU.mult),
                reads=zbuf_list + [bmv, buf("lnp")], writes=[zbb])
            P.op("dve", lambda e, k=k: e.reciprocal(rs[:, k, 1:2], rs[:, k, 0:1]), reads=[brs], writes=[brs])
            rb = buf("resid%d" % blk)
            P.op("dve", lambda e, k=k, blk=blk, zt=zt: e.scalar_tensor_tensor(
                resid[:, blk, :], zt, rs[:, k, 1:2], lnp_t[:, 1, :], ALU.mult, ALU.add),
                reads=[zbb, brs, buf("lnp")], writes=[rb])
            if not want_T:
                return None

            def do_T(blk=blk, rb=rb):
                def f_tp(e):
                    for c in range(8):
                        ins = e.transpose(pb8[6 + c // 4][:, (c % 4) * 128:(c % 4 + 1) * 128],
                                          resid[:, blk, c * 128:(c + 1) * 128], identf[:])
                    return ins
                P.op("pe", f_tp, reads=[rb, buf("identf")], writes=[bank[6], bank[7]])
                xb_ = buf("xT%d" % blk)
                if evac == "act":
                    P.op("act", lambda e: e.copy(xT[:, 0:4, blk * 128:(blk + 1) * 128],
                                                 pb8[6][:, :].rearrange("p (c t) -> p c t", c=4)),
                         reads=[bank[6]], writes=[xb_])
                    P.op("act", lambda e: e.copy(xT[:, 4:8, blk * 128:(blk + 1) * 128],
                                                 pb8[7][:, :].rearrange("p (c t) -> p c t", c=4)),
                         reads=[bank[7]], writes=[xb_])
                else:
                    P.op("dve", lambda e: e.tensor_copy(xT[:, 0:4, blk * 128:(blk + 1) * 128],
                                                        pb8[6][:, :].rearrange("p (c t) -> p c t", c=4)),
                         reads=[bank[6]], writes=[xb_])
                    P.op("dve", lambda e: e.tensor_copy(xT[:, 4:8, blk * 128:(blk + 1) * 128],
                                                        pb8[7][:, :].rearrange("p (c t) -> p c t", c=4)),
                         reads=[bank[7]], writes=[xb_])
            return do_T

        def emit_ffn(L, lnp_idx, last):
            load_lnp(lnp_idx)
            wgv = wg_d[L].rearrange("(c p) f -> p c f", p=128)
            wuv = wu_d[L].rearrange("(c p) f -> p c f", p=128)
            wdv = wdn_d[L].rearrange("(a p) d -> p a d", p=128)
            gctr = 0
            defer_T = []
            for tt in range(2):
                for part in range(2):
                    groups = [(0, 2), (2, 2), (4, 2), (6, 2), (8, 2), (10, 1)]
                    gtok = {}

                    def load_gu(gi, part=part):
                        j0_, nj_ = groups[gi]
                        gb_ = (gctr + gi) % 3
                        f0 = (part * 11 + j0_) * 128
                        w = nj_ * 128
                        gbuf_ = buf("gu%d" % gb_)
                        P.dma("pool", gu[gb_][:, 0, :, 0:w], wgv[:, :, f0:f0 + w], "ld_gu%d" % gb_, writes=[gbuf_])
                        P.dma("pool", gu[gb_][:, 1, :, 0:w], wuv[:, :, f0:f0 + w], "ld_gu%d" % gb_, writes=[])
                        gbuf_.w = ("ld_gu%d" % gb_, P.cnt["ld_gu%d" % gb_])
                    for gi in range(3):
                        load_gu(gi)
                    P.dma("pool", wd[:, :, :], wdv[:, part * 11:(part + 1) * 11, :], "ld_wd", writes=[buf("wd")])
                    for gi, (j0, nj) in enumerate(groups):
                        gb = (gctr + gi) % 3
                        gbuf = buf("gu%d" % gb)
                        if gi >= 3:
                            load_gu(gi)
                        for j in range(nj):
                            jl = j0 + j
                            for ts in range(2):
                                k = (jl * 2 + ts) % 2
                                t0 = tt * 1024 + ts * 512
                                xbufs = [buf("xT%d" % (t0 // 128 + q)) for q in range(4)]

                                def f_gu(e, gb=gb, j=j, t0=t0, k=k):
                                    for c in range(8):
                                        e.matmul(pb[k][:, :], gu[gb][:, 0, c, j * 128:(j + 1) * 128],
                                                 xT[:, c, t0:t0 + 512], start=(c == 0), stop=(c == 7))
                                    for c in range(8):
                                        ins = e.matmul(pb[2 + k][:, :], gu[gb][:, 1, c, j * 128:(j + 1) * 128],
                                                       xT[:, c, t0:t0 + 512], start=(c == 0), stop=(c == 7))
                                    return ins
                                P.op("pe", f_gu, reads=[gbuf] + xbufs, writes=[bank[k], bank[2 + k]])
                                P.op("act", lambda e, k=k: e.activation(out=sg[:, k, :], in_=pb[k][:, :], func=AF.Silu),
                                     reads=[bank[k]], writes=[buf("sg%d" % k)])
                                P.op("dve", lambda e, k=k, jl=jl, ts=ts: e.tensor_tensor(
                                    hT[:, jl, ts * 512:(ts + 1) * 512], sg[:, k, :], pb[2 + k][:, :], ALU.mult),
                                    reads=[buf("sg%d" % k), bank[2 + k]], writes=[buf("hT%d" % ts)])
                        for _ in range(2):
                            if defer_T:
                                defer_T.pop(0)()
                    gctr += len(groups)
                    for b8 in range(8):
                        blk = tt * 8 + b8
                        ts = b8 // 4
                        rb = buf("resid%d" % blk)
                        for dh in range(2):
                            k = (b8 * 2 + dh) % 2

                            def f_dn(e, b8=b8, dh=dh, k=k):
                                for j in range(11):
                                    ins = e.matmul(pb[4 + k][:, :], hT[:, j, b8 * 128:(b8 + 1) * 128],
                                                   wd[:, j, dh * 512:(dh + 1) * 512], start=(j == 0), stop=(j == 10))
                                return ins
                            P.op("pe", f_dn, reads=[buf("hT%d" % ts), buf("wd")], writes=[bank[4 + k]])
                            if part == 0:
                                P.op("dve", lambda e, blk=blk, dh=dh, k=k: e.scalar_tensor_tensor(
                                    resid[:, blk, dh * 512:(dh + 1) * 512], resid[:, blk, dh * 512:(dh + 1) * 512],
                                    ALPHA, pb[4 + k][:, :], ALU.mult, ALU.add),
                                    reads=[bank[4 + k]], writes=[rb])
                            else:
                                P.op("dve", lambda e, blk=blk, dh=dh, k=k: e.tensor_tensor(
                                    resid[:, blk, dh * 512:(dh + 1) * 512], resid[:, blk, dh * 512:(dh + 1) * 512],
                                    pb[4 + k][:, :], ALU.add),
                                    reads=[bank[4 + k]], writes=[rb])
                        if part == 1:
                            dT = emit_ln(resid[:, blk, :], [rb], blk, want_T=not last, us=blk % 2)
                            if dT is not None:
                                defer_T.append(dT)
                            if last:
                                P.dma("sp", out_d[blk * 128:(blk + 1) * 128, :], resid[:, blk, :], "st_out", reads=[rb])
            while defer_T:
                defer_T.pop(0)()

        if do_l0:
            xsplit = [(0, 1), (1, 2), (2, 4), (4, 8), (8, 12), (12, 16)]
            for xi, (b0, b1) in enumerate(xsplit[:2]):
                P.dma("sp", resid[:, b0:b1, :], x_d[b0 * 128:b1 * 128, :].rearrange("(b p) d -> p b d", p=128),
                      "ld_x%d" % xi, writes=[buf("resid%d" % q) for q in range(b0, b1)])
            P.dma("sp", amat[:, :, :], amat_d[:, :, :], "ld_amat", writes=[buf("amat")])
            P.dma("sp", ps_bc, pscale_d[0:1, :].partition_broadcast(128), "ld_psbc", writes=[buf("f32m")])
            load_lnp(0)
            for xi, (b0, b1) in list(enumerate(xsplit))[2:]:
                P.dma("sp", resid[:, b0:b1, :], x_d[b0 * 128:b1 * 128, :].rearrange("(b p) d -> p b d", p=128),
                      "ld_x%d" % xi, writes=[buf("resid%d" % q) for q in range(b0, b1)])
            P.dma("pool", wp[:, :, :, :], poolw_d.rearrange("g (k p) d -> p g k d", p=128), "ld_wp", writes=[buf("wp")])

            def f_wps(e):
                for g in range(4):
                    for kc in range(2):
                        ins = e.tensor_tensor(wp[:, g, kc, :], wp[:, g, kc, :], ps_bc[:, g * 256:(g + 1) * 256], ALU.mult)
                return ins
            P.op("dve", f_wps, reads=[buf("f32m")], writes=[buf("wp")])

            mixT = []

            def mix_front_a(i):
                rb = buf("resid%d" % i)
                hs = i % 4
                hb = buf("halo%d" % hs)
                P.dma("pool", halo[0:16, hs, :], halo_d[i, :, :], "ld_halo%d" % hs, writes=[hb])
                k2 = i % 3
                xb = buf("xin%d" % k2)
                P.op("act", lambda e: e.copy(xin_bf[:, k2, :], resid[:, i, :]), reads=[rb], writes=[xb])
                a0 = 0 if i == 0 else 4

                def f_pm(e):
                    for c in range(8):
                        g = c // 2
                        o = pb[c // 4][:, (c % 4) * 128:(c % 4 + 1) * 128]
                        e.matmul(o, xin_bf[:, k2, c * 128:(c + 1) * 128], amat[:, a0 + g, :], start=True, stop=False)
                        ins = e.matmul(o[:, 0:16], halo[0:16, hs, c * 128:(c + 1) * 128], amat[0:16, 8 + g, 0:16],
                                       start=False, stop=True)
                    return ins
                P.op("pe", f_pm, reads=[xb, hb, buf("amat")], writes=[bank[0], bank[1]])
                pmb = buf("pmT%d" % k2)

                def f_pmT(e):
                    e.copy(pmT[:, k2, 0:4, :], pb[0][:, :].rearrange("p (c t) -> p c t", c=4))
                    return e.copy(pmT[:, k2, 4:8, :], pb[1][:, :].rearrange("p (c t) -> p c t", c=4))
                P.op("act", f_pmT, reads=[bank[0], bank[1]], writes=[pmb])

            def mix_front_b(i):
                k2 = i % 3
                pmb = buf("pmT%d" % k2)
                mb = 2 + 2 * (i % 2)

                def f_mix(e):
                    for g in range(4):
                        for kc in range(2):
                            ins = e.matmul(pb[mb + g // 2][:, (g % 2) * 256:(g % 2 + 1) * 256],
                                           pmT[:, k2, 2 * g + kc, :], wp[:, g, kc, :], start=(kc == 0), stop=(kc == 1))
                    return ins
                P.op("pe", f_mix, reads=[pmb, buf("wp")], writes=[bank[mb], bank[mb + 1]])

            def mix_z(i):
                rb = buf("resid%d" % i)
                k2 = i % 2
                zt = zbs[k2]
                zbb = buf("zb%d" % k2)

                mb = 2 + 2 * (i % 2)

                def f_z1(e):
                    e.scalar_tensor_tensor(zt[:, 0:512], resid[:, i, 0:512], ALPHA, pb[mb][:, :], ALU.mult, ALU.add)
                    return e.scalar_tensor_tensor(zt[:, 512:1024], resid[:, i, 512:1024], ALPHA, pb[mb + 1][:, :],
                                                  ALU.mult, ALU.add)
                P.op("dve", f_z1, reads=[bank[mb], bank[mb + 1], rb], writes=[zbb])

            for j in range(2):
                mix_front_a(j)
                mix_front_b(j)
            for i in range(NB):
                mix_z(i)
                if i + 2 < NB:
                    mix_front_a(i + 2)
                    mix_front_b(i + 2)
                k2 = i % 2
                mixT.append(emit_ln(zbs[k2], [buf("zb%d" % k2)], i, want_T=True, us=k2, evac="dve"))
                while len(mixT) > 1:
                    mixT.pop(0)()
            while mixT:
                mixT.pop(0)()

            emit_ffn(0, 1, last=False)

            wqv = wqkv_d.rearrange("(c p) f -> p c f", p=128)
            allx = [buf("xT%d" % q) for q in range(NB)]
            for k in range(2):
                P.op("pool", lambda e, k=k: e.memset(vst[k][:, :, 128:VW], 0.0), writes=[buf("vst%d" % k)])
                P.op("pool", lambda e, k=k: e.memset(vst[k][:, :, 128:129], 1.0), writes=[buf("vst%d" % k)])
            bctr = 0
            for gidx, grp in enumerate((2, 3, 4, 5, 0, 1)):
                wbi = gidx % 3
                wb = buf("gu%d" % wbi)
                P.dma("pool", wq[wbi][:, :, :], wqv[:, :, grp * 512:(grp + 1) * 512], "ld_gu%d" % wbi, writes=[wb])
                if grp < 4:
                    for hh in range(4):
                        h = (grp % 2) * 4 + hh
                        if grp >= 2:
                            ks = h % 2
                            kb_ = buf("kst%d" % ks)
                        for ts in range(4):
                            k = bctr % 2
                            bctr += 1

                            def f_qk(e, wbi=wbi, hh=hh, ts=ts, k=k):
                                for c in range(8):
                                    ins = e.matmul(pb[k][:, :], wq[wbi][:, c, hh * 128:(hh + 1) * 128],
                                                   xT[:, c, ts * 512:(ts + 1) * 512], start=(c == 0), stop=(c == 7))
                                return ins
                            P.op("pe", f_qk, reads=[wb] + allx[ts * 4:ts * 4 + 4], writes=[bank[k]])
                            if grp < 2:
                                P.op("act", lambda e, h=h, ts=ts, k=k: e.copy(
                                    qT[:, h, ts * 512:(ts + 1) * 512], pb[k][:, :]),
                                    reads=[bank[k]], writes=[buf("qT")])
                            else:
                                P.op("act", lambda e, ks=ks, ts=ts, k=k: e.copy(
                                    kst[ks][:, ts * 512:(ts + 1) * 512], pb[k][:, :]),
                                    reads=[bank[k]], writes=[kb_])
                        if grp >= 2:
                            P.dma("sp", kT_own[h * 128:(h + 1) * 128, :], kst[ks][:, :], "st_kst%d" % ks, reads=[kb_])
                else:
                    half = grp - 4
                    for blk in range(NB):
                        k = bctr % 2
                        bctr += 1
                        vs = blk % 2
                        vb_ = buf("vst%d" % vs)

                        def f_v(e, wbi=wbi, blk=blk, k=k):
                            for c in range(8):
                                ins = e.matmul(pb[k][:, :], xT[:, c, blk * 128:(blk + 1) * 128], wq[wbi][:, c, :],
                                               start=(c == 0), stop=(c == 7))
                            return ins
                        P.op("pe", f_v, reads=[wb, allx[blk]], writes=[bank[k]])
                        P.op("dve", lambda e, vs=vs, half=half, k=k: e.tensor_copy(
                            vst[vs][:, half * 4:(half + 1) * 4, 0:128],
                            pb[k][:, :].rearrange("p (h v) -> p h v", h=4)),
                            reads=[bank[k]], writes=[vb_])
                        dst = v_own.ap().rearrange("(h k) (j v) -> k h j v", k=128, v=VW)[:, half * 4:(half + 1) * 4, blk, :]
                        P.dma("sp", dst, vst[vs][:, half * 4:(half + 1) * 4, :], "st_vst%d" % vs, reads=[vb_])
            if mode == "A":
                for blk in range(NB):
                    P.dma("sp", x1_d[blk * 128:(blk + 1) * 128, :], resid[:, blk, :], "st_x1",
                          reads=[buf("resid%d" % blk)])
                P.dma("sp", qT_d[:, :, :], qT[:, :, :], "st_qT", reads=[buf("qT")])

        kv_ready = None
        if mode == "fused":
            for nm in ("st_kst0", "st_kst1", "st_vst0", "st_vst1"):
                P.wait("pool", (nm, P.cnt[nm]))
            P.newsem("cc")
            groups = [[2 * b, 2 * b + 1] for b in range(N_CORES // 2)]
            hcc = P.sem["cc"]

            if not NOCC:
                for h in range(NH):
                    def f_k(e, h=h):
                        return e.collective_compute("AllGather", ALU.bypass, replica_groups=groups,
                                                    ins=[kT_own[h * 128:(h + 1) * 128, :]],
                                                    outs=[kT_all[h * 256:(h + 1) * 256, :]])

                    def f_v(e, h=h):
                        return e.collective_compute("AllGather", ALU.bypass, replica_groups=groups,
                                                    ins=[v_own[h * 128:(h + 1) * 128, :]],
                                                    outs=[v_all[h * 256:(h + 1) * 256, :]])
                    P.q["pool"].append(lambda e, f=f_k: f(e).then_inc(hcc, 1))
                    P.q["pool"].append(lambda e, f=f_v: f(e).then_inc(hcc, 1))
                kv_ready = True

        if do_att:
            if mode == "B":
                for q4 in range(4):
                    P.dma("sp", resid[:, q4 * 4:(q4 + 1) * 4, :],
                          x1_d[q4 * 512:(q4 + 1) * 512, :].rearrange("(b p) d -> p b d", p=128), "ld_x%d" % q4,
                          writes=[buf("resid%d" % (q4 * 4 + q)) for q in range(4)])
                P.dma("sp", qT[:, :, :], qT_d[:, :, :], "ld_qT", writes=[buf("qT")])
            P.dma("sp", cb[:, :], rel_d[31:32, :].partition_broadcast(128), "ld_cb", writes=[buf("cb")])
            P.dma("sp", lamt[:, :], lam_d[0:1, :].partition_broadcast(128), "ld_lam", writes=[buf("lamt")])
            P.dma("sp", sgb[:, :], sg_d[0:1, :].partition_broadcast(128), "ld_sgb", writes=[buf("sgb")])
            P.op("dve", lambda e: e.tensor_scalar(sgb[:, :], sgb[:, :], (1.0 - LAMBDA_INIT) * math.sqrt(128.0), None,
                                                  ALU.mult), reads=[buf("sgb")], writes=[buf("sgb")])
            lv = lamt[:, :].rearrange("p (a b d) -> p a b d", a=2, b=2)
            P.op("dve", lambda e: e.tensor_tensor(osb[:, 0, :].rearrange("p (a d) -> p a d", a=2),
                                                  lv[:, :, 0, :], lv[:, :, 1, :], ALU.mult),
                 reads=[buf("lamt")], writes=[buf("osb0")])
            P.op("dve", lambda e: e.tensor_reduce(sm[:, 0:2], osb[:, 0, :].rearrange("p (a d) -> p a d", a=2),
                                                  AX.X, ALU.add), reads=[buf("osb0")], writes=[buf("sm_lam")])
            P.op("act", lambda e: e.activation(out=sm[:, 2:4], in_=sm[:, 0:2], func=AF.Exp),
                 reads=[buf("sm_lam")], writes=[buf("sm_lam")])
            P.op("dve", lambda e: e.tensor_tensor(sm[:, 4:5], sm[:, 3:4], sm[:, 2:3], ALU.subtract),
                 reads=[buf("sm_lam")], writes=[buf("sm_lam")])
            P.op("dve", lambda e: e.tensor_scalar(sm[:, 4:5], sm[:, 4:5], -LAMBDA_INIT, None, ALU.add),
                 reads=[buf("sm_lam")], writes=[buf("sm_lam")])
            lamb = buf("sm_lam")
            P.op("dve", lambda e: e.memset(sm[:, 6:7], -0.5), writes=[buf("mhalf")])
            P.dma("sp", tmask[:, :].rearrange("p (a q) -> p a q", a=3), tbm_d[:, :, :], "ld_tm", writes=[buf("tmask")])

            kall = kT_all.ap().rearrange("(h s k) t -> k s h t", s=2, h=NH)
            vall = v_all.ap().rearrange("(h s k) (j v) -> k s h j v", s=2, h=NH, v=VW)
            LOOK = int(os.environ.get('K_LOOK', '2'))
            sp_ctr = [0]
            cn_ctr = [0]
            ep_ctr = [0]
            pend = []
            tpend = []

            b1pend = []
            b2pend = []

            def drain_epilogue(keep1, keep2):
                while len(b2pend) > keep2:
                    b2pend.pop(0)()
                while len(b1pend) > keep1:
                    b1pend.pop(0)()

            def emit_epilogue(h, i, ob, hb):
                obank = bank[4 + ob]
                ek = ep_ctr[0] % 2
                ep_ctr[0] += 1
                sb_ = buf("sm_e%d" % ek)
                o0 = 8 + ek * 8
                ov = pb[4 + ob][:, :].rearrange("p (c w) -> p c w", c=2)
                while len(b2pend) > 0:
                    b2pend.pop(0)()
                while len(b1pend) > 0:
                    b1pend.pop(0)()
                P.op("dve", lambda e: e.reciprocal(sm[:, o0:o0 + 2], ov[:, :, 128]), reads=[obank], writes=[sb_])
                P.op("dve", lambda e: e.tensor_tensor(sm[:, o0 + 2:o0 + 3], sm[:, o0 + 1:o0 + 2], sm[:, 4:5], ALU.mult),
                     reads=[sb_, lamb], writes=[sb_])
                osbb = buf("osb%d" % ek)
                P.op("dve", lambda e: e.tensor_scalar(osb[:, ek, :], pb[4 + ob][:, 0:128], sm[:, o0:o0 + 1], None, ALU.mult),
                     reads=[obank, sb_], writes=[osbb])
                P.op("dve", lambda e: e.scalar_tensor_tensor(
                    osb[:, ek, :], pb[4 + ob][:, 256:384], sm[:, o0 + 2:o0 + 3], osb[:, ek, :], ALU.mult, ALU.add),
                    reads=[obank, sb_], writes=[osbb])
                P.op("dve", lambda e: e.scalar_tensor_tensor(
                    sg[:, 0, 0:128], osb[:, ek, :], 1.0, osb[:, ek, :], ALU.mult, ALU.mult, accum_out=sm[:, o0 + 3:o0 + 4]),
                    reads=[osbb], writes=[sb_, buf("junk")])

                def stage_b1():
                    if POOL_POW:
                        P.op("pool", lambda e: e.tensor_scalar(sm[:, o0 + 4:o0 + 5], sm[:, o0 + 3:o0 + 4], 128.0 * EPS,
                                                               None, ALU.add), reads=[sb_], writes=[sb_])
                        P.op("pool", lambda e: e.tensor_tensor(sm[:, o0 + 5:o0 + 6], sm[:, o0 + 4:o0 + 5], sm[:, 6:7],
                                                               ALU.pow), reads=[sb_, buf("mhalf")], writes=[sb_])
                    else:
                        P.op("act", lambda e: e.activation(out=sm[:, o0 + 4:o0 + 5], in_=sm[:, o0 + 3:o0 + 4],
                                                           func=AF.Ln, bias=128.0 * EPS, scale=1.0),
                             reads=[sb_], writes=[sb_])
                        P.op("act", lambda e: e.activation(out=sm[:, o0 + 5:o0 + 6], in_=sm[:, o0 + 4:o0 + 5],
                                                           func=AF.Exp, scale=-0.5), reads=[sb_], writes=[sb_])
                    P.op("dve", lambda e: e.scalar_tensor_tensor(
                        onb[ek][:, :], osb[:, ek, :], sm[:, o0 + 5:o0 + 6], sgb[:, :], ALU.mult, ALU.mult),
                        reads=[osbb, sb_, buf("sgb")], writes=[buf("onb%d" % ek)])

                    def stage_b2():
                        tpv = pb[4 + ob][:, 448:512].bitcast(BF16)
                        P.op("pe", lambda e: e.transpose(tpv, onb[ek][:, :], ident[:]),
                             reads=[buf("onb%d" % ek), buf("ident")], writes=[obank])
                        P.op("dve", lambda e: e.tensor_copy(xT[:, h, i * 128:(i + 1) * 128], tpv),
                             reads=[obank], writes=[buf("xT%d" % i)])
                    b2pend.append(stage_b2)
                b1pend.append(stage_b1)

            def flush(upto):
                while len(pend) > upto:
                    pend.pop(0)()

            for h in range(NH):
                hb = h % 2
                kvb = buf("kv%d" % hb)
                tbb = buf("tb%d" % hb)
                if kv_ready is not None:
                    P.wait("sp", ("cc", 2 * h + 2))
                P.dma("sp", kTt[hb][:, :, :], kall[:, :, h, :], "ld_kv%d" % hb, writes=[kvb])
                P.dma("sp", vt[hb][:, :, :, :], vall[:, :, h, :, :], "ld_kv%d" % hb, writes=[])
                kvb.w = ("ld_kv%d" % hb, P.cnt["ld_kv%d" % hb])
                P.dma("sp", Tb[hb].rearrange("p (a q) -> p a q", a=3), tbg_d[:, h, :, :], "ld_tb%d" % hb, writes=[tbb])
                P.op("dve", lambda e, hb=hb, h=h: e.scalar_tensor_tensor(Tb[hb], Tb[hb], cb[:, h:h + 1], tmask[:, :],
                                                                         ALU.subtract, ALU.add),
                     reads=[buf("tmask"), buf("cb")], writes=[tbb])

                for i in range(NB):
                    ob = (h * NB + i) % 2
                    obank = bank[4 + ob]
                    special = [(0, i, 0), (1, i, 1)] + ([(1, i - 1, 2)] if i >= 1 else [])
                    consts = [(0, j) for j in range(i)] + [(1, j) for j in range(i - 1)]
                    cu = [consts[u:u + 4] for u in range(0, len(consts), 4)]
                    def special_front(blks=special, h=h, i=i, hb=hb, kvb=kvb, tbb=tbb):
                        bks = [bank[6], bank[7]]
                        pbs = [pb8[6], pb8[7]]

                        def f_qk_s(e):
                            for (s_, j, kd) in blks:
                                for c in range(2):
                                    ins = e.matmul(pbs[c][:, kd * 128:(kd + 1) * 128],
                                                   kTt[hb][c * 64:(c + 1) * 64, s_, j * 128:(j + 1) * 128],
                                                   qT[c * 64:(c + 1) * 64, h, i * 128:(i + 1) * 128],
                                                   start=True, stop=True)
                            return ins
                        P.op("pe", f_qk_s, reads=[kvb, buf("qT")], writes=bks)
                        nsp = len(blks)

                        def f_sadd(e):
                            for c in range(2):
                                ins = e.scalar_tensor_tensor(
                                    tmpA[c][:, 0:nsp * 128], pbs[c][:, 0:nsp * 128], 0.125, Tb[hb][:, 0:nsp * 128],
                                    ALU.mult, ALU.add)
                            return ins
                        P.op("dve", f_sadd, reads=bks + [tbb], writes=[buf("tmpA")])
                    special_front()

                    plist = [("c", u_) for u_ in cu] + [("s", special)]
                    for pn, (kind, blks) in enumerate(plist):
                        first_pair = (pn == 0)
                        last_pair = (pn == len(plist) - 1)
                        if kind == "s":
                            sps = sp_ctr[0] % 2
                            sp_ctr[0] += 1

                            nsp = len(blks)
                            tsb = buf("tmpA")
                            psb = buf("pTs%d" % (2 * sps))
                            ps2 = av(36864 + sps * 768, 768, "p (c n) -> p c n", c=2)
                            P.op("act", lambda e, nsp=nsp, ps2=ps2: e.activation(
                                out=ps2[:, :, 0:nsp * 128],
                                in_=f32m[:, 768:1536].rearrange("p (c n) -> p c n", c=2)[:, :, 0:nsp * 128], func=AF.Exp),
                                reads=[tsb], writes=[psb])
                            srcs = [(pTs[2 * sps], psb), (pTs[2 * sps + 1], psb)]
                            cols = [kd for (_, _, kd) in blks]
                            kbl = [(s_, j) for (s_, j, _) in blks]
                        else:
                            cps = cn_ctr[0] % 2
                            pps = cn_ctr[0] % 3
                            cn_ctr[0] += 1
                            bks = [bank[2 * cps], bank[2 * cps + 1]]
                            pbs = [pb[2 * cps], pb[2 * cps + 1]]

                            def f_qk_c(e, blks=blks, h=h, i=i, hb=hb, pbs=pbs):
                                for n, (s_, j) in enumerate(blks):
                                    for c in range(2):
                                        ins = e.matmul(pbs[c][:, n * 128:(n + 1) * 128],
                                                       kTt[hb][c * 64:(c + 1) * 64, s_, j * 128:(j + 1) * 128],
                                                       qT[c * 64:(c + 1) * 64, h, i * 128:(i + 1) * 128],
                                                       start=True, stop=True)
                                return ins
                            P.op("pe", f_qk_c, reads=[kvb, buf("qT")], writes=bks)
                            nb_ = len(blks)
                            ptb = buf("pT%d" % (2 * pps))
                            pt2 = av(33792 + pps * 1024, 1024, "p (c n) -> p c n", c=2)
                            P.op("act", lambda e, cps=cps, nb_=nb_, h=h, pt2=pt2: e.activation(
                                out=pt2[:, :, 0:nb_ * 128],
                                in_=pq[cps][:, :].rearrange("p (c n) -> p c n", c=2)[:, :, 0:nb_ * 128], func=AF.Exp,
                                scale=0.125), reads=bks, writes=[ptb])
                            srcs = [(pT[2 * pps], ptb), (pT[2 * pps + 1], ptb)]
                            cols = list(range(nb_))
                            kbl = list(blks)

                        def mk_pv(kbl=kbl, cols=cols, srcs=srcs, hb=hb, ob=ob, obank=obank, kvb=kvb,
                                  first_pair=first_pair, last_pair=last_pair, h=h, i=i):
                            def f_pv(e):
                                for c in range(2):
                                    for n, (s_, j) in enumerate(kbl):
                                        ins = e.matmul(pb[4 + ob][:, c * 256:c * 256 + 129],
                                                       srcs[c][0][:, cols[n] * 128:(cols[n] + 1) * 128],
                                                       vt[hb][:, s_, j, 0:129],
                                                       start=(first_pair and c == 0 and n == 0),
                                                       stop=(last_pair and n == len(kbl) - 1),
                                                       skip_group_check=True)
                                return ins

                            def go():
                                P.op("pe", f_pv, reads=[srcs[0][1], srcs[1][1], kvb], writes=[obank])
                                if last_pair:
                                    emit_epilogue(h, i, ob, hb)
                            return go
                        pend.append(mk_pv())
                        flush(LOOK)
            flush(0)
            drain_epilogue(0, 0)
            drain_epilogue(0, 0)

            load_lnp(2)
            wob = buf("wo")
            if os.environ.get('K_BAR'):
                P.wait("pool", ("c_pe", P.cnt["c_pe"]))
            P.dma("pool", wo[:, :, :], wo_d.rearrange("(a p) f -> p a f", p=128), "ld_wo", writes=[wob])
            woT = []

            def wo_front(i):
                bp = 2 * (i % 2)

                def f_wo(e):
                    for half in range(2):
                        for a_ in range(8):
                            ins = e.matmul(pb[bp + half][:, :], xT[:, a_, i * 128:(i + 1) * 128],
                                           wo[:, a_, half * 512:(half + 1) * 512], start=(a_ == 0), stop=(a_ == 7))
                    return ins
                P.op("pe", f_wo, reads=[buf("xT%d" % i), wob], writes=[bank[bp], bank[bp + 1]])

            wo_front(0)
            for i in range(NB):
                rb = buf("resid%d" % i)
                if i + 1 < NB:
                    wo_front(i + 1)
                bp = 2 * (i % 2)
                zt = zbs[i % 2]
                zbb = buf("zb%d" % (i % 2))

                def f_zo(e, i=i, zt=zt, bp=bp):
                    e.scalar_tensor_tensor(zt[:, 0:512], resid[:, i, 0:512], ALPHA, pb[bp][:, :], ALU.mult, ALU.add)
                    return e.scalar_tensor_tensor(zt[:, 512:1024], resid[:, i, 512:1024], ALPHA, pb[bp + 1][:, :],
                                                  ALU.mult, ALU.add)
                P.op("dve", f_zo, reads=[bank[bp], bank[bp + 1], rb], writes=[zbb])
                woT.append(emit_ln(zt, [zbb], i, want_T=True, us=i % 2))
                while len(woT) > 1:
                    woT.pop(0)()
            while woT:
                woT.pop(0)()
            emit_ffn(1, 3, last=True)
            P.wait("sp", ("st_out", P.cnt["st_out"]))
        else:
            for nm in ("st_kst0", "st_kst1", "st_vst0", "st_vst1", "st_x1", "st_qT"):
                P.wait("sp", (nm, P.cnt[nm]))

        P.emit()
    return nc


def _rel_bucket_np(n):
    n = np.maximum(n, 0)
    nf = np.maximum(n, 1).astype(np.float32)
    large = 16 + (np.log(nf / np.float32(16)) / np.float32(math.log(8.0)) * np.float32(16)).astype(np.int32)
    large = np.minimum(large, 31)
    return np.where(n < 16, n, large)


def _amat(rank):
    A = np.zeros((128, 12, 128), np.float32)
    s = np.arange(128)[:, None]
    t = np.arange(128)[None, :]
    for g, w in enumerate(POOL_WINDOWS):
        band = ((t - s) >= 0) & ((t - s) < w)
        eye = (s == t).astype(np.float32)
        diag = band.astype(np.float32) / w - eye
        cnt = np.minimum(t + 1, w).astype(np.float32)
        first = band.astype(np.float32) / cnt - eye
        A[:, 4 + g, :] = diag
        A[:, g, :] = first if rank == 0 else diag
        sh = np.arange(16)[:, None] - 16
        bandh = ((t - sh) >= 0) & ((t - sh) < w)
        A[0:16, 8 + g, :] = bandh.astype(np.float32) / w
    return A.astype(ml_dtypes.bfloat16)


def _bias_idx(rank):
    k = np.arange(128)[:, None]
    q = np.arange(128)[None, :]
    idx = np.zeros((128, 3, 128), np.int64)
    msk = np.zeros((128, 3, 128), np.float32)
    for tdx, delta in enumerate((rank, rank - 1, rank + 1)):
        rel = delta * 128 + q - k
        idx[:, tdx, :] = _rel_bucket_np(rel)
        msk[:, tdx, :] = np.where(rel >= 0, 0.0, NEG)
    return idx, msk


_NC_CACHE = {}


def _get_nc(mode):
    if mode not in _NC_CACHE:
        _NC_CACHE[mode] = build(mode)
    return _NC_CACHE[mode]


FUSED = True
POOL_POW = True
NOCC = False


def kernel(x, pool_w, pool_scale, w_qkv, w_o, lam_p, subln_g, rel_table, w_gate, w_up, w_down,
           ln_mix_g, ln_mix_b, ln_ffn_g, ln_ffn_b):
    f32 = lambda a: np.ascontiguousarray(np.asarray(a, dtype=np.float32))
    x = f32(x)
    Bn, S, _ = x.shape
    lnp = np.stack([f32(ln_mix_g)[0], f32(ln_mix_b)[0], f32(ln_ffn_g)[0], f32(ln_ffn_b)[0],
                    f32(ln_mix_g)[1], f32(ln_mix_b)[1], f32(ln_ffn_g)[1], f32(ln_ffn_b)[1]], 0)
    ident = np.eye(128, dtype=np.float32).astype(ml_dtypes.bfloat16)
    rel_table = f32(rel_table)
    common = {
        "ident": ident, "identf": np.eye(128, dtype=np.float32), "lnp": lnp,
        "w_gate": f32(w_gate), "w_up": f32(w_up), "w_down": f32(w_down),
    }
    l0 = {"pool_w": f32(pool_w)[0], "pool_scale": f32(pool_scale), "w_qkv": f32(w_qkv)[0]}
    l1 = {"w_o": f32(w_o)[0], "lam_p": f32(lam_p).reshape(1, 256), "subln_g": f32(subln_g).reshape(1, 128),
          "rel_table": rel_table}
    per_core = []
    for core in range(N_CORES):
        b, r = core // 2, core % 2
        xb = x[b].reshape(32, 128, D)
        own = xb[r::2]
        halo = np.zeros((NB, 16, D), np.float32)
        for i in range(NB):
            g = 2 * i + r
            if g > 0:
                halo[i] = xb[g - 1][112:128]
        idx, msk = _bias_idx(r)
        tb_g = np.ascontiguousarray(rel_table[idx].transpose(0, 3, 1, 2))
        per_core.append({"x_own": np.ascontiguousarray(own.reshape(T, D)), "x_halo": halo, "amat": _amat(r),
                         "tb_g": tb_g, "tb_m": msk})

    def assemble(outs):
        y = np.zeros((Bn, 32, 128, D), np.float32)
        for core in range(N_CORES):
            b, r = core // 2, core % 2
            y[b, r::2] = outs[core].reshape(NB, 128, D)
        return y.reshape(Bn, S, D)

    if FUSED:
        nc = _get_nc("fused")
        maps = []
        for core in range(N_CORES):
            m = dict(common); m.update(l0); m.update(l1); m.update(per_core[core])
            maps.append(m)
        res = run_bass_kernel_spmd(nc, maps, core_ids=list(range(N_CORES)))
        return assemble([res.results[c]["out"] for c in range(N_CORES)])

    ncA = _get_nc("A")
    mapsA = []
    for core in range(N_CORES):
        m = dict(common); m.update(l0)
        for kk in ("x_own", "x_halo", "amat"):
            m[kk] = per_core[core][kk]
        mapsA.append(m)
    resA = run_bass_kernel_spmd(ncA, mapsA, core_ids=list(range(N_CORES))).results
    ncB = _get_nc("B")
    mapsB = []
    for core in range(N_CORES):
        b = core // 2
        m = dict(common); m.update(l1)
        m["tb_g"] = per_core[core]["tb_g"]; m["tb_m"] = per_core[core]["tb_m"]
        m["x1"] = resA[core]["x1"]; m["qT"] = resA[core]["qT"]
        m["kT_all"] = np.stack([resA[2 * b]["kT_own"].reshape(NH, 128, T),
                                resA[2 * b + 1]["kT_own"].reshape(NH, 128, T)], 1).reshape(2 * NH * 128, T)
        m["v_all"] = np.stack([resA[2 * b]["v_own"].reshape(NH, 128, NB * VW),
                               resA[2 * b + 1]["v_own"].reshape(NH, 128, NB * VW)], 1).reshape(2 * NH * 128, NB * VW)
        mapsB.append(m)
    resB = run_bass_kernel_spmd(ncB, mapsB, core_ids=list(range(N_CORES))).results
    return assemble([resB[c]["out"] for c in range(N_CORES)])
```

```python
import math
import os
from contextlib import ExitStack

import numpy as np
import ml_dtypes

import concourse.bass as bass
import concourse.mybir as mybir
from concourse.bass_utils import run_bass_kernel_spmd

F32 = mybir.dt.float32
BF16 = mybir.dt.bfloat16
AF = mybir.ActivationFunctionType
ALU = mybir.AluOpType
AX = mybir.AxisListType

D = 1024
DFF = 2816
NB = 16
T = NB * 128
NH = 8
VW = 144
ALPHA = 4.0 ** 0.25
EPS = 1e-5
LAMBDA_INIT = 0.8 - 0.6 * math.exp(-0.3 * 1)
NEG = -30000.0
POOL_WINDOWS = (2, 4, 8, 16)
N_CORES = 8


class Buf:
    __slots__ = ("name", "w", "r", "rng")

    def __init__(self, name, rng=None):
        self.name = name
        self.w = None
        self.r = {}
        self.rng = rng


class Prog:
    ENG = ("pe", "act", "dve", "pool", "sp")

    def __init__(self, nc, es):
        self.nc = nc
        self.es = es
        self.q = {e: [] for e in self.ENG}
        self.sem = {}
        self.cnt = {}
        self.waited = {e: {} for e in self.ENG}
        self.ranged = []
        for e in ("pe", "act", "dve", "pool"):
            self.newsem("c_" + e)

    def newsem(self, name):
        self.sem[name] = self.es.enter_context(self.nc.semaphore(name))
        self.cnt[name] = 0

    def wait(self, eng, tok):
        if tok is None:
            return
        s, v = tok
        if self.waited[eng].get(s, 0) >= v:
            return
        self.waited[eng][s] = v
        h = self.sem[s]
        self.q[eng].append(lambda e, h=h, v=v: e.wait_ge(h, v))

    def _deps(self, eng, reads, writes):
        for b in reads:
            self.wait(eng, b.w)
        for b in writes:
            self.wait(eng, b.w)
            for t in b.r.values():
                self.wait(eng, t)
            if b.rng is not None:
                for y in self.ranged:
                    if y is not b and y.rng[0] < b.rng[1] and b.rng[0] < y.rng[1]:
                        self.wait(eng, y.w)
                        for t in y.r.values():
                            self.wait(eng, t)

    def _mark(self, eng, tok, reads, writes):
        for b in reads:
            b.r[eng] = tok
        for b in writes:
            b.w = tok
            b.r = {}

    def op(self, eng, fn, reads=(), writes=()):
        self._deps(eng, reads, writes)
        s = "c_" + eng
        self.cnt[s] += 1
        tok = (s, self.cnt[s])
        h = self.sem[s]
        self.q[eng].append(lambda e, fn=fn, h=h: fn(e).then_inc(h, 1))
        self._mark(eng, tok, reads, writes)
        return tok

    def dma(self, eng, out, in_, sem, reads=(), writes=()):
        if sem not in self.sem:
            self.newsem(sem)
        self._deps(eng, reads, writes)
        self.cnt[sem] += 16
        tok = (sem, self.cnt[sem])
        h = self.sem[sem]
        self.q[eng].append(lambda e, out=out, in_=in_, h=h: e.dma_start(out=out, in_=in_).then_inc(h, 16))
        self._mark("dma_" + sem, tok, reads, writes)
        return tok

    def emit(self):
        nc = self.nc
        with nc.Block() as block:
            @block.tensor
            def _(e):
                for f in self.q["pe"]:
                    f(e)

            @block.scalar
            def _(e):
                for f in self.q["act"]:
                    f(e)

            @block.vector
            def _(e):
                for f in self.q["dve"]:
                    f(e)

            @block.gpsimd
            def _(e):
                for f in self.q["pool"]:
                    f(e)

            @block.sync
            def _(e):
                for f in self.q["sp"]:
                    f(e)


def build(mode):
    nc = bass.Bass("TRN2", target_bir_lowering=False)
    do_l0 = mode in ("fused", "A")
    do_att = mode in ("fused", "B")

    def din(name, shape, dt=F32):
        return nc.dram_tensor(name, list(shape), dt, kind="ExternalInput")

    def dout(name, shape, dt=F32):
        return nc.dram_tensor(name, list(shape), dt, kind="ExternalOutput")

    ident_d = din("ident", [128, 128], BF16)
    identf_d = din("identf", [128, 128], F32)
    lnp_d = din("lnp", [8, D])
    if do_l0:
        x_d = din("x_own", [T, D])
        halo_d = din("x_halo", [NB, 16, D])
        amat_d = din("amat", [128, 12, 128], BF16)
        poolw_d = din("pool_w", [4, 256, 256])
        pscale_d = din("pool_scale", [1, D])
        wqkv_d = din("w_qkv", [D, 3 * D])
    wg_d = din("w_gate", [2, D, DFF])
    wu_d = din("w_up", [2, D, DFF])
    wdn_d = din("w_down", [2, DFF, D])
    if do_att:
        wo_d = din("w_o", [D, D])
        lam_d = din("lam_p", [1, 256])
        sg_d = din("subln_g", [1, 128])
        rel_d = din("rel_table", [32, NH])
        tbg_d = din("tb_g", [128, NH, 3, 128])
        tbm_d = din("tb_m", [128, 3, 128])
        out_d = dout("out", [T, D])

    if mode == "fused":
        kT_own = nc.dram_tensor("kT_own", [NH * 128, T], BF16)
        v_own = nc.dram_tensor("v_own", [NH * 128, NB * VW], BF16)
        kT_all = nc.dram_tensor("kT_all", [2 * NH * 128, T], BF16)
        v_all = nc.dram_tensor("v_all", [2 * NH * 128, NB * VW], BF16)
    elif mode == "A":
        kT_own = dout("kT_own", [NH * 128, T], BF16)
        v_own = dout("v_own", [NH * 128, NB * VW], BF16)
        x1_d = dout("x1", [T, D])
        qT_d = dout("qT", [128, NH, T], BF16)
    else:
        kT_all = din("kT_all", [2 * NH * 128, T], BF16)
        v_all = din("v_all", [2 * NH * 128, NB * VW], BF16)
        x1_d = din("x1", [T, D])
        qT_d = din("qT", [128, NH, T], BF16)

    with ExitStack() as es:
        def sb(name, shape, dt):
            return es.enter_context(nc.sbuf_tensor(name, list(shape), dt))

        def ps(name, shape, dt):
            return es.enter_context(nc.psum_tensor(name, list(shape), dt))

        resid = sb("resid", [128, NB, D], F32)
        xT = sb("xT", [128, 8, T], BF16)
        lnp_t = sb("lnp_t", [128, 2, D], F32)
        zb = sb("zb", [128, D], F32)
        identf = sb("identf_s", [128, 128], F32)
        sg = sb("sg", [128, 2, 512], BF16)
        f32m = sb("f32m", [128, 1536], F32)
        tmask = sb("tmask", [128, 384], F32)
        osb = sb("osb", [128, 2, 128], F32)
        sgb = sb("sgb", [128, 128], F32)
        ident = sb("ident_s", [128, 128], BF16)
        st = sb("st", [128, 2, 2, 6], F32)
        mv = sb("mv", [128, 2, 2], F32)
        rs = sb("rs", [128, 2, 2], F32)
        sm = sb("sm", [128, 64], F32)
        cb = sb("cb", [128, NH], F32)
        lamt = sb("lamt", [128, 256], F32)
        AR = 43008
        arena = sb("arena", [128, AR], BF16)

        def av(lo, n, pat=None, **kw):
            a = arena[:, lo:lo + n]
            return a.rearrange(pat, **kw) if pat else a

        hT = av(0, 11264, "p (a b) -> p a b", a=11)
        wd = av(11264, 11264, "p (a b) -> p a b", a=11)
        gu = [av(22528 + k * 4096, 4096, "p (g c f) -> p g c f", g=2, c=8) for k in range(3)]
        halo = av(0, 4096, "p (a b) -> p a b", a=4)
        xin_bf = av(4096, 3072, "p (a b) -> p a b", a=3)
        pmT = av(7168, 3072, "p (a c t) -> p a c t", a=3, c=8)
        amat = av(34816, 1536, "p (a b) -> p a b", a=12)
        wp = av(36352, 2048, "p (g k d) -> p g k d", g=4, k=2)
        ps_bc = f32m[:, 0:1024]
        wq = [av(22528 + k * 4096, 4096, "p (c f) -> p c f", c=8) for k in range(3)]
        kst = [av(34816 + k * 2048, 2048) for k in range(2)]
        vst = [av(38912 + k * 1152, 1152, "p (h v) -> p h v", h=NH) for k in range(2)]
        qT = av(0, 16384, "p (h t) -> p h t", h=NH)
        kTt = [av(16384 + k * 8704, 4096, "p (s t) -> p s t", s=2) for k in range(2)]
        vt = [av(16384 + k * 8704 + 4096, 4608, "p (s j v) -> p s j v", s=2, j=NB) for k in range(2)]
        pT = [av(33792 + k * 512, 512) for k in range(6)]
        pTs = [av(36864 + k * 384, 384) for k in range(4)]
        onb = [av(40448 + k * 128, 128) for k in range(2)]
        wo = av(16384, 8192, "p (a f) -> p a f", a=8)
        Tb = [f32m[:, k * 384:(k + 1) * 384] for k in range(2)]
        tmpA = [f32m[:, 768 + k * 384:768 + (k + 1) * 384] for k in range(2)]

        pq = [ps("pq%d" % k, [128, 1024], F32) for k in range(4)]
        pb8 = [pq[k // 2][:, (k % 2) * 512:(k % 2 + 1) * 512] for k in range(8)]
        pb = pb8[:6]

        P = Prog(nc, es)
        B = {}

        RNG = {"hT0": (0, 11264), "hT1": (0, 11264), "wd": (11264, 22528),
               "gu0": (22528, 26624), "gu1": (26624, 30720), "gu2": (30720, 34816),
               "halo0": (0, 4096), "halo1": (0, 4096), "halo2": (0, 4096), "halo3": (0, 4096),
               "xin0": (4096, 5120), "xin1": (5120, 6144), "xin2": (6144, 7168),
               "pmT0": (7168, 8192), "pmT1": (8192, 9216), "pmT2": (9216, 10240),
               "amat": (34816, 36352), "wp": (36352, 38400),
               "kst0": (34816, 36864), "kst1": (36864, 38912), "vst0": (38912, 40064), "vst1": (40064, 41216),
               "qT": (0, 16384), "kv0": (16384, 25088), "kv1": (25088, 33792),
               "pT0": (33792, 34816), "pT2": (34816, 35840), "pT4": (35840, 36864),
               "pTs0": (36864, 37632), "pTs2": (37632, 38400),
               "onb0": (40448, 40576), "onb1": (40576, 40704),
               "wo": (16384, 24576), "zb1": (38400, 40448)}

        def buf(name):
            if name not in B:
                B[name] = Buf(name, RNG.get(name))
                if name in RNG:
                    P.ranged.append(B[name])
            return B[name]

        bank = [buf("bank%d" % k) for k in range(8)]

        P.dma("sp", ident[:], ident_d[:, :], "ld_ident", writes=[buf("ident")])
        P.dma("sp", identf[:], identf_d[:, :], "ld_identf", writes=[buf("identf")])
        zbs = [zb[:], arena[:, 38400:40448].bitcast(F32)]

        def load_lnp(idx):
            P.dma("sp", lnp_t[:, 0, :], lnp_d[2 * idx:2 * idx + 1, :].partition_broadcast(128), "ld_lnp",
                  writes=[buf("lnp")])
            P.dma("sp", lnp_t[:, 1, :], lnp_d[2 * idx + 1:2 * idx + 2, :].partition_broadcast(128), "ld_lnp",
                  writes=[])
            B["lnp"].w = ("ld_lnp", P.cnt["ld_lnp"])

        ln_ctr = [0]

        def emit_ln(zsrc, zbuf_list, blk, want_T, us, evac="act"):
            k = ln_ctr[0] % 2
            ln_ctr[0] += 1
            bst, bmv, brs = buf("st%d" % k), buf("mv%d" % k), buf("rs%d" % k)
            zt = zbs[us]
            zbb = buf("zb%d" % us)

            def f_stats(e, k=k, zsrc=zsrc):
                e.bn_stats(st[:, k, 0, :], zsrc[:, 0:512])
                return e.bn_stats(st[:, k, 1, :], zsrc[:, 512:1024])
            P.op("dve", f_stats, reads=zbuf_list, writes=[bst])
            P.op("dve", lambda e, k=k: e.bn_aggr(mv[:, k, :], st[:, k, :, :]), reads=[bst], writes=[bmv])
            P.op("act", lambda e, k=k: e.activation(out=rs[:, k, 0:1], in_=mv[:, k, 1:2], func=AF.Sqrt,
                                                    bias=EPS, scale=1.0),
                 reads=[bmv], writes=[brs])
            P.op("dve", lambda e, k=k, zsrc=zsrc, zt=zt: e.scalar_tensor_tensor(
                zt, zsrc, mv[:, k, 0:1], lnp_t[:, 0, :], ALU.subtract, ALU.mult),
                reads=zbuf_list + [bmv, buf("lnp")], writes=[zbb])
            P.op("dve", lambda e, k=k: e.reciprocal(rs[:, k, 1:2], rs[:, k, 0:1]), reads=[brs], writes=[brs])
            rb = buf("resid%d" % blk)
            P.op("dve", lambda e, k=k, blk=blk, zt=zt: e.scalar_tensor_tensor(
                resid[:, blk, :], zt, rs[:, k, 1:2], lnp_t[:, 1, :], ALU.mult, ALU.add),
                reads=[zbb, brs, buf("lnp")], writes=[rb])
            if not want_T:
                return None

            def do_T(blk=blk, rb=rb):
                def f_tp(e):
                    for c in range(8):
                        ins = e.transpose(pb8[6 + c // 4][:, (c % 4) * 128:(c % 4 + 1) * 128],
                                          resid[:, blk, c * 128:(c + 1) * 128], identf[:])
                    return ins
                P.op("pe", f_tp, reads=[rb, buf("identf")], writes=[bank[6], bank[7]])
                xb_ = buf("xT%d" % blk)
                if evac == "act":
                    P.op("act", lambda e: e.copy(xT[:, 0:4, blk * 128:(blk + 1) * 128],
                                                 pb8[6][:, :].rearrange("p (c t) -> p c t", c=4)),
                         reads=[bank[6]], writes=[xb_])
                    P.op("act", lambda e: e.copy(xT[:, 4:8, blk * 128:(blk + 1) * 128],
                                                 pb8[7][:, :].rearrange("p (c t) -> p c t", c=4)),
                         reads=[bank[7]], writes=[xb_])
                else:
                    P.op("dve", lambda e: e.tensor_copy(xT[:, 0:4, blk * 128:(blk + 1) * 128],
                                                        pb8[6][:, :].rearrange("p (c t) -> p c t", c=4)),
                         reads=[bank[6]], writes=[xb_])
                    P.op("dve", lambda e: e.tensor_copy(xT[:, 4:8, blk * 128:(blk + 1) * 128],
                                                        pb8[7][:, :].rearrange("p (c t) -> p c t", c=4)),
                         reads=[bank[7]], writes=[xb_])
            return do_T

        def emit_ffn(L, lnp_idx, last):
            load_lnp(lnp_idx)
            wgv = wg_d[L].rearrange("(c p) f -> p c f", p=128)
            wuv = wu_d[L].rearrange("(c p) f -> p c f", p=128)
            wdv = wdn_d[L].rearrange("(a p) d -> p a d", p=128)
            gctr = 0
            defer_T = []
            for tt in range(2):
                for part in range(2):
                    groups = [(0, 2), (2, 2), (4, 2), (6, 2), (8, 2), (10, 1)]
                    gtok = {}

                    def load_gu(gi, part=part):
                        j0_, nj_ = groups[gi]
                        gb_ = (gctr + gi) % 3
                        f0 = (part * 11 + j0_) * 128
                        w = nj_ * 128
                        gbuf_ = buf("gu%d" % gb_)
                        P.dma("pool", gu[gb_][:, 0, :, 0:w], wgv[:, :, f0:f0 + w], "ld_gu%d" % gb_, writes=[gbuf_])
                        P.dma("pool", gu[gb_][:, 1, :, 0:w], wuv[:, :, f0:f0 + w], "ld_gu%d" % gb_, writes=[])
                        gbuf_.w = ("ld_gu%d" % gb_, P.cnt["ld_gu%d" % gb_])
                    for gi in range(3):
                        load_gu(gi)
                    P.dma("pool", wd[:, :, :], wdv[:, part * 11:(part + 1) * 11, :], "ld_wd", writes=[buf("wd")])
                    for gi, (j0, nj) in enumerate(groups):
                        gb = (gctr + gi) % 3
                        gbuf = buf("gu%d" % gb)
                        if gi >= 3:
                            load_gu(gi)
                        for j in range(nj):
                            jl = j0 + j
                            if defer_T:
                                defer_T.pop(0)()
                            for ts in range(2):
                                k = (jl * 2 + ts) % 2
                                t0 = tt * 1024 + ts * 512
                                xbufs = [buf("xT%d" % (t0 // 128 + q)) for q in range(4)]

                                def f_gu(e, gb=gb, j=j, t0=t0, k=k):
                                    for c in range(8):
                                        e.matmul(pb[k][:, :], gu[gb][:, 0, c, j * 128:(j + 1) * 128],
                                                 xT[:, c, t0:t0 + 512], start=(c == 0), stop=(c == 7))
                                    for c in range(8):
                                        ins = e.matmul(pb[2 + k][:, :], gu[gb][:, 1, c, j * 128:(j + 1) * 128],
                                                       xT[:, c, t0:t0 + 512], start=(c == 0), stop=(c == 7))
                                    return ins
                                P.op("pe", f_gu, reads=[gbuf] + xbufs, writes=[bank[k], bank[2 + k]])
                                P.op("act", lambda e, k=k: e.activation(out=sg[:, k, :], in_=pb[k][:, :], func=AF.Silu),
                                     reads=[bank[k]], writes=[buf("sg%d" % k)])
                                P.op("dve", lambda e, k=k, jl=jl, ts=ts: e.tensor_tensor(
                                    hT[:, jl, ts * 512:(ts + 1) * 512], sg[:, k, :], pb[2 + k][:, :], ALU.mult),
                                    reads=[buf("sg%d" % k), bank[2 + k]], writes=[buf("hT%d" % ts)])
                    gctr += len(groups)
                    for b8 in range(8):
                        blk = tt * 8 + b8
                        ts = b8 // 4
                        rb = buf("resid%d" % blk)
                        for dh in range(2):
                            k = (b8 * 2 + dh) % 2

                            def f_dn(e, b8=b8, dh=dh, k=k):
                                for j in range(11):
                                    ins = e.matmul(pb[4 + k][:, :], hT[:, j, b8 * 128:(b8 + 1) * 128],
                                                   wd[:, j, dh * 512:(dh + 1) * 512], start=(j == 0), stop=(j == 10))
                                return ins
                            P.op("pe", f_dn, reads=[buf("hT%d" % ts), buf("wd")], writes=[bank[4 + k]])
                            if part == 0:
                                P.op("dve", lambda e, blk=blk, dh=dh, k=k: e.scalar_tensor_tensor(
                                    resid[:, blk, dh * 512:(dh + 1) * 512], resid[:, blk, dh * 512:(dh + 1) * 512],
                                    ALPHA, pb[4 + k][:, :], ALU.mult, ALU.add),
                                    reads=[bank[4 + k]], writes=[rb])
                            else:
                                P.op("dve", lambda e, blk=blk, dh=dh, k=k: e.tensor_tensor(
                                    resid[:, blk, dh * 512:(dh + 1) * 512], resid[:, blk, dh * 512:(dh + 1) * 512],
                                    pb[4 + k][:, :], ALU.add),
                                    reads=[bank[4 + k]], writes=[rb])
                        if part == 1:
                            dT = emit_ln(resid[:, blk, :], [rb], blk, want_T=not last, us=blk % 2)
                            if dT is not None:
                                defer_T.append(dT)
                            if last:
                                P.dma("sp", out_d[blk * 128:(blk + 1) * 128, :], resid[:, blk, :], "st_out", reads=[rb])
            while defer_T:
                defer_T.pop(0)()

        if do_l0:
            xsplit = [(0, 1), (1, 2), (2, 4), (4, 8), (8, 12), (12, 16)]
            for xi, (b0, b1) in enumerate(xsplit[:2]):
                P.dma("sp", resid[:, b0:b1, :], x_d[b0 * 128:b1 * 128, :].rearrange("(b p) d -> p b d", p=128),
                      "ld_x%d" % xi, writes=[buf("resid%d" % q) for q in range(b0, b1)])
            P.dma("sp", amat[:, :, :], amat_d[:, :, :], "ld_amat", writes=[buf("amat")])
            P.dma("sp", ps_bc, pscale_d[0:1, :].partition_broadcast(128), "ld_psbc", writes=[buf("f32m")])
            load_lnp(0)
            for xi, (b0, b1) in list(enumerate(xsplit))[2:]:
                P.dma("sp", resid[:, b0:b1, :], x_d[b0 * 128:b1 * 128, :].rearrange("(b p) d -> p b d", p=128),
                      "ld_x%d" % xi, writes=[buf("resid%d" % q) for q in range(b0, b1)])
            P.dma("pool", wp[:, :, :, :], poolw_d.rearrange("g (k p) d -> p g k d", p=128), "ld_wp", writes=[buf("wp")])

            def f_wps(e):
                for g in range(4):
                    for kc in range(2):
                        ins = e.tensor_tensor(wp[:, g, kc, :], wp[:, g, kc, :], ps_bc[:, g * 256:(g + 1) * 256], ALU.mult)
                return ins
            P.op("dve", f_wps, reads=[buf("f32m")], writes=[buf("wp")])

            mixT = []

            def mix_front_a(i):
                rb = buf("resid%d" % i)
                hs = i % 4
                hb = buf("halo%d" % hs)
                P.dma("pool", halo[0:16, hs, :], halo_d[i, :, :], "ld_halo%d" % hs, writes=[hb])
                k2 = i % 3
                xb = buf("xin%d" % k2)
                P.op("act", lambda e: e.copy(xin_bf[:, k2, :], resid[:, i, :]), reads=[rb], writes=[xb])
                a0 = 0 if i == 0 else 4

                def f_pm(e):
                    for c in range(8):
                        g = c // 2
                        o = pb[c // 4][:, (c % 4) * 128:(c % 4 + 1) * 128]
                        e.matmul(o, xin_bf[:, k2, c * 128:(c + 1) * 128], amat[:, a0 + g, :], start=True, stop=False)
                        ins = e.matmul(o[:, 0:16], halo[0:16, hs, c * 128:(c + 1) * 128], amat[0:16, 8 + g, 0:16],
                                       start=False, stop=True)
                    return ins
                P.op("pe", f_pm, reads=[xb, hb, buf("amat")], writes=[bank[0], bank[1]])
                pmb = buf("pmT%d" % k2)

                def f_pmT(e):
                    e.copy(pmT[:, k2, 0:4, :], pb[0][:, :].rearrange("p (c t) -> p c t", c=4))
                    return e.copy(pmT[:, k2, 4:8, :], pb[1][:, :].rearrange("p (c t) -> p c t", c=4))
                P.op("act", f_pmT, reads=[bank[0], bank[1]], writes=[pmb])

            def mix_front_b(i):
                k2 = i % 3
                pmb = buf("pmT%d" % k2)
                mb = 2 + 2 * (i % 2)

                def f_mix(e):
                    for g in range(4):
                        for kc in range(2):
                            ins = e.matmul(pb[mb + g // 2][:, (g % 2) * 256:(g % 2 + 1) * 256],
                                           pmT[:, k2, 2 * g + kc, :], wp[:, g, kc, :], start=(kc == 0), stop=(kc == 1))
                    return ins
                P.op("pe", f_mix, reads=[pmb, buf("wp")], writes=[bank[mb], bank[mb + 1]])

            def mix_z(i):
                rb = buf("resid%d" % i)
                k2 = i % 2
                zt = zbs[k2]
                zbb = buf("zb%d" % k2)

                mb = 2 + 2 * (i % 2)

                def f_z1(e):
                    e.scalar_tensor_tensor(zt[:, 0:512], resid[:, i, 0:512], ALPHA, pb[mb][:, :], ALU.mult, ALU.add)
                    return e.scalar_tensor_tensor(zt[:, 512:1024], resid[:, i, 512:1024], ALPHA, pb[mb + 1][:, :],
                                                  ALU.mult, ALU.add)
                P.op("dve", f_z1, reads=[bank[mb], bank[mb + 1], rb], writes=[zbb])

            for j in range(2):
                mix_front_a(j)
                mix_front_b(j)
            for i in range(NB):
                mix_z(i)
                if i + 2 < NB:
                    mix_front_a(i + 2)
                    mix_front_b(i + 2)
                k2 = i % 2
                mixT.append(emit_ln(zbs[k2], [buf("zb%d" % k2)], i, want_T=True, us=k2, evac="dve"))
                while len(mixT) > 1:
                    mixT.pop(0)()
            while mixT:
                mixT.pop(0)()

            emit_ffn(0, 1, last=False)

            wqv = wqkv_d.rearrange("(c p) f -> p c f", p=128)
            allx = [buf("xT%d" % q) for q in range(NB)]
            for k in range(2):
                P.op("pool", lambda e, k=k: e.memset(vst[k][:, :, 128:VW], 0.0), writes=[buf("vst%d" % k)])
                P.op("pool", lambda e, k=k: e.memset(vst[k][:, :, 128:129], 1.0), writes=[buf("vst%d" % k)])
            bctr = 0
            for gidx, grp in enumerate((2, 3, 4, 5, 0, 1)):
                wbi = gidx % 3
                wb = buf("gu%d" % wbi)
                P.dma("pool", wq[wbi][:, :, :], wqv[:, :, grp * 512:(grp + 1) * 512], "ld_gu%d" % wbi, writes=[wb])
                if grp < 4:
                    for hh in range(4):
                        h = (grp % 2) * 4 + hh
                        if grp >= 2:
                            ks = h % 2
                            kb_ = buf("kst%d" % ks)
                        for ts in range(4):
                            k = bctr % 2
                            bctr += 1

                            def f_qk(e, wbi=wbi, hh=hh, ts=ts, k=k):
                                for c in range(8):
                                    ins = e.matmul(pb[k][:, :], wq[wbi][:, c, hh * 128:(hh + 1) * 128],
                                                   xT[:, c, ts * 512:(ts + 1) * 512], start=(c == 0), stop=(c == 7))
                                return ins
                            P.op("pe", f_qk, reads=[wb] + allx[ts * 4:ts * 4 + 4], writes=[bank[k]])
                            if grp < 2:
                                P.op("act", lambda e, h=h, ts=ts, k=k: e.copy(
                                    qT[:, h, ts * 512:(ts + 1) * 512], pb[k][:, :]),
                                    reads=[bank[k]], writes=[buf("qT")])
                            else:
                                P.op("act", lambda e, ks=ks, ts=ts, k=k: e.copy(
                                    kst[ks][:, ts * 512:(ts + 1) * 512], pb[k][:, :]),
                                    reads=[bank[k]], writes=[kb_])
                        if grp >= 2:
                            P.dma("sp", kT_own[h * 128:(h + 1) * 128, :], kst[ks][:, :], "st_kst%d" % ks, reads=[kb_])
                else:
                    half = grp - 4
                    for blk in range(NB):
                        k = bctr % 2
                        bctr += 1
                        vs = blk % 2
                        vb_ = buf("vst%d" % vs)

                        def f_v(e, wbi=wbi, blk=blk, k=k):
                            for c in range(8):
                                ins = e.matmul(pb[k][:, :], xT[:, c, blk * 128:(blk + 1) * 128], wq[wbi][:, c, :],
                                               start=(c == 0), stop=(c == 7))
                            return ins
                        P.op("pe", f_v, reads=[wb, allx[blk]], writes=[bank[k]])
                        P.op("dve", lambda e, vs=vs, half=half, k=k: e.tensor_copy(
                            vst[vs][:, half * 4:(half + 1) * 4, 0:128],
                            pb[k][:, :].rearrange("p (h v) -> p h v", h=4)),
                            reads=[bank[k]], writes=[vb_])
                        dst = v_own.ap().rearrange("(h k) (j v) -> k h j v", k=128, v=VW)[:, half * 4:(half + 1) * 4, blk, :]
                        P.dma("sp", dst, vst[vs][:, half * 4:(half + 1) * 4, :], "st_vst%d" % vs, reads=[vb_])
            if mode == "A":
                for blk in range(NB):
                    P.dma("sp", x1_d[blk * 128:(blk + 1) * 128, :], resid[:, blk, :], "st_x1",
                          reads=[buf("resid%d" % blk)])
                P.dma("sp", qT_d[:, :, :], qT[:, :, :], "st_qT", reads=[buf("qT")])

        kv_ready = None
        if mode == "fused":
            for nm in ("st_kst0", "st_kst1", "st_vst0", "st_vst1"):
                P.wait("pool", (nm, P.cnt[nm]))
            P.newsem("cc")
            groups = [[2 * b, 2 * b + 1] for b in range(N_CORES // 2)]
            hcc = P.sem["cc"]

            if not NOCC:
                for h in range(NH):
                    def f_k(e, h=h):
                        return e.collective_compute("AllGather", ALU.bypass, replica_groups=groups,
                                                    ins=[kT_own[h * 128:(h + 1) * 128, :]],
                                                    outs=[kT_all[h * 256:(h + 1) * 256, :]])

                    def f_v(e, h=h):
                        return e.collective_compute("AllGather", ALU.bypass, replica_groups=groups,
                                                    ins=[v_own[h * 128:(h + 1) * 128, :]],
                                                    outs=[v_all[h * 256:(h + 1) * 256, :]])
                    P.q["pool"].append(lambda e, f=f_k: f(e).then_inc(hcc, 1))
                    P.q["pool"].append(lambda e, f=f_v: f(e).then_inc(hcc, 1))
                kv_ready = True

        if do_att:
            if mode == "B":
                for q4 in range(4):
                    P.dma("sp", resid[:, q4 * 4:(q4 + 1) * 4, :],
                          x1_d[q4 * 512:(q4 + 1) * 512, :].rearrange("(b p) d -> p b d", p=128), "ld_x%d" % q4,
                          writes=[buf("resid%d" % (q4 * 4 + q)) for q in range(4)])
                P.dma("sp", qT[:, :, :], qT_d[:, :, :], "ld_qT", writes=[buf("qT")])
            P.dma("sp", cb[:, :], rel_d[31:32, :].partition_broadcast(128), "ld_cb", writes=[buf("cb")])
            P.dma("sp", lamt[:, :], lam_d[0:1, :].partition_broadcast(128), "ld_lam", writes=[buf("lamt")])
            P.dma("sp", sgb[:, :], sg_d[0:1, :].partition_broadcast(128), "ld_sgb", writes=[buf("sgb")])
            P.op("dve", lambda e: e.tensor_scalar(sgb[:, :], sgb[:, :], (1.0 - LAMBDA_INIT) * math.sqrt(128.0), None,
                                                  ALU.mult), reads=[buf("sgb")], writes=[buf("sgb")])
            lv = lamt[:, :].rearrange("p (a b d) -> p a b d", a=2, b=2)
            P.op("dve", lambda e: e.tensor_tensor(osb[:, 0, :].rearrange("p (a d) -> p a d", a=2),
                                                  lv[:, :, 0, :], lv[:, :, 1, :], ALU.mult),
                 reads=[buf("lamt")], writes=[buf("osb0")])
            P.op("dve", lambda e: e.tensor_reduce(sm[:, 0:2], osb[:, 0, :].rearrange("p (a d) -> p a d", a=2),
                                                  AX.X, ALU.add), reads=[buf("osb0")], writes=[buf("sm_lam")])
            P.op("act", lambda e: e.activation(out=sm[:, 2:4], in_=sm[:, 0:2], func=AF.Exp),
                 reads=[buf("sm_lam")], writes=[buf("sm_lam")])
            P.op("dve", lambda e: e.tensor_tensor(sm[:, 4:5], sm[:, 3:4], sm[:, 2:3], ALU.subtract),
                 reads=[buf("sm_lam")], writes=[buf("sm_lam")])
            P.op("dve", lambda e: e.tensor_scalar(sm[:, 4:5], sm[:, 4:5], -LAMBDA_INIT, None, ALU.add),
                 reads=[buf("sm_lam")], writes=[buf("sm_lam")])
            lamb = buf("sm_lam")
            P.op("dve", lambda e: e.memset(sm[:, 6:7], -0.5), writes=[buf("mhalf")])
            P.dma("sp", tmask[:, :].rearrange("p (a q) -> p a q", a=3), tbm_d[:, :, :], "ld_tm", writes=[buf("tmask")])

            kall = kT_all.ap().rearrange("(h s k) t -> k s h t", s=2, h=NH)
            vall = v_all.ap().rearrange("(h s k) (j v) -> k s h j v", s=2, h=NH, v=VW)
            LOOK = int(os.environ.get('K_LOOK', '2'))
            sp_ctr = [0]
            cn_ctr = [0]
            ep_ctr = [0]
            pend = []
            tpend = []

            b1pend = []
            b2pend = []

            def drain_epilogue(keep1, keep2):
                while len(b2pend) > keep2:
                    b2pend.pop(0)()
                while len(b1pend) > keep1:
                    b1pend.pop(0)()

            def emit_epilogue(h, i, ob, hb):
                obank = bank[4 + ob]
                ek = ep_ctr[0] % 2
                ep_ctr[0] += 1
                sb_ = buf("sm_e%d" % ek)
                o0 = 8 + ek * 8
                ov = pb[4 + ob][:, :].rearrange("p (c w) -> p c w", c=2)
                while len(b2pend) > 0:
                    b2pend.pop(0)()
                while len(b1pend) > 0:
                    b1pend.pop(0)()
                P.op("dve", lambda e: e.reciprocal(sm[:, o0:o0 + 2], ov[:, :, 128]), reads=[obank], writes=[sb_])
                P.op("dve", lambda e: e.tensor_tensor(sm[:, o0 + 2:o0 + 3], sm[:, o0 + 1:o0 + 2], sm[:, 4:5], ALU.mult),
                     reads=[sb_, lamb], writes=[sb_])
                osbb = buf("osb%d" % ek)
                P.op("dve", lambda e: e.tensor_scalar(osb[:, ek, :], pb[4 + ob][:, 0:128], sm[:, o0:o0 + 1], None, ALU.mult),
                     reads=[obank, sb_], writes=[osbb])
                P.op("dve", lambda e: e.scalar_tensor_tensor(
                    osb[:, ek, :], pb[4 + ob][:, 256:384], sm[:, o0 + 2:o0 + 3], osb[:, ek, :], ALU.mult, ALU.add),
                    reads=[obank, sb_], writes=[osbb])
                P.op("dve", lambda e: e.scalar_tensor_tensor(
                    sg[:, 0, 0:128], osb[:, ek, :], 1.0, osb[:, ek, :], ALU.mult, ALU.mult, accum_out=sm[:, o0 + 3:o0 + 4]),
                    reads=[osbb], writes=[sb_, buf("junk")])

                def stage_b1():
                    if POOL_POW:
                        P.op("pool", lambda e: e.tensor_scalar(sm[:, o0 + 4:o0 + 5], sm[:, o0 + 3:o0 + 4], 128.0 * EPS,
                                                               None, ALU.add), reads=[sb_], writes=[sb_])
                        P.op("pool", lambda e: e.tensor_tensor(sm[:, o0 + 5:o0 + 6], sm[:, o0 + 4:o0 + 5], sm[:, 6:7],
                                                               ALU.pow), reads=[sb_, buf("mhalf")], writes=[sb_])
                    else:
                        P.op("act", lambda e: e.activation(out=sm[:, o0 + 4:o0 + 5], in_=sm[:, o0 + 3:o0 + 4],
                                                           func=AF.Ln, bias=128.0 * EPS, scale=1.0),
                             reads=[sb_], writes=[sb_])
                        P.op("act", lambda e: e.activation(out=sm[:, o0 + 5:o0 + 6], in_=sm[:, o0 + 4:o0 + 5],
                                                           func=AF.Exp, scale=-0.5), reads=[sb_], writes=[sb_])
                    P.op("dve", lambda e: e.scalar_tensor_tensor(
                        onb[ek][:, :], osb[:, ek, :], sm[:, o0 + 5:o0 + 6], sgb[:, :], ALU.mult, ALU.mult),
                        reads=[osbb, sb_, buf("sgb")], writes=[buf("onb%d" % ek)])

                    def stage_b2():
                        tpv = pb[4 + ob][:, 448:512].bitcast(BF16)
                        P.op("pe", lambda e: e.transpose(tpv, onb[ek][:, :], ident[:]),
                             reads=[buf("onb%d" % ek), buf("ident")], writes=[obank])
                        P.op("dve", lambda e: e.tensor_copy(xT[:, h, i * 128:(i + 1) * 128], tpv),
                             reads=[obank], writes=[buf("xT%d" % i)])
                    b2pend.append(stage_b2)
                b1pend.append(stage_b1)

            def flush(upto):
                while len(pend) > upto:
                    pend.pop(0)()

            for h in range(NH):
                hb = h % 2
                kvb = buf("kv%d" % hb)
                tbb = buf("tb%d" % hb)
                if kv_ready is not None:
                    P.wait("sp", ("cc", 2 * h + 2))
                P.dma("sp", kTt[hb][:, :, :], kall[:, :, h, :], "ld_kv%d" % hb, writes=[kvb])
                P.dma("sp", vt[hb][:, :, :, :], vall[:, :, h, :, :], "ld_kv%d" % hb, writes=[])
                kvb.w = ("ld_kv%d" % hb, P.cnt["ld_kv%d" % hb])
                P.dma("sp", Tb[hb].rearrange("p (a q) -> p a q", a=3), tbg_d[:, h, :, :], "ld_tb%d" % hb, writes=[tbb])
                P.op("dve", lambda e, hb=hb, h=h: e.scalar_tensor_tensor(Tb[hb], Tb[hb], cb[:, h:h + 1], tmask[:, :],
                                                                         ALU.subtract, ALU.add),
                     reads=[buf("tmask"), buf("cb")], writes=[tbb])

                for i in range(NB):
                    if h == NH - 1 and i == 3:
                        P.dma("pool", wo[:, :, :], wo_d.rearrange("(a p) f -> p a f", p=128), "ld_wo", writes=[buf("wo")])
                    ob = (h * NB + i) % 2
                    obank = bank[4 + ob]
                    special = [(0, i, 0), (1, i, 1)] + ([(1, i - 1, 2)] if i >= 1 else [])
                    consts = [(0, j) for j in range(i)] + [(1, j) for j in range(i - 1)]
                    cu = [consts[u:u + 4] for u in range(0, len(consts), 4)]
                    def special_front(blks=special, h=h, i=i, hb=hb, kvb=kvb, tbb=tbb):
                        bks = [bank[6], bank[7]]
                        pbs = [pb8[6], pb8[7]]

                        def f_qk_s(e):
                            for (s_, j, kd) in blks:
                                for c in range(2):
                                    ins = e.matmul(pbs[c][:, kd * 128:(kd + 1) * 128],
                                                   kTt[hb][c * 64:(c + 1) * 64, s_, j * 128:(j + 1) * 128],
                                                   qT[c * 64:(c + 1) * 64, h, i * 128:(i + 1) * 128],
                                                   start=True, stop=True)
                            return ins
                        P.op("pe", f_qk_s, reads=[kvb, buf("qT")], writes=bks)
                        nsp = len(blks)

                        def f_sadd(e):
                            for c in range(2):
                                ins = e.scalar_tensor_tensor(
                                    tmpA[c][:, 0:nsp * 128], pbs[c][:, 0:nsp * 128], 0.125, Tb[hb][:, 0:nsp * 128],
                                    ALU.mult, ALU.add)
                            return ins
                        P.op("dve", f_sadd, reads=bks + [tbb], writes=[buf("tmpA")])
                    special_front()

                    plist = [("c", u_) for u_ in cu] + [("s", special)]
                    for pn, (kind, blks) in enumerate(plist):
                        first_pair = (pn == 0)
                        last_pair = (pn == len(plist) - 1)
                        if kind == "s":
                            sps = sp_ctr[0] % 2
                            sp_ctr[0] += 1

                            nsp = len(blks)
                            tsb = buf("tmpA")
                            psb = buf("pTs%d" % (2 * sps))
                            ps2 = av(36864 + sps * 768, 768, "p (c n) -> p c n", c=2)
                            P.op("act", lambda e, nsp=nsp, ps2=ps2: e.activation(
                                out=ps2[:, :, 0:nsp * 128],
                                in_=f32m[:, 768:1536].rearrange("p (c n) -> p c n", c=2)[:, :, 0:nsp * 128], func=AF.Exp),
                                reads=[tsb], writes=[psb])
                            srcs = [(pTs[2 * sps], psb), (pTs[2 * sps + 1], psb)]
                            cols = [kd for (_, _, kd) in blks]
                            kbl = [(s_, j) for (s_, j, _) in blks]
                        else:
                            cps = cn_ctr[0] % 2
                            pps = cn_ctr[0] % 3
                            cn_ctr[0] += 1
                            bks = [bank[2 * cps], bank[2 * cps + 1]]
                            pbs = [pb[2 * cps], pb[2 * cps + 1]]

                            def f_qk_c(e, blks=blks, h=h, i=i, hb=hb, pbs=pbs):
                                for n, (s_, j) in enumerate(blks):
                                    for c in range(2):
                                        ins = e.matmul(pbs[c][:, n * 128:(n + 1) * 128],
                                                       kTt[hb][c * 64:(c + 1) * 64, s_, j * 128:(j + 1) * 128],
                                                       qT[c * 64:(c + 1) * 64, h, i * 128:(i + 1) * 128],
                                                       start=True, stop=True)
                                return ins
                            P.op("pe", f_qk_c, reads=[kvb, buf("qT")], writes=bks)
                            nb_ = len(blks)
                            ptb = buf("pT%d" % (2 * pps))
                            pt2 = av(33792 + pps * 1024, 1024, "p (c n) -> p c n", c=2)
                            P.op("act", lambda e, cps=cps, nb_=nb_, h=h, pt2=pt2: e.activation(
                                out=pt2[:, :, 0:nb_ * 128],
                                in_=pq[cps][:, :].rearrange("p (c n) -> p c n", c=2)[:, :, 0:nb_ * 128], func=AF.Exp,
                                scale=0.125), reads=bks, writes=[ptb])
                            srcs = [(pT[2 * pps], ptb), (pT[2 * pps + 1], ptb)]
                            cols = list(range(nb_))
                            kbl = list(blks)

                        def mk_pv(kbl=kbl, cols=cols, srcs=srcs, hb=hb, ob=ob, obank=obank, kvb=kvb,
                                  first_pair=first_pair, last_pair=last_pair, h=h, i=i):
                            def f_pv(e):
                                for c in range(2):
                                    for n, (s_, j) in enumerate(kbl):
                                        ins = e.matmul(pb[4 + ob][:, c * 256:c * 256 + 129],
                                                       srcs[c][0][:, cols[n] * 128:(cols[n] + 1) * 128],
                                                       vt[hb][:, s_, j, 0:129],
                                                       start=(first_pair and c == 0 and n == 0),
                                                       stop=(last_pair and n == len(kbl) - 1),
                                                       skip_group_check=True)
                                return ins

                            def go():
                                P.op("pe", f_pv, reads=[srcs[0][1], srcs[1][1], kvb], writes=[obank])
                                if last_pair:
                                    emit_epilogue(h, i, ob, hb)
                            return go
                        pend.append(mk_pv())
                        flush(LOOK)
            flush(0)
            drain_epilogue(0, 0)
            drain_epilogue(0, 0)

            load_lnp(2)
            wob = buf("wo")
            woT = []

            def wo_front(i):
                bp = 2 * (i % 2)

                def f_wo(e):
                    for half in range(2):
                        for a_ in range(8):
                            ins = e.matmul(pb[bp + half][:, :], xT[:, a_, i * 128:(i + 1) * 128],
                                           wo[:, a_, half * 512:(half + 1) * 512], start=(a_ == 0), stop=(a_ == 7))
                    return ins
                P.op("pe", f_wo, reads=[buf("xT%d" % i), wob], writes=[bank[bp], bank[bp + 1]])

            wo_front(0)
            for i in range(NB):
                rb = buf("resid%d" % i)
                if i + 1 < NB:
                    wo_front(i + 1)
                bp = 2 * (i % 2)
                zt = zbs[i % 2]
                zbb = buf("zb%d" % (i % 2))

                def f_zo(e, i=i, zt=zt, bp=bp):
                    e.scalar_tensor_tensor(zt[:, 0:512], resid[:, i, 0:512], ALPHA, pb[bp][:, :], ALU.mult, ALU.add)
                    return e.scalar_tensor_tensor(zt[:, 512:1024], resid[:, i, 512:1024], ALPHA, pb[bp + 1][:, :],
                                                  ALU.mult, ALU.add)
                P.op("dve", f_zo, reads=[bank[bp], bank[bp + 1], rb], writes=[zbb])
                woT.append(emit_ln(zt, [zbb], i, want_T=True, us=i % 2))
                while len(woT) > 1:
                    woT.pop(0)()
            while woT:
                woT.pop(0)()
            emit_ffn(1, 3, last=True)
            P.wait("sp", ("st_out", P.cnt["st_out"]))
        else:
            for nm in ("st_kst0", "st_kst1", "st_vst0", "st_vst1", "st_x1", "st_qT"):
                P.wait("sp", (nm, P.cnt[nm]))

        P.emit()
    return nc


def _rel_bucket_np(n):
    n = np.maximum(n, 0)
    nf = np.maximum(n, 1).astype(np.float32)
    large = 16 + (np.log(nf / np.float32(16)) / np.float32(math.log(8.0)) * np.float32(16)).astype(np.int32)
    large = np.minimum(large, 31)
    return np.where(n < 16, n, large)


def _amat(rank):
    A = np.zeros((128, 12, 128), np.float32)
    s = np.arange(128)[:, None]
    t = np.arange(128)[None, :]
    for g, w in enumerate(POOL_WINDOWS):
        band = ((t - s) >= 0) & ((t - s) < w)
        eye = (s == t).astype(np.float32)
        diag = band.astype(np.float32) / w - eye
        cnt = np.minimum(t + 1, w).astype(np.float32)
        first = band.astype(np.float32) / cnt - eye
        A[:, 4 + g, :] = diag
        A[:, g, :] = first if rank == 0 else diag
        sh = np.arange(16)[:, None] - 16
        bandh = ((t - sh) >= 0) & ((t - sh) < w)
        A[0:16, 8 + g, :] = bandh.astype(np.float32) / w
    return A.astype(ml_dtypes.bfloat16)


def _bias_idx(rank):
    k = np.arange(128)[:, None]
    q = np.arange(128)[None, :]
    idx = np.zeros((128, 3, 128), np.int64)
    msk = np.zeros((128, 3, 128), np.float32)
    for tdx, delta in enumerate((rank, rank - 1, rank + 1)):
        rel = delta * 128 + q - k
        idx[:, tdx, :] = _rel_bucket_np(rel)
        msk[:, tdx, :] = np.where(rel >= 0, 0.0, NEG)
    return idx, msk


_NC_CACHE = {}


def _get_nc(mode):
    if mode not in _NC_CACHE:
        _NC_CACHE[mode] = build(mode)
    return _NC_CACHE[mode]


FUSED = True
POOL_POW = True
NOCC = False


def kernel(x, pool_w, pool_scale, w_qkv, w_o, lam_p, subln_g, rel_table, w_gate, w_up, w_down,
           ln_mix_g, ln_mix_b, ln_ffn_g, ln_ffn_b):
    f32 = lambda a: np.ascontiguousarray(np.asarray(a, dtype=np.float32))
    x = f32(x)
    Bn, S, _ = x.shape
    lnp = np.stack([f32(ln_mix_g)[0], f32(ln_mix_b)[0], f32(ln_ffn_g)[0], f32(ln_ffn_b)[0],
                    f32(ln_mix_g)[1], f32(ln_mix_b)[1], f32(ln_ffn_g)[1], f32(ln_ffn_b)[1]], 0)
    ident = np.eye(128, dtype=np.float32).astype(ml_dtypes.bfloat16)
    rel_table = f32(rel_table)
    common = {
        "ident": ident, "identf": np.eye(128, dtype=np.float32), "lnp": lnp,
        "w_gate": f32(w_gate), "w_up": f32(w_up), "w_down": f32(w_down),
    }
    l0 = {"pool_w": f32(pool_w)[0], "pool_scale": f32(pool_scale), "w_qkv": f32(w_qkv)[0]}
    l1 = {"w_o": f32(w_o)[0], "lam_p": f32(lam_p).reshape(1, 256), "subln_g": f32(subln_g).reshape(1, 128),
          "rel_table": rel_table}
    per_core = []
    for core in range(N_CORES):
        b, r = core // 2, core % 2
        xb = x[b].reshape(32, 128, D)
        own = xb[r::2]
        halo = np.zeros((NB, 16, D), np.float32)
        for i in range(NB):
            g = 2 * i + r
            if g > 0:
                halo[i] = xb[g - 1][112:128]
        idx, msk = _bias_idx(r)
        tb_g = np.ascontiguousarray(rel_table[idx].transpose(0, 3, 1, 2))
        per_core.append({"x_own": np.ascontiguousarray(own.reshape(T, D)), "x_halo": halo, "amat": _amat(r),
                         "tb_g": tb_g, "tb_m": msk})

    def assemble(outs):
        y = np.zeros((Bn, 32, 128, D), np.float32)
        for core in range(N_CORES):
            b, r = core // 2, core % 2
            y[b, r::2] = outs[core].reshape(NB, 128, D)
        return y.reshape(Bn, S, D)

    if FUSED:
        nc = _get_nc("fused")
        maps = []
        for core in range(N_CORES):
            m = dict(common); m.update(l0); m.update(l1); m.update(per_core[core])
            maps.append(m)
        res = run_bass_kernel_spmd(nc, maps, core_ids=list(range(N_CORES)))
        return assemble([res.results[c]["out"] for c in range(N_CORES)])

    ncA = _get_nc("A")
    mapsA = []
    for core in range(N_CORES):
        m = dict(common); m.update(l0)
        for kk in ("x_own", "x_halo", "amat"):
            m[kk] = per_core[core][kk]
        mapsA.append(m)
    resA = run_bass_kernel_spmd(ncA, mapsA, core_ids=list(range(N_CORES))).results
    ncB = _get_nc("B")
    mapsB = []
    for core in range(N_CORES):
        b = core // 2
        m = dict(common); m.update(l1)
        m["tb_g"] = per_core[core]["tb_g"]; m["tb_m"] = per_core[core]["tb_m"]
        m["x1"] = resA[core]["x1"]; m["qT"] = resA[core]["qT"]
        m["kT_all"] = np.stack([resA[2 * b]["kT_own"].reshape(NH, 128, T),
                                resA[2 * b + 1]["kT_own"].reshape(NH, 128, T)], 1).reshape(2 * NH * 128, T)
        m["v_all"] = np.stack([resA[2 * b]["v_own"].reshape(NH, 128, NB * VW),
                               resA[2 * b + 1]["v_own"].reshape(NH, 128, NB * VW)], 1).reshape(2 * NH * 128, NB * VW)
        mapsB.append(m)
    resB = run_bass_kernel_spmd(ncB, mapsB, core_ids=list(range(N_CORES))).results
    return assemble([resB[c]["out"] for c in range(N_CORES)])
```
